# Optimizing a Trainium2 kernel written in Bass

```python
import jax
import jax.numpy as jnp
from jax import lax
import numpy as np

D_MODEL = 2048
BATCH = 2
SEQ = 4096
DEPTH = 2

GRID_W = 64

POOL_WINDOWS = (2, 4, 8, 16)
POOL_GROUPS = 4
POOL_GROUP_W = D_MODEL // 16
POOL_W = POOL_GROUPS * POOL_GROUP_W

ATT_HEAD_DIM = 128
ATT_HEADS = (D_MODEL - POOL_W) // ATT_HEAD_DIM
ATT_KV_HEADS = 4
ATT_GROUP = ATT_HEADS // ATT_KV_HEADS
Q_W = ATT_HEADS * ATT_HEAD_DIM
KV_W = ATT_KV_HEADS * ATT_HEAD_DIM
Q_BLOCK = 128
ROPE_THETA = 10000.0
AB_IN_W = POOL_W + Q_W + 2 * KV_W
AB_OUT_W = POOL_W + Q_W

MLSTM_HEADS = 8
MLSTM_HEAD_DIM = D_MODEL // MLSTM_HEADS
MLSTM_W = MLSTM_HEADS * MLSTM_HEAD_DIM
MLSTM_CHUNK = 64
C_IN_W = 4 * MLSTM_W + 4 * MLSTM_HEADS
FORGET_BIAS = 3.0

D_FF = 5632
CONV_WIDTH = 3

NORM_EPS = 1e-6

kernel_name = "hybrid_pool_gqa_mlstm_convffn_encoder"


def rms_norm(x, g):
    xf = x.astype(jnp.float32)
    y = xf * lax.rsqrt(jnp.mean(xf * xf, axis=-1, keepdims=True) + NORM_EPS)
    return (y * g.astype(jnp.float32)).astype(x.dtype)


def axial_rope(seq_len):
    rows = seq_len // GRID_W
    row = jnp.repeat(jnp.arange(rows), GRID_W).astype(jnp.float32)
    col = jnp.tile(jnp.arange(GRID_W), rows).astype(jnp.float32)
    half = ATT_HEAD_DIM // 2
    inv = 1.0 / (ROPE_THETA ** (jnp.arange(0, half, 2, dtype=jnp.float32) / half))
    a_r = row[:, None] * inv[None, :]
    a_c = col[:, None] * inv[None, :]
    ang = jnp.concatenate([a_r, a_r, a_c, a_c], axis=-1)
    return jnp.cos(ang)[:, None, :], jnp.sin(ang)[:, None, :]


def apply_rope(x, cos, sin):
    xf = x.astype(jnp.float32)
    half = ATT_HEAD_DIM // 2
    quarter = ATT_HEAD_DIM // 4

    def rot(z):
        return jnp.concatenate([-z[..., quarter:], z[..., :quarter]], axis=-1)

    rotated = jnp.concatenate([rot(xf[..., :half]), rot(xf[..., half:])], axis=-1)
    return (xf * cos + rotated * sin).astype(x.dtype)


def multiscale_pool(u, pool_w, pool_scale):
    b, s, _ = u.shape
    uf = u.astype(jnp.float32).reshape(b, s, POOL_GROUPS, POOL_GROUP_W)
    cs = jnp.concatenate(
        [jnp.zeros((b, 1, POOL_GROUPS, POOL_GROUP_W), jnp.float32), jnp.cumsum(uf, axis=1)], axis=1
    )
    t = jnp.arange(s)
    means = []
    for gi, w in enumerate(POOL_WINDOWS):
        lo = jnp.clip(t - w // 2, 0, s)
        hi = jnp.clip(t + w // 2, 0, s)
        csg = cs[:, :, gi]
        window_sum = jnp.take(csg, hi, axis=1) - jnp.take(csg, lo, axis=1)
        means.append(window_sum / (hi - lo).astype(jnp.float32)[None, :, None])
    pooled = jnp.stack(means, axis=2) - uf
    y = jnp.einsum("bsgc,gcd->bsgd", pooled, pool_w.astype(jnp.float32))
    y = y.reshape(b, s, POOL_W) * pool_scale.astype(jnp.float32)
    return y.astype(u.dtype)


def blocked_gqa(q, k, v):
    b, s, _, _ = q.shape
    nb = s // Q_BLOCK
    scale = ATT_HEAD_DIM ** -0.5
    qb = q.reshape(b, nb, Q_BLOCK, ATT_KV_HEADS, ATT_GROUP, ATT_HEAD_DIM).transpose(1, 0, 2, 3, 4, 5)

    def one_block(q_blk):
        scores = jnp.einsum("bqkgd,bskd->bkgqs", q_blk, k).astype(jnp.float32) * scale
        probs = jax.nn.softmax(scores, axis=-1).astype(v.dtype)
        return jnp.einsum("bkgqs,bskd->bqkgd", probs, v)

    out = lax.map(one_block, qb)
    return out.transpose(1, 0, 2, 3, 4, 5).reshape(b, s, Q_W)


def mlstm_one_direction(q, k, v, i_pre, f_pre):
    b, h, s, d = q.shape
    L = MLSTM_CHUNK
    nc = s // L

    def chunks(z):
        return jnp.moveaxis(z.reshape((b, h, nc, L) + z.shape[3:]), 2, 0)

    logf = jax.nn.log_sigmoid(f_pre)
    mask = jnp.tril(jnp.ones((L, L), dtype=bool))

    def step(carry, inp):
        c_st, n_st, m_st = carry
        qc, kc, vc, ic, lfc = inp
        bcum = jnp.cumsum(lfc, axis=-1)
        dmat = bcum[..., :, None] - bcum[..., None, :] + ic[..., None, :]
        dmat = jnp.where(mask, dmat, -jnp.inf)
        m_inter = bcum + m_st[..., None]
        m_t = jnp.maximum(m_inter, jnp.max(dmat, axis=-1))
        w = jnp.exp(dmat - m_t[..., None])
        a_inter = jnp.exp(m_inter - m_t)
        scores = jnp.einsum("bhtd,bhsd->bhts", qc, kc) * w
        num = jnp.einsum("bhts,bhse->bhte", scores, vc) + a_inter[..., None] * jnp.einsum(
            "bhtd,bhde->bhte", qc, c_st
        )
        den = jnp.sum(scores, axis=-1) + a_inter * jnp.einsum("bhtd,bhd->bht", qc, n_st)
        h_out = num / jnp.maximum(jnp.abs(den), jnp.exp(-m_t))[..., None]
        b_last = bcum[..., -1]
        g_s = b_last[..., None] - bcum + ic
        m_next = jnp.maximum(b_last + m_st, jnp.max(g_s, axis=-1))
        decay = jnp.exp(b_last + m_st - m_next)
        ws = jnp.exp(g_s - m_next[..., None])
        c_new = decay[..., None, None] * c_st + jnp.einsum("bhs,bhsd,bhse->bhde", ws, kc, vc)
        n_new = decay[..., None] * n_st + jnp.einsum("bhs,bhsd->bhd", ws, kc)
        return (c_new, n_new, m_next), h_out

    init = (
        jnp.zeros((b, h, d, d), jnp.float32),
        jnp.zeros((b, h, d), jnp.float32),
        jnp.zeros((b, h), jnp.float32),
    )
    _, hs = lax.scan(step, init, (chunks(q), chunks(k), chunks(v), chunks(i_pre), chunks(logf)))
    return jnp.moveaxis(hs, 0, 2).reshape(b, h, s, d)


def mlstm_mixer(h, w_in, b_gate, h_norm, w_out):
    b, s, _ = h.shape
    proj = h @ w_in

    def heads(z):
        return z.reshape(b, s, MLSTM_HEADS, MLSTM_HEAD_DIM).transpose(0, 2, 1, 3).astype(jnp.float32)

    q = heads(proj[..., :MLSTM_W]) * (MLSTM_HEAD_DIM ** -0.5)
    k = heads(proj[..., MLSTM_W:2 * MLSTM_W])
    v = heads(proj[..., 2 * MLSTM_W:3 * MLSTM_W])
    o_gate = jax.nn.sigmoid(proj[..., 3 * MLSTM_W:4 * MLSTM_W].astype(jnp.float32))
    gates = proj[..., 4 * MLSTM_W:].astype(jnp.float32) + b_gate.astype(jnp.float32)
    gates = gates.reshape(b, s, 4, MLSTM_HEADS).transpose(2, 0, 3, 1)
    h_fw = mlstm_one_direction(q, k, v, gates[0], gates[1])

    def flip(z):
        return jnp.flip(z, axis=2)

    h_bw = flip(mlstm_one_direction(flip(q), flip(k), flip(v), flip(gates[2]), flip(gates[3])))
    hsum = (h_fw + h_bw).transpose(0, 2, 1, 3)
    hn = hsum * lax.rsqrt(jnp.mean(hsum * hsum, axis=-1, keepdims=True) + NORM_EPS)
    hn = hn * h_norm.astype(jnp.float32).reshape(MLSTM_HEADS, MLSTM_HEAD_DIM)
    y = (hn.reshape(b, s, MLSTM_W) * o_gate).astype(h.dtype)
    return y @ w_out


def conv_ffn(x, w_up, conv_w, conv_b, w_down):
    u = x @ w_up
    up = jnp.pad(u, ((0, 0), (1, 1), (0, 0)))
    c = conv_w.astype(u.dtype)
    u = c[0] * up[:, :-2] + c[1] * up[:, 1:-1] + c[2] * up[:, 2:] + conv_b.astype(u.dtype)
    gate, val = jnp.split(u, 2, axis=-1)
    return (jax.nn.silu(gate) * val) @ w_down


def setup_inputs(seed: int = 0) -> dict:
    key = jax.random.key(seed)
    ks = jax.random.split(key, 20)
    n_even = (DEPTH + 1) // 2
    n_odd = DEPTH // 2
    f32 = jnp.float32

    def dense(k, shape, fan_in):
        return jax.random.normal(k, shape, f32) * (fan_in ** -0.5)

    def gain(k, shape):
        return 1.0 + 0.02 * jax.random.normal(k, shape, f32)

    x = jax.random.normal(ks[0], (BATCH, SEQ, D_MODEL), f32)
    norm_mix = gain(ks[1], (DEPTH, D_MODEL))
    norm_ffn = gain(ks[2], (DEPTH, D_MODEL))
    w_in_ab = dense(ks[3], (n_even, D_MODEL, AB_IN_W), D_MODEL)
    pool_w = dense(ks[4], (n_even, POOL_GROUPS, POOL_GROUP_W, POOL_GROUP_W), POOL_GROUP_W)
    pool_scale = gain(ks[5], (n_even, POOL_W))
    q_norm = gain(ks[6], (n_even, ATT_HEAD_DIM))
    k_norm = gain(ks[7], (n_even, ATT_HEAD_DIM))
    w_out_ab = dense(ks[8], (n_even, AB_OUT_W, D_MODEL), AB_OUT_W)
    w_in_c = dense(ks[9], (n_odd, D_MODEL, C_IN_W), D_MODEL)
    gate_base = jnp.array([0.0, FORGET_BIAS, 0.0, FORGET_BIAS], f32)[None, :, None]
    b_gate_c = (gate_base + 0.1 * jax.random.normal(ks[10], (n_odd, 4, MLSTM_HEADS), f32)).reshape(
        n_odd, 4 * MLSTM_HEADS
    )
    h_norm_c = gain(ks[11], (n_odd, MLSTM_W))
    w_out_c = dense(ks[12], (n_odd, MLSTM_W, D_MODEL), MLSTM_W)
    w_up = dense(ks[13], (DEPTH, D_MODEL, 2 * D_FF), D_MODEL)
    conv_w = dense(ks[14], (DEPTH, CONV_WIDTH, 2 * D_FF), CONV_WIDTH)
    conv_b = 0.02 * jax.random.normal(ks[15], (DEPTH, 2 * D_FF), f32)
    w_down = dense(ks[16], (DEPTH, D_FF, D_MODEL), D_FF)
    return {
        "x": x,
        "norm_mix": norm_mix,
        "norm_ffn": norm_ffn,
        "w_in_ab": w_in_ab,
        "pool_w": pool_w,
        "pool_scale": pool_scale,
        "q_norm": q_norm,
        "k_norm": k_norm,
        "w_out_ab": w_out_ab,
        "w_in_c": w_in_c,
        "b_gate_c": b_gate_c,
        "h_norm_c": h_norm_c,
        "w_out_c": w_out_c,
        "w_up": w_up,
        "conv_w": conv_w,
        "conv_b": conv_b,
        "w_down": w_down,
    }


def reference(x, norm_mix, norm_ffn, w_in_ab, pool_w, pool_scale, q_norm, k_norm, w_out_ab,
              w_in_c, b_gate_c, h_norm_c, w_out_c, w_up, conv_w, conv_b, w_down):
    b, s, _ = x.shape
    cos, sin = axial_rope(s)
    for layer in range(DEPTH):
        h = rms_norm(x, norm_mix[layer])
        if layer % 2 == 0:
            e = layer // 2
            proj = h @ w_in_ab[e]
            u = proj[..., :POOL_W]
            q = proj[..., POOL_W:POOL_W + Q_W].reshape(b, s, ATT_HEADS, ATT_HEAD_DIM)
            k = proj[..., POOL_W + Q_W:POOL_W + Q_W + KV_W].reshape(b, s, ATT_KV_HEADS, ATT_HEAD_DIM)
            v = proj[..., POOL_W + Q_W + KV_W:].reshape(b, s, ATT_KV_HEADS, ATT_HEAD_DIM)
            q = apply_rope(rms_norm(q, q_norm[e]), cos, sin)
            k = apply_rope(rms_norm(k, k_norm[e]), cos, sin)
            pool_out = multiscale_pool(u, pool_w[e], pool_scale[e])
            att_out = blocked_gqa(q, k, v)
            mixed = jnp.concatenate([pool_out, att_out], axis=-1) @ w_out_ab[e]
        else:
            o = layer // 2
            mixed = mlstm_mixer(h, w_in_c[o], b_gate_c[o], h_norm_c[o], w_out_c[o])
        x = x + mixed
        x = x + conv_ffn(rms_norm(x, norm_ffn[layer]), w_up[layer], conv_w[layer], conv_b[layer], w_down[layer])
    return x
```

```python
import numpy as np
import ml_dtypes
from contextlib import ExitStack
import concourse.bass as bass
import concourse.mybir as mybir
from concourse.bass_utils import run_bass_kernel_spmd

F32, BF16 = mybir.dt.float32, mybir.dt.bfloat16
AF = mybir.ActivationFunctionType
ALU = mybir.AluOpType
NPBF = ml_dtypes.bfloat16
NDMA = 12
D = 2048
S = 4096
DFF = 5632
EPS = 1e-6
NCORES = 8


PSUM_KEYS = ("pacc", "ps_ss", "ps_h", "ps_r", "ps_p", "ps_s", "ps_o", "ps_m", "psS", "psN", "psC")


class Res:
    __slots__ = ("w", "rd", "excl")

    def __init__(self, excl=False):
        self.w = None
        self.rd = []
        self.excl = excl


class Prog:
    def __init__(self):
        self.nc = bass.Bass("TRN2", target_bir_lowering=False)
        self.ops = []
        self.st = ExitStack()
        self.res = {}
        self.nm = 0

    def R(self, *key):
        r = self.res.get(key)
        if r is None:
            r = self.res[key] = Res(key[0] in PSUM_KEYS)
        return r

    def sb(self, shape, dt, name=None):
        self.nm += 1
        return self.st.enter_context(self.nc.sbuf_tensor("S_" + (name or f"sb{self.nm}"), list(shape), dt))

    def ps(self, name=None):
        self.nm += 1
        return self.st.enter_context(self.nc.psum_tensor("P_" + (name or f"ps{self.nm}"), [128, 512], F32))

    def din(self, name, shape, dt):
        return self.nc.dram_tensor(name, list(shape), dt, kind="ExternalInput").ap()

    def dout(self, name, shape, dt):
        return self.nc.dram_tensor(name, list(shape), dt, kind="ExternalOutput").ap()

    def op(self, eng, fn, rd=(), wr=()):
        i = len(self.ops)
        deps = set()
        wr = list(wr) + [r for r in rd if r.excl]
        for r in rd:
            if r.w is not None:
                deps.add(r.w)
        for r in wr:
            if r.w is not None:
                deps.add(r.w)
            deps.update(r.rd)
        for r in rd:
            r.rd.append(i)
        for r in wr:
            r.w = i
            r.rd = []
        deps.discard(i)
        self.ops.append((eng, fn, deps))
        return i

    def i(self, eng, meth, kw, rd=(), wr=()):
        return self.op(eng, lambda e: getattr(e, meth)(**kw), rd, wr)

    def dma(self, out, in_, rd=(), wr=()):
        return self.op("sp", lambda e: e.dma_start(out=out, in_=in_), rd, wr)

    def finish(self):
        nc, ops, st = self.nc, self.ops, self.st
        n = len(ops)
        signal = [False] * n
        for (_, _, deps) in ops:
            for d in deps:
                signal[d] = True
        engs = ["pe", "act", "dve", "pool"]
        tok = [None] * n
        cnt = {e: 0 for e in engs}
        ndma = 0
        idx = {e: [] for e in engs + ["sp"]}
        for i, (eng, _, _) in enumerate(ops):
            idx[eng].append(i)
            if eng == "sp":
                tok[i] = (("d", ndma % NDMA), 16 * (ndma // NDMA + 1))
                ndma += 1
            elif signal[i]:
                cnt[eng] += 1
                tok[i] = (eng, cnt[eng])
        sems = {e: st.enter_context(nc.semaphore("s_" + e)) for e in engs}
        for k in range(NDMA):
            sems[("d", k)] = st.enter_context(nc.semaphore(f"s_d{k}"))
        block = st.enter_context(nc.Block())

        def emit(engname, e):
            known = {}
            for i in idx[engname]:
                _, fn, deps = ops[i]
                need = {}
                for d in deps:
                    if engname == "pe" and ops[d][0] == "pe":
                        continue
                    s, v = tok[d]
                    if need.get(s, 0) < v:
                        need[s] = v
                if engname == "sp":
                    s, v = tok[i]
                    if v > 16:
                        need[s] = max(need.get(s, 0), v - 16)
                for s, v in need.items():
                    if known.get(s, 0) < v:
                        e.wait_ge(sems[s], v)
                        known[s] = v
                ins = fn(e)
                if tok[i] is not None:
                    ins.then_inc(sems[tok[i][0]], 16 if engname == "sp" else 1)
            if engname == "sp":
                for k in range(min(NDMA, ndma)):
                    tot = 16 * ((ndma - 1 - k) // NDMA + 1)
                    if known.get(("d", k), 0) < tot:
                        e.wait_ge(sems[("d", k)], tot)

        block.sync(lambda e: emit("sp", e))
        block.tensor(lambda e: emit("pe", e))
        block.scalar(lambda e: emit("act", e))
        block.vector(lambda e: emit("dve", e))
        block.gpsimd(lambda e: emit("pool", e))
        st.close()
        return nc


def mm_group(P, ps_ap, psR, pairs, rd, start=True, stop=True):
    pairs = list(pairs)

    def fn(e):
        n = len(pairs)
        ins = None
        for k, (l, r) in enumerate(pairs):
            ins = e.matmul(ps_ap, lhsT=l, rhs=r, start=(start and k == 0), stop=(stop and k == n - 1))
        return ins

    return P.op("pe", fn, rd=rd, wr=[psR])


class WStream:
    def __init__(self, P, slots):
        self.P = P
        self.slots = slots
        self.stage = P.sb([128, slots * 128], F32, "wstage")
        self.wb = [P.sb([128, slots * 128], BF16, f"wbf{i}") for i in range(2)]
        self.k = 0

    def load(self, pieces, scale_aps=None):
        P = self.P
        i = self.k % 2
        self.k += 1
        sR = P.R("wstage")
        bR = P.R("wbf", i)
        off = 0
        views = []
        for (ap, kc, m) in pieces:
            dst = self.stage[:, off:off + kc * m].rearrange("p (k m) -> p k m", m=m)
            src = ap.rearrange("(k p) m -> p k m", p=128)
            P.dma(dst, src, wr=[sR])
            views.append(self.wb[i][:, off:off + kc * m].rearrange("p (k m) -> p k m", m=m))
            off += kc * m
        eng = "pool" if (self.k % 2) else "act"
        src_all = self.stage[:, 0:off]
        dst_all = self.wb[i][:, 0:off]
        if eng == "pool":
            P.i("pool", "tensor_copy", dict(out=dst_all, in_=src_all), rd=[sR], wr=[bR])
        else:
            P.i("act", "activation", dict(out=dst_all, in_=src_all, func=AF.Copy), rd=[sR], wr=[bR])
        return views, bR


def consts(P):
    c = {}
    c["ones"] = P.sb([128, 128], BF16, "ones")
    c["eps"] = P.sb([128, 1], F32, "epsc")
    P.i("pool", "memset", dict(ap=c["ones"][:], constant=1.0), wr=[P.R("ones")])
    P.i("pool", "memset", dict(ap=c["eps"][:], constant=EPS), wr=[P.R("eps")])
    return c


def blocks_of(T):
    if T % 512 == 0:
        return [(i * 512, 512) for i in range(T // 512)]
    assert T % 3 == 0
    w = T // 3
    return [(i * w, w) for i in range(3)]


def rmsnorm_fm(P, c, xT, xkey, g_sb, hT, hkey, T, ps_ss, dim):
    KC = dim // 128
    blks = blocks_of(T)
    rstd = P.sb([128, T], F32, "rstd_" + hkey)
    lnv = P.sb([128, 512], F32, "lnv_" + hkey)
    for b, (t0, tw) in enumerate(blks):
        for k in range(KC):
            P.i("act", "activation", dict(out=hT[:, k, t0:t0 + tw], in_=xT[:, k, t0:t0 + tw],
                                                            func=AF.Square),
                 rd=[P.R(xkey, k, b)], wr=[P.R(hkey, k, b)])
        mm_group(P, ps_ss[:, 0:tw], P.R("ps_ss"),
                 [(c["ones"][:], hT[:, k, t0:t0 + tw]) for k in range(KC)],
                 rd=[P.R("ones")] + [P.R(hkey, k, b) for k in range(KC)])
        P.i("act", "activation", dict(out=lnv[:, 0:tw], in_=ps_ss[:, 0:tw], func=AF.Ln, bias=c["eps"][:],
                                           scale=1.0 / dim),
             rd=[P.R("ps_ss"), P.R("eps")], wr=[P.R("lnv", hkey)])
        P.i("act", "activation", dict(out=rstd[:, t0:t0 + tw], in_=lnv[:, 0:tw], func=AF.Exp, scale=-0.5),
             rd=[P.R("lnv", hkey)], wr=[P.R("rstd", hkey, b)])
        for k in range(KC):
            P.i("dve", "scalar_tensor_tensor", dict(
                out=hT[:, k, t0:t0 + tw], in0=xT[:, k, t0:t0 + tw], scalar=g_sb[:, k:k + 1],
                in1=rstd[:, t0:t0 + tw], op0=ALU.mult, op1=ALU.mult),
                 rd=[P.R(xkey, k, b), P.R("rstd", hkey, b), P.R("gsb", hkey)], wr=[P.R(hkey, k, b)])


def build_inproj0():
    T = 1024
    P = Prog()
    xT_d = P.din("xT", [D, T], F32)
    g_d = P.din("g", [128, 16], F32)
    w_d = P.din("w", [D, 3072], F32)
    qk_d = P.din("qkn", [128, 2], F32)
    cs_d = P.din("cs", [128, 2, T], F32)
    rt_d = P.din("rt", [128, 128], BF16)
    qT_o = P.dout("qT", [1536, T], BF16)
    kT_o = P.dout("kT", [512, T], BF16)
    vT_o = P.dout("vT", [512, T], BF16)
    uT_o = P.dout("uT", [512, T], BF16)
    c = consts(P)
    xT = P.sb([128, 16, T], F32, "xT")
    hT = P.sb([128, 16, T], BF16, "hT")
    g_sb = P.sb([128, 16], F32, "g_sb")
    qk_sb = P.sb([128, 2], F32, "qk_sb")
    cs = P.sb([128, 2, T], F32, "cs")
    rt = P.sb([128, 128], BF16, "rt")
    blks = blocks_of(T)
    P.dma(g_sb[:], g_d, wr=[P.R("gsb", "h")])
    P.dma(qk_sb[:], qk_d, wr=[P.R("qk")])
    P.dma(cs[:], cs_d, wr=[P.R("cs")])
    P.dma(rt[:], rt_d, wr=[P.R("rt")])
    xv = xT_d.rearrange("(k p) t -> p k t", p=128)
    for k in range(16):
        for b, (t0, tw) in enumerate(blks):
            P.dma(xT[:, k, t0:t0 + tw], xv[:, k, t0:t0 + tw], wr=[P.R("x", k, b)])
    ps_ss = P.ps("ps_ss")
    rmsnorm_fm(P, c, xT, "x", g_sb, hT, "h", T, ps_ss, D)
    ws = WStream(P, 16)
    pacc = [P.ps("pacc0"), P.ps("pacc1")]
    ps_h = P.ps("ps_h")
    ps_r = P.ps("ps_r")
    ev = [P.sb([128, 512], BF16, f"ev{i}") for i in range(2)]
    sqh = P.sb([128, 512], BF16, "sqh")
    qg = P.sb([128, 512], BF16, "qg")
    lnh = P.sb([128, 512], F32, "lnh")
    rsh = P.sb([128, 512], F32, "rsh")
    t1 = P.sb([128, 512], F32, "t1")
    t2 = P.sb([128, 512], F32, "t2")
    it = 0
    for m in range(24):
        (wv,), wR = ws.load([(w_d[:, m * 128:(m + 1) * 128], 16, 128)])
        for b, (t0, tw) in enumerate(blks):
            pa = pacc[it % 2]
            paR = P.R("pacc", it % 2)
            e_ = ev[it % 2]
            eR = P.R("ev", it % 2)
            it += 1
            mm_group(P, pa[:, 0:tw], paR, [(wv[:, k, :], hT[:, k, t0:t0 + tw]) for k in range(16)],
                     rd=[wR] + [P.R("h", k, b) for k in range(16)])
            if m < 4 or m >= 20:
                dst = (uT_o[m * 128:(m + 1) * 128, t0:t0 + tw] if m < 4
                       else vT_o[(m - 20) * 128:(m - 19) * 128, t0:t0 + tw])
                P.i("act", "activation", dict(out=e_[:, 0:tw], in_=pa[:, 0:tw],
                                                                         func=AF.Copy), rd=[paR], wr=[eR])
                P.dma(dst, e_[:, 0:tw], rd=[eR])
            else:
                isq = m < 16
                gi = 0 if isq else 1
                dst = (qT_o[(m - 4) * 128:(m - 3) * 128, t0:t0 + tw] if isq
                       else kT_o[(m - 16) * 128:(m - 15) * 128, t0:t0 + tw])
                P.i("act", "activation", dict(out=sqh[:, 0:tw], in_=pa[:, 0:tw],
                                                                  func=AF.Square), rd=[paR], wr=[P.R("sqh")])
                P.i("dve", "tensor_scalar", dict(
                    out=qg[:, 0:tw], in0=pa[:, 0:tw], scalar1=qk_sb[:, gi:gi + 1], scalar2=None, op0=ALU.mult),
                     rd=[paR, P.R("qk")], wr=[P.R("qg")])
                mm_group(P, ps_h[:, 0:tw], P.R("ps_h"), [(c["ones"][:], sqh[:, 0:tw])], rd=[P.R("ones"), P.R("sqh")])
                mm_group(P, ps_r[:, 0:tw], P.R("ps_r"), [(rt[:], qg[:, 0:tw])], rd=[P.R("rt"), P.R("qg")])
                P.i("act", "activation", dict(out=lnh[:, 0:tw], in_=ps_h[:, 0:tw], func=AF.Ln, bias=c["eps"][:],
                                                   scale=1.0 / 128), rd=[P.R("ps_h"), P.R("eps")], wr=[P.R("lnh")])
                P.i("act", "activation", dict(out=rsh[:, 0:tw], in_=lnh[:, 0:tw], func=AF.Exp, scale=-0.5),
                     rd=[P.R("lnh")], wr=[P.R("rsh")])
                P.i("dve", "tensor_tensor", dict(out=t1[:, 0:tw], in0=qg[:, 0:tw],
                                                                     in1=cs[:, 0, t0:t0 + tw], op=ALU.mult),
                     rd=[P.R("qg"), P.R("cs")], wr=[P.R("t1")])
                P.i("dve", "tensor_tensor", dict(out=t2[:, 0:tw], in0=ps_r[:, 0:tw],
                                                                     in1=cs[:, 1, t0:t0 + tw], op=ALU.mult),
                     rd=[P.R("ps_r"), P.R("cs")], wr=[P.R("t2")])
                P.i("pool", "tensor_tensor", dict(out=t1[:, 0:tw], in0=t1[:, 0:tw], in1=t2[:, 0:tw], op=ALU.add),
                     rd=[P.R("t1"), P.R("t2")], wr=[P.R("t1")])
                P.i("pool", "tensor_tensor", dict(out=e_[:, 0:tw], in0=t1[:, 0:tw],
                                                                      in1=rsh[:, 0:tw], op=ALU.mult),
                     rd=[P.R("t1"), P.R("rsh")], wr=[eR])
                P.dma(dst, e_[:, 0:tw], rd=[eR])
    return P.finish()


def build_attn():
    P = Prog()
    qT_d = P.din("qT", [3, 128, S], BF16)
    kT_d = P.din("kT", [128, S], BF16)
    v_d = P.din("v", [128, 32, 128], BF16)
    uT_d = P.din("uT", [128, S], BF16)
    pw_d = P.din("pw", [128, 128], F32)
    pc_d = P.din("pc", [128, 6], F32)
    ic_d = P.din("ic", [128, S], F32)
    att_o = P.dout("attT", [3, 128, S], BF16)
    pool_o = P.dout("poolT", [128, S], BF16)
    c = consts(P)
    qT = P.sb([128, 3, S], BF16, "qT")
    kT = P.sb([128, S], BF16, "kT")
    v = P.sb([128, 32, 128], BF16, "v")
    pw = P.sb([128, 128], F32, "pw")
    pwb = P.sb([128, 128], BF16, "pwb")
    pc = P.sb([128, 6], F32, "pc")
    for h in range(3):
        P.dma(qT[:, h, :], qT_d[h], wr=[P.R("q", h)])
    P.dma(kT[:], kT_d, wr=[P.R("k")])
    P.dma(v[:], v_d, wr=[P.R("v")])
    P.dma(pw[:], pw_d, wr=[P.R("pw")])
    P.dma(pc[:], pc_d, wr=[P.R("pc")])
    P.i("act", "activation", dict(out=pwb[:], in_=pw[:], func=AF.Copy), rd=[P.R("pw")], wr=[P.R("pwb")])
    W = S + 32
    ub = P.sb([128, S], BF16, "ub")
    u = P.sb([128, W], F32, "u")
    sa = P.sb([128, W], F32, "sa")
    sb_ = P.sb([128, W], F32, "sbb")
    acc = P.sb([128, S], F32, "acc")
    ic = P.sb([128, S], F32, "ic")
    pl = P.sb([128, S], BF16, "pl")
    P.dma(ub[:], uT_d, wr=[P.R("ub")])
    P.dma(ic[:], ic_d, wr=[P.R("ic")])
    for nm, t in (("u", u), ("sa", sa), ("sbb", sb_)):
        P.i("pool", "memset", dict(ap=t[:], constant=0.0), wr=[P.R(nm)])
    P.i("act", "activation", dict(out=u[:, 16:16 + S], in_=ub[:], func=AF.Copy), rd=[P.R("ub")], wr=[P.R("u")])
    P.i("dve", "tensor_tensor", dict(out=sa[:, 1:W], in0=u[:, 0:W - 1], in1=u[:, 1:W], op=ALU.add),
         rd=[P.R("u")], wr=[P.R("sa")])
    P.i("dve", "tensor_scalar", dict(out=acc[:], in0=sa[:, 16:16 + S], scalar1=pc[:, 0:1], scalar2=None,
                                          op0=ALU.mult), rd=[P.R("sa"), P.R("pc")], wr=[P.R("acc")])
    P.i("dve", "tensor_tensor", dict(out=sb_[:, 2:W - 2], in0=sa[:, 1:W - 3], in1=sa[:, 3:W - 1], op=ALU.add),
         rd=[P.R("sa")], wr=[P.R("sbb")])
    P.i("dve", "scalar_tensor_tensor", dict(out=acc[:], in0=sb_[:, 16:16 + S], scalar=pc[:, 1:2], in1=acc[:],
                                                 op0=ALU.mult, op1=ALU.add),
         rd=[P.R("sbb"), P.R("pc"), P.R("acc")], wr=[P.R("acc")])
    P.i("dve", "tensor_tensor", dict(out=sa[:, 4:W - 4], in0=sb_[:, 2:W - 6], in1=sb_[:, 6:W - 2], op=ALU.add),
         rd=[P.R("sbb")], wr=[P.R("sa")])
    P.i("dve", "scalar_tensor_tensor", dict(out=acc[:], in0=sa[:, 16:16 + S], scalar=pc[:, 2:3], in1=acc[:],
                                                 op0=ALU.mult, op1=ALU.add),
         rd=[P.R("sa"), P.R("pc"), P.R("acc")], wr=[P.R("acc")])
    P.i("dve", "tensor_tensor", dict(out=sb_[:, 8:W - 8], in0=sa[:, 4:W - 12], in1=sa[:, 12:W - 4], op=ALU.add),
         rd=[P.R("sa")], wr=[P.R("sbb")])
    P.i("dve", "scalar_tensor_tensor", dict(out=acc[:], in0=sb_[:, 16:16 + S], scalar=pc[:, 3:4], in1=acc[:],
                                                 op0=ALU.mult, op1=ALU.add),
         rd=[P.R("sbb"), P.R("pc"), P.R("acc")], wr=[P.R("acc")])
    P.i("dve", "tensor_tensor", dict(out=acc[:], in0=acc[:], in1=ic[:], op=ALU.mult),
         rd=[P.R("acc"), P.R("ic")], wr=[P.R("acc")])
    P.i("dve", "tensor_tensor", dict(out=pl[:], in0=acc[:], in1=u[:, 16:16 + S], op=ALU.subtract),
         rd=[P.R("acc"), P.R("u")], wr=[P.R("pl")])
    ps_p = P.ps("ps_p")
    pev = [P.sb([128, 512], BF16, f"pev{i}") for i in range(2)]
    for b in range(8):
        mm_group(P, ps_p[:], P.R("ps_p"), [(pwb[:], pl[:, b * 512:(b + 1) * 512])], rd=[P.R("pwb"), P.R("pl")])
        P.i("act", "activation", dict(out=pev[b % 2][:], in_=ps_p[:], func=AF.Identity,
                                                        scale=pc[:, 4:5]),
             rd=[P.R("ps_p"), P.R("pc")], wr=[P.R("pev", b % 2)])
        P.dma(pool_o[:, b * 512:(b + 1) * 512], pev[b % 2][:], rd=[P.R("pev", b % 2)])
    ps_s = [P.ps("ps_s0"), P.ps("ps_s1")]
    ps_o = P.ps("ps_o")
    ps_m = P.ps("ps_m")
    pT = [P.sb([128, 512], BF16, f"pT{i}") for i in range(3)]
    rinv = P.sb([128, 512], F32, "rinv")
    aev = [P.sb([128, 512], BF16, f"aev{i}") for i in range(2)]
    scale = 128 ** -0.5
    it = 0
    ob = 0
    for h in range(3):
        for qb in range(8):
            q_ap = qT[:, h, qb * 512:(qb + 1) * 512]
            for kt in range(32):
                s_ap = ps_s[it % 2]
                sR = P.R("ps_s", it % 2)
                p_ap = pT[it % 3]
                pR = P.R("pT", it % 3)
                it += 1
                mm_group(P, s_ap[:], sR, [(kT[:, kt * 128:(kt + 1) * 128], q_ap)], rd=[P.R("k"), P.R("q", h)])
                P.i("act", "activation", dict(out=p_ap[:], in_=s_ap[:], func=AF.Exp,
                                                                                scale=scale), rd=[sR], wr=[pR])
                mm_group(P, ps_o[:], P.R("ps_o"), [(v[:, kt, :], p_ap[:])], rd=[P.R("v"), pR],
                         start=(kt == 0), stop=(kt == 31))
                mm_group(P, ps_m[:], P.R("ps_m"), [(c["ones"][:], p_ap[:])], rd=[P.R("ones"), pR],
                         start=(kt == 0), stop=(kt == 31))
            P.i("dve", "reciprocal", dict(out=rinv[:], in_=ps_m[:]), rd=[P.R("ps_m")], wr=[P.R("rinv")])
            a_ap = aev[ob % 2]
            aR = P.R("aev", ob % 2)
            ob += 1
            P.i("dve", "tensor_tensor", dict(out=a_ap[:], in0=ps_o[:], in1=rinv[:],
                                                                     op=ALU.mult),
                 rd=[P.R("ps_o"), P.R("rinv")], wr=[aR])
            P.dma(att_o[h, :, qb * 512:(qb + 1) * 512], a_ap[:], rd=[aR])
    return P.finish()


def build_outffn(kind):
    T = 1026
    P = Prog()
    aT_d = P.din("aT", [D, T], BF16)
    xT_d = P.din("xT", [D, T], F32)
    wo_d = P.din("wo", [D, D], F32)
    g_d = P.din("g", [128, 16], F32)
    wu_d = P.din("wu", [D, 2 * DFF], F32)
    cw_d = P.din("cw", [128, 88, 4], F32)
    wd_d = P.din("wd", [DFF, D], F32)
    if kind == "c":
        oT_d = P.din("oT", [D, T], BF16)
        hn_d = P.din("hn", [128, 16], F32)
    out_o = P.dout("oxT", [D, 1024], F32)
    xm_s = P.nc.dram_tensor("xm_s", [D, T], F32, kind="Internal").ap()
    c = consts(P)
    blks = blocks_of(T)
    big = P.sb([128, 44 * 1024], BF16, "big")
    xT = big[:, 0:16 * T * 2].bitcast(F32).rearrange("p (k t) -> p k t", t=T)
    actT = big[:].rearrange("p (j t) -> p j t", t=1024)
    A = P.sb([128, 16, T], BF16, "A")
    g_sb = P.sb([128, 16], F32, "g_sb")
    cw = P.sb([128, 88, 4], F32, "cw")
    P.dma(g_sb[:], g_d, wr=[P.R("gsb", "A")])
    P.dma(cw[:], cw_d, wr=[P.R("cw")])
    xv = xT_d.rearrange("(k p) t -> p k t", p=128)
    av = aT_d.rearrange("(k p) t -> p k t", p=128)
    for k in range(16):
        P.dma(A[:, k, :], av[:, k, :], wr=[P.R("A", k, b) for b in range(3)])
        P.dma(xT[:, k, :], xv[:, k, :], wr=[P.R("x", k, b) for b in range(3)] + [P.R("alias")])
    if kind == "c":
        O = [P.sb([128, T], BF16, f"O{i}") for i in range(2)]
        hn = P.sb([128, 16], F32, "hn")
        ov = oT_d.rearrange("(k p) t -> p k t", p=128)
        P.dma(hn[:], hn_d, wr=[P.R("hn")])
        for k in range(16):
            P.dma(O[k % 2][:], ov[:, k, :], wr=[P.R("O", k % 2)])
            P.i("dve", "scalar_tensor_tensor", dict(
                out=A[:, k, :], in0=A[:, k, :], scalar=hn[:, k:k + 1], in1=O[k % 2][:], op0=ALU.mult, op1=ALU.mult),
                 rd=[P.R("O", k % 2), P.R("hn")] + [P.R("A", k, b) for b in range(3)],
                 wr=[P.R("A", k, b) for b in range(3)])
    ws = WStream(P, 32)
    pacc = [P.ps(f"pacc{i}") for i in range(6)]
    ps_ss = P.ps("ps_ss")
    xmv = xm_s.rearrange("(k p) t -> p k t", p=128)
    it = 0
    for m in range(16):
        (wv,), wR = ws.load([(wo_d[:, m * 128:(m + 1) * 128], 16, 128)])
        for b, (t0, tw) in enumerate(blks):
            pa = pacc[it % 6]
            paR = P.R("pacc", it % 6)
            it += 1
            mm_group(P, pa[:, 0:tw], paR, [(wv[:, k, :], A[:, k, t0:t0 + tw]) for k in range(16)],
                     rd=[wR] + [P.R("A", k, b) for k in range(16)])
            P.i("dve", "tensor_tensor", dict(
                out=xT[:, m, t0:t0 + tw], in0=pa[:, 0:tw], in1=xT[:, m, t0:t0 + tw], op=ALU.add),
                 rd=[paR, P.R("x", m, b)], wr=[P.R("x", m, b)])
        P.dma(xmv[:, m, :], xT[:, m, :], rd=[P.R("x", m, b) for b in range(3)] + [P.R("alias")], wr=[P.R("xm", m)])
    rmsnorm_fm(P, c, xT, "x", g_sb, A, "A", T, ps_ss, D)
    allx = [P.R("x", k, b) for k in range(16) for b in range(3)]
    P.i("pool", "memset", dict(ap=c["eps"][:], constant=EPS), rd=allx + [P.R("eps")], wr=[P.R("alias"), P.R("eps")])
    raw = [P.sb([128, T], F32, f"raw{i}") for i in range(4)]
    tg = P.sb([128, 1024], F32, "tg")
    tv = P.sb([128, 1024], F32, "tv")
    sg = P.sb([128, 1024], F32, "sg")
    for j in range(44):
        (wg, wvv), wR = ws.load([(wu_d[:, j * 128:(j + 1) * 128], 16, 128),
                                 (wu_d[:, DFF + j * 128:DFF + (j + 1) * 128], 16, 128)])
        for gv, wv in enumerate((wg, wvv)):
            r_ap = raw[(j % 2) * 2 + gv]
            rR = P.R("raw", (j % 2) * 2 + gv)
            for b, (t0, tw) in enumerate(blks):
                pa = pacc[it % 6]
                paR = P.R("pacc", it % 6)
                it += 1
                mm_group(P, pa[:, 0:tw], paR, [(wv[:, k, :], A[:, k, t0:t0 + tw]) for k in range(16)],
                         rd=[wR] + [P.R("A", k, b) for k in range(16)])
                P.i("act", "activation", dict(
                    out=r_ap[:, t0:t0 + tw], in_=pa[:, 0:tw], func=AF.Copy), rd=[paR], wr=[rR])
            ch = gv * 44 + j
            t_ap = tg if gv == 0 else tv
            tR = P.R("tg") if gv == 0 else P.R("tv")
            eng = "dve"
            P.i(eng, "tensor_scalar", dict(
                out=t_ap[:], in0=r_ap[:, 1:1025], scalar1=cw[:, ch, 1:2], scalar2=cw[:, ch, 3:4],
                op0=ALU.mult, op1=ALU.add), rd=[rR, P.R("cw")], wr=[tR])
            P.i(eng, "scalar_tensor_tensor", dict(
                out=t_ap[:], in0=r_ap[:, 0:1024], scalar=cw[:, ch, 0:1], in1=t_ap[:], op0=ALU.mult, op1=ALU.add),
                 rd=[rR, P.R("cw"), tR], wr=[tR])
            P.i(eng, "scalar_tensor_tensor", dict(
                out=t_ap[:], in0=r_ap[:, 2:1026], scalar=cw[:, ch, 2:3], in1=t_ap[:], op0=ALU.mult, op1=ALU.add),
                 rd=[rR, P.R("cw"), tR], wr=[tR])
        P.i("act", "activation", dict(out=sg[:], in_=tg[:], func=AF.Silu), rd=[P.R("tg")], wr=[P.R("sg")])
        P.i("dve", "tensor_tensor", dict(out=actT[:, j, :], in0=sg[:], in1=tv[:], op=ALU.mult),
             rd=[P.R("sg"), P.R("tv"), P.R("alias")], wr=[P.R("act", j)])
    xo = [P.sb([128, 1024], F32, f"xo{i}") for i in range(2)]
    for m in range(16):
        P.dma(xo[m % 2][:], xmv[:, m, 1:1025], rd=[P.R("xm", m)], wr=[P.R("xo", m % 2)])
        pas = []
        for half in range(2):
            (wv,), wR = ws.load([(wd_d[half * 2816:(half + 1) * 2816, m * 128:(m + 1) * 128], 22, 128)])
            for b in range(2):
                if half == 0:
                    pas.append((pacc[it % 6], P.R("pacc", it % 6)))
                    it += 1
                pa, paR = pas[b]
                mm_group(P, pa[:], paR, [(wv[:, k, :], actT[:, half * 22 + k, b * 512:(b + 1) * 512]) for k in range(22)],
                         rd=[wR] + [P.R("act", half * 22 + k) for k in range(22)], start=(half == 0), stop=(half == 1))
        for b in range(2):
            pa, paR = pas[b]
            P.i("dve", "tensor_tensor", dict(
                out=xo[m % 2][:, b * 512:(b + 1) * 512], in0=pa[:], in1=xo[m % 2][:, b * 512:(b + 1) * 512],
                op=ALU.add), rd=[paR, P.R("xo", m % 2)], wr=[P.R("xo", m % 2)])
        P.dma(out_o[m * 128:(m + 1) * 128, :], xo[m % 2][:], rd=[P.R("xo", m % 2)])
    return P.finish()


def build_inproj1():
    T = 1024
    P = Prog()
    xT_d = P.din("xT", [D, T], F32)
    g_d = P.din("g", [128, 16], F32)
    w_d = P.din("w", [D, 8224], F32)
    bg_d = P.din("bg", [32, 1], F32)
    o_o = P.dout("pT", [8192, T], BF16)
    gt_o = P.dout("gT", [32, T], F32)
    c = consts(P)
    xT = P.sb([128, 16, T], F32, "xT")
    hT = P.sb([128, 16, T], BF16, "hT")
    g_sb = P.sb([128, 16], F32, "g_sb")
    bg = P.sb([32, 1], F32, "bg")
    blks = blocks_of(T)
    P.dma(g_sb[:], g_d, wr=[P.R("gsb", "h")])
    P.dma(bg[:], bg_d, wr=[P.R("bg")])
    xv = xT_d.rearrange("(k p) t -> p k t", p=128)
    for k in range(16):
        for b, (t0, tw) in enumerate(blks):
            P.dma(xT[:, k, t0:t0 + tw], xv[:, k, t0:t0 + tw], wr=[P.R("x", k, b)])
    ps_ss = P.ps("ps_ss")
    rmsnorm_fm(P, c, xT, "x", g_sb, hT, "h", T, ps_ss, D)
    ws = WStream(P, 16)
    pacc = [P.ps(f"pacc{i}") for i in range(4)]
    ev = [P.sb([128, 512], BF16, f"ev{i}") for i in range(4)]
    gev = P.sb([32, 512], F32, "gev")
    it = 0
    for m in range(65):
        mw = 128 if m < 64 else 32
        (wv,), wR = ws.load([(w_d[:, m * 128:m * 128 + mw], 16, mw)])
        for b, (t0, tw) in enumerate(blks):
            pa = pacc[it % 4]
            paR = P.R("pacc", it % 4)
            e_ = ev[it % 4]
            eR = P.R("ev", it % 4)
            it += 1
            mm_group(P, pa[0:mw, 0:tw], paR, [(wv[:, k, :], hT[:, k, t0:t0 + tw]) for k in range(16)],
                     rd=[wR] + [P.R("h", k, b) for k in range(16)])
            if m == 64:
                P.i("act", "activation", dict(out=gev[:, 0:tw], in_=pa[0:32, 0:tw],
                                                                  func=AF.Identity, bias=bg[:], scale=1.0),
                     rd=[paR, P.R("bg")], wr=[P.R("gev")])
                P.dma(gt_o[:, t0:t0 + tw], gev[:, 0:tw], rd=[P.R("gev")])
            else:
                if m < 16:
                    kw = dict(out=e_[:, 0:tw], in_=pa[:, 0:tw], func=AF.Identity, scale=1.0 / 16.0)
                elif m < 48:
                    kw = dict(out=e_[:, 0:tw], in_=pa[:, 0:tw], func=AF.Copy)
                else:
                    kw = dict(out=e_[:, 0:tw], in_=pa[:, 0:tw], func=AF.Sigmoid)
                P.i("act", "activation", kw, rd=[paR], wr=[eR])
                P.dma(o_o[m * 128:(m + 1) * 128, t0:t0 + tw], e_[:, 0:tw], rd=[eR])
    return P.finish()


def build_mlstm():
    NP = 2
    NCH = 32
    P = Prog()
    qT_d = P.din("qT", [NP, 2, 128, S], BF16)
    kT_d = P.din("kT", [NP, 2, 128, S], BF16)
    k_d = P.din("k", [NP, 128, NCH, 256], BF16)
    v_d = P.din("v", [NP, 128, NCH, 256], BF16)
    gt_d = P.din("gt", [NP, 128, 4, NCH], F32)
    tri_d = P.din("tri", [128, 2, 128], F32)
    hn_o = P.dout("hn", [NP, 128, NCH, 256], BF16)
    c = consts(P)
    tri = P.sb([128, 2, 128], F32, "tri")
    onesf = P.sb([128, 128], F32, "onesf")
    one1 = P.sb([128, 1], F32, "one1")
    P.dma(tri[:], tri_d, wr=[P.R("tri")])
    P.i("pool", "memset", dict(ap=onesf[:], constant=1.0), wr=[P.R("onesf")])
    P.i("pool", "memset", dict(ap=one1[:], constant=1.0), wr=[P.R("one1")])
    qT = P.sb([128, 2, S], BF16, "qT")
    kT = P.sb([128, 2, S], BF16, "kT")
    kk = P.sb([128, NCH, 256], BF16, "kk")
    vx = P.sb([128, NCH, 257], BF16, "vx")
    gt = P.sb([128, 4, NCH], F32, "gt")
    lf = P.sb([128, 2, NCH], F32, "lf")
    bc = P.sb([128, 2, NCH], F32, "bc")
    tot = P.sb([128, 2, NCH], F32, "tot")
    av = P.sb([128, 2, NCH], F32, "av")
    bv = P.sb([128, 2, NCH], F32, "bv")
    b2 = P.sb([128, 2, NCH], F32, "b2")
    dc = P.sb([128, 2, NCH], F32, "dc")
    tmp = P.sb([128, 2, NCH], F32, "tmp")
    hacc = P.sb([128, NCH, 256], F32, "hacc")
    ssq = P.sb([128, NCH], F32, "ssq")
    junk = P.sb([128, 256], F32, "junk")
    psS = [P.ps("psS0"), P.ps("psS1")]
    ps_g = psS[0]
    psN = [P.ps("psN0"), P.ps("psN1")]
    psC = [[P.ps("psC00"), P.ps("psC01")], [P.ps("psC10"), P.ps("psC11")]]
    Cst = [P.sb([128, 2, 257], F32, f"Cst{d}") for d in range(2)]
    Cbf = [P.sb([128, 2, 257], BF16, f"Cbf{d}") for d in range(2)]
    Sm = [P.sb([128, 128], BF16, f"Sm{d}") for d in range(2)]
    k2 = [P.sb([128, 256], BF16, f"k2{d}") for d in range(2)]
    dn = [P.sb([128, 4], F32, f"dn{d}") for d in range(2)]
    hev = [P.sb([128, 256], BF16, f"hev{i}") for i in range(2)]
    for p in range(NP):
        for h in range(2):
            P.dma(qT[:, h, :], qT_d[p, h], wr=[P.R("qT")])
            P.dma(kT[:, h, :], kT_d[p, h], wr=[P.R("kT")])
        P.dma(kk[:], k_d[p], wr=[P.R("kk")])
        P.dma(vx[:, :, 0:256], v_d[p], wr=[P.R("vx")])
        P.i("pool", "memset", dict(ap=vx[:, :, 256:257], constant=1.0), wr=[P.R("vx")], rd=[])
        P.dma(gt[:], gt_d[p], wr=[P.R("gt")])
        for d in range(2):
            P.i("act", "activation", dict(out=lf[:, d, :], in_=gt[:, 2 * d + 1, :], func=AF.Exp,
                                                            scale=-1.0), rd=[P.R("gt")], wr=[P.R("lf")])
        P.i("act", "activation", dict(out=lf[:], in_=lf[:], func=AF.Ln, bias=one1[:], scale=1.0),
             rd=[P.R("lf"), P.R("one1")], wr=[P.R("lf")])
        P.i("dve", "tensor_scalar", dict(out=lf[:], in0=lf[:], scalar1=-1.0, scalar2=None, op0=ALU.mult),
             rd=[P.R("lf")], wr=[P.R("lf")])
        for d in range(2):
            mm_group(P, ps_g[:, d * NCH:(d + 1) * NCH], P.R("psS", 0), [(tri[:, d, :], lf[:, d, :])],
                     rd=[P.R("tri"), P.R("lf")])
        mm_group(P, ps_g[:, 2 * NCH:4 * NCH], P.R("psS", 0), [(onesf[:], lf[:].rearrange("p d c -> p (d c)"))],
                 rd=[P.R("onesf"), P.R("lf")])
        P.i("dve", "tensor_copy", dict(out=bc[:].rearrange("p d c -> p (d c)"), in_=ps_g[:, 0:2 * NCH]),
             rd=[P.R("psS", 0)], wr=[P.R("bc")])
        P.i("dve", "tensor_copy", dict(out=tot[:].rearrange("p d c -> p (d c)"), in_=ps_g[:, 2 * NCH:4 * NCH]),
             rd=[P.R("psS", 0)], wr=[P.R("tot")])
        P.i("act", "activation", dict(out=av[:], in_=bc[:], func=AF.Exp), rd=[P.R("bc")], wr=[P.R("av")])
        P.i("act", "activation", dict(out=dc[:], in_=tot[:], func=AF.Exp), rd=[P.R("tot")], wr=[P.R("dc")])
        for d in range(2):
            P.i("dve", "tensor_tensor", dict(out=tmp[:, d, :], in0=gt[:, 2 * d, :], in1=bc[:, d, :],
                                                               op=ALU.subtract),
                 rd=[P.R("gt"), P.R("bc")], wr=[P.R("tmp")])
        P.i("act", "activation", dict(out=bv[:], in_=tmp[:], func=AF.Exp), rd=[P.R("tmp")], wr=[P.R("bv")])
        P.i("dve", "tensor_tensor", dict(out=tmp[:], in0=tmp[:], in1=tot[:], op=ALU.add),
             rd=[P.R("tmp"), P.R("tot"), P.R("bv")], wr=[P.R("tmp")])
        P.i("act", "activation", dict(out=b2[:], in_=tmp[:], func=AF.Exp), rd=[P.R("tmp")], wr=[P.R("b2")])
        gR = [P.R("av"), P.R("bv"), P.R("b2"), P.R("dc")]
        for d in range(2):
            P.i("pool", "memset", dict(ap=Cst[d][:], constant=0.0), wr=[P.R("Cst", d)])
            P.i("pool", "memset", dict(ap=Cbf[d][:], constant=0.0), wr=[P.R("Cbf", d)])
        for step in range(NCH):
            for d in range(2):
                ch = step if d == 0 else NCH - 1 - step
                cs_ = slice(ch * 128, (ch + 1) * 128)
                S_ap, SR = psS[d], P.R("psS", d)
                N_ap, NR = psN[d], P.R("psN", d)
                mm_group(P, S_ap[:, 0:128], SR, [(kT[:, h, cs_], qT[:, h, cs_]) for h in range(2)],
                         rd=[P.R("kT"), P.R("qT")])
                P.i("dve", "scalar_tensor_tensor", dict(
                    out=Sm[d][:], in0=S_ap[:, 0:128], scalar=bv[:, d, ch:ch + 1], in1=tri[:, d, :],
                    op0=ALU.mult, op1=ALU.mult), rd=[SR, P.R("tri")] + gR, wr=[P.R("Sm", d)])
                mm_group(P, N_ap[:, 0:257], NR,
                         [(Sm[d][:], vx[:, ch, :])] + [(qT[:, h, cs_], Cbf[d][:, h, :]) for h in range(2)],
                         rd=[P.R("Sm", d), P.R("vx"), P.R("qT"), P.R("Cbf", d)])
                P.i("act", "activation", dict(out=dn[d][:, 0:1], in_=N_ap[:, 256:257], func=AF.Abs,
                                              scale=av[:, d, ch:ch + 1]), rd=[NR] + gR, wr=[P.R("dn", d)])
                P.i("dve", "tensor_scalar", dict(
                    out=dn[d][:, 1:2], in0=dn[d][:, 0:1], scalar1=1.0, scalar2=None, op0=ALU.max),
                     rd=[P.R("dn", d)], wr=[P.R("dn", d)])
                P.i("dve", "reciprocal", dict(out=dn[d][:, 2:3], in_=dn[d][:, 1:2]),
                     rd=[P.R("dn", d)], wr=[P.R("dn", d)])
                P.i("dve", "tensor_tensor", dict(
                    out=dn[d][:, 3:4], in0=dn[d][:, 2:3], in1=av[:, d, ch:ch + 1], op=ALU.mult),
                     rd=[P.R("dn", d)] + gR, wr=[P.R("dn", d)])
                if step < NCH // 2:
                    P.i("act", "activation", dict(
                        out=hacc[:, ch, :], in_=N_ap[:, 0:256], func=AF.Identity, scale=dn[d][:, 3:4]),
                         rd=[NR, P.R("dn", d)], wr=[P.R("hacc", ch)])
                else:
                    P.i("dve", "scalar_tensor_tensor", dict(
                        out=hacc[:, ch, :], in0=N_ap[:, 0:256], scalar=dn[d][:, 3:4], in1=hacc[:, ch, :],
                        op0=ALU.mult, op1=ALU.add), rd=[NR, P.R("dn", d), P.R("hacc", ch)], wr=[P.R("hacc", ch)])
                P.i("pool", "tensor_scalar", dict(
                    out=k2[d][:], in0=kk[:, ch, :], scalar1=b2[:, d, ch:ch + 1], scalar2=None, op0=ALU.mult),
                     rd=[P.R("kk")] + gR, wr=[P.R("k2", d)])
                for h in range(2):
                    mm_group(P, psC[d][h][:, 0:257], P.R("psC", d, h), [(k2[d][:, h * 128:(h + 1) * 128], vx[:, ch, :])],
                             rd=[P.R("k2", d), P.R("vx")])
                    P.i("dve", "scalar_tensor_tensor", dict(
                        out=Cst[d][:, h, :], in0=Cst[d][:, h, :], scalar=dc[:, d, ch:ch + 1], in1=psC[d][h][:, 0:257],
                        op0=ALU.mult, op1=ALU.add), rd=[P.R("psC", d, h), P.R("Cst", d)] + gR, wr=[P.R("Cst", d)])
                P.i("act", "activation", dict(out=Cbf[d][:], in_=Cst[d][:], func=AF.Copy),
                     rd=[P.R("Cst", d)], wr=[P.R("Cbf", d)])
        for ch in range(NCH):
            P.i("act", "activation", dict(out=junk[:], in_=hacc[:, ch, :], func=AF.Square,
                                                              accum_out=ssq[:, ch:ch + 1]),
                 rd=[P.R("hacc", ch)], wr=[P.R("junk"), P.R("ssq")])
        P.i("act", "activation", dict(out=ssq[:], in_=ssq[:], func=AF.Ln, bias=c["eps"][:], scale=1.0 / 256),
             rd=[P.R("ssq"), P.R("eps")], wr=[P.R("ssq")])
        P.i("act", "activation", dict(out=ssq[:], in_=ssq[:], func=AF.Exp, scale=-0.5),
             rd=[P.R("ssq")], wr=[P.R("ssq")])
        for ch in range(NCH):
            P.i("dve", "tensor_scalar", dict(
                out=hev[ch % 2][:], in0=hacc[:, ch, :], scalar1=ssq[:, ch:ch + 1], scalar2=None, op0=ALU.mult),
                 rd=[P.R("hacc", ch), P.R("ssq")], wr=[P.R("hev", ch % 2)])
            P.dma(hn_o[p, :, ch, :], hev[ch % 2][:], rd=[P.R("hev", ch % 2)])
    return P.finish()


N_LAUNCH = [0]


def run(nc, in_maps):
    N_LAUNCH[0] += 1
    res = run_bass_kernel_spmd(nc, in_maps, core_ids=list(range(NCORES)))
    return res.results


def pc128(v, kc):
    return np.ascontiguousarray(np.asarray(v, np.float32).reshape(kc, 128).T)


def rope_tables():
    rows = S // 64
    row = np.repeat(np.arange(rows), 64).astype(np.float32)
    col = np.tile(np.arange(64), rows).astype(np.float32)
    inv = (1.0 / (np.float32(10000.0) ** (np.arange(0, 64, 2, dtype=np.float32) / np.float32(64)))).astype(np.float32)
    a_r = row[:, None] * inv[None, :]
    a_c = col[:, None] * inv[None, :]
    ang = np.concatenate([a_r, a_r, a_c, a_c], axis=-1)
    return np.cos(ang).astype(np.float32), np.sin(ang).astype(np.float32)


def rot_matrix_T():
    R = np.zeros((128, 128), np.float32)
    for base in (0, 64):
        for j in range(32):
            R[base + j, base + j + 32] = -1.0
            R[base + j + 32, base + j] = 1.0
    return np.ascontiguousarray(R.T).astype(NPBF)


def halo_T(a_tok, b, q):
    F_ = a_tok.shape[-1]
    out = np.zeros((F_, 1026), a_tok.dtype)
    lo, hi = q * 1024 - 1, q * 1024 + 1025
    l2, h2 = max(lo, 0), min(hi, S)
    out[:, l2 - lo:h2 - lo] = a_tok[b, l2:h2].T
    return out


def ffn_inputs(layer, norm_ffn, w_up, conv_w, conv_b, w_down):
    cw = np.zeros((128, 88, 4), np.float32)
    cwl = np.asarray(conv_w[layer], np.float32)
    cbl = np.asarray(conv_b[layer], np.float32)
    for i in range(3):
        cw[:, :, i] = cwl[i].reshape(88, 128).T
    cw[:, :, 3] = cbl.reshape(88, 128).T
    return {"g": pc128(norm_ffn[layer], 16), "wu": np.ascontiguousarray(w_up[layer], np.float32), "cw": cw,
            "wd": np.ascontiguousarray(w_down[layer], np.float32)}


def kernel(x, norm_mix, norm_ffn, w_in_ab, pool_w, pool_scale, q_norm, k_norm, w_out_ab,
           w_in_c, b_gate_c, h_norm_c, w_out_c, w_up, conv_w, conv_b, w_down):
    x = np.asarray(x, np.float32)
    B = x.shape[0]
    cores = [(c // 4, c % 4) for c in range(NCORES)]
    cos, sin = rope_tables()
    nc = build_inproj0()
    ims = []
    for (b, q) in cores:
        sl = slice(q * 1024, (q + 1) * 1024)
        cs = np.stack([cos[sl].T, sin[sl].T], axis=1)
        ims.append({"xT": np.ascontiguousarray(x[b, sl].T), "g": pc128(norm_mix[0], 16),
                    "w": np.ascontiguousarray(w_in_ab[0], np.float32),
                    "qkn": np.ascontiguousarray(np.stack([q_norm[0], k_norm[0]], axis=1), np.float32),
                    "cs": np.ascontiguousarray(cs, np.float32), "rt": rot_matrix_T()})
    r1 = run(nc, ims)
    qT = np.zeros((B, 1536, S), NPBF)
    kT = np.zeros((B, 512, S), NPBF)
    vT = np.zeros((B, 512, S), NPBF)
    uT = np.zeros((B, 512, S), NPBF)
    for ci, (b, q) in enumerate(cores):
        sl = slice(q * 1024, (q + 1) * 1024)
        qT[b][:, sl] = r1[ci]["qT"]
        kT[b][:, sl] = r1[ci]["kT"]
        vT[b][:, sl] = r1[ci]["vT"]
        uT[b][:, sl] = r1[ci]["uT"]
    nc = build_attn()
    ims = []
    t = np.arange(S)
    for (b, g) in cores:
        w = (2, 4, 8, 16)[g]
        lo = np.clip(t - w // 2, 0, S)
        hi = np.clip(t + w // 2, 0, S)
        ic = np.broadcast_to((1.0 / (hi - lo).astype(np.float32))[None, :], (128, S))
        pc = np.zeros((128, 6), np.float32)
        pc[:, g] = 1.0
        pc[:, 4] = np.asarray(pool_scale[0], np.float32)[g * 128:(g + 1) * 128]
        vg = vT[b][g * 128:(g + 1) * 128]
        v_tm = np.ascontiguousarray(vg.T.reshape(32, 128, 128).transpose(1, 0, 2))
        ims.append({"qT": np.ascontiguousarray(qT[b][g * 384:(g + 1) * 384].reshape(3, 128, S)),
                    "kT": np.ascontiguousarray(kT[b][g * 128:(g + 1) * 128]), "v": v_tm,
                    "uT": np.ascontiguousarray(uT[b][g * 128:(g + 1) * 128]),
                    "pw": np.ascontiguousarray(pool_w[0][g], np.float32), "pc": pc,
                    "ic": np.ascontiguousarray(ic, np.float32)})
    r2 = run(nc, ims)
    cat = np.zeros((B, S, D), NPBF)
    for ci, (b, g) in enumerate(cores):
        cat[b][:, g * 128:(g + 1) * 128] = r2[ci]["poolT"].T
        cat[b][:, 512 + g * 384:512 + (g + 1) * 384] = r2[ci]["attT"].reshape(384, S).T
    nc = build_outffn("ab")
    f0 = ffn_inputs(0, norm_ffn, w_up, conv_w, conv_b, w_down)
    ims = []
    for (b, q) in cores:
        d = {"aT": halo_T(cat, b, q), "xT": halo_T(x, b, q), "wo": np.ascontiguousarray(w_out_ab[0], np.float32)}
        d.update(f0)
        ims.append(d)
    r3 = run(nc, ims)
    x1 = np.zeros((B, S, D), np.float32)
    for ci, (b, q) in enumerate(cores):
        x1[b, q * 1024:(q + 1) * 1024] = r3[ci]["oxT"].T
    nc = build_inproj1()
    ims = []
    for (b, q) in cores:
        sl = slice(q * 1024, (q + 1) * 1024)
        ims.append({"xT": np.ascontiguousarray(x1[b, sl].T), "g": pc128(norm_mix[1], 16),
                    "w": np.ascontiguousarray(w_in_c[0], np.float32),
                    "bg": np.ascontiguousarray(np.asarray(b_gate_c[0], np.float32).reshape(32, 1))})
    r4 = run(nc, ims)
    pT = np.zeros((B, 8192, S), NPBF)
    gT = np.zeros((B, 32, S), np.float32)
    for ci, (b, q) in enumerate(cores):
        sl = slice(q * 1024, (q + 1) * 1024)
        pT[b][:, sl] = r4[ci]["pT"]
        gT[b][:, sl] = r4[ci]["gT"]
    nc = build_mlstm()
    tri = np.zeros((128, 2, 128), np.float32)
    ii = np.arange(128)
    tri[:, 0, :] = (ii[:, None] <= ii[None, :])
    tri[:, 1, :] = (ii[:, None] >= ii[None, :])
    ims = []
    pairs = [(p // 8, p % 8) for p in range(16)]
    for ci in range(NCORES):
        d = {k: [] for k in ("qT", "kT", "k", "v", "gt")}
        for (b, h) in pairs[2 * ci:2 * ci + 2]:
            qh = pT[b][h * 256:(h + 1) * 256]
            kh = pT[b][2048 + h * 256:2048 + (h + 1) * 256]
            vh = pT[b][4096 + h * 256:4096 + (h + 1) * 256]
            d["qT"].append(qh.reshape(2, 128, S))
            d["kT"].append(kh.reshape(2, 128, S))
            d["k"].append(kh.T.reshape(32, 128, 256).transpose(1, 0, 2))
            d["v"].append(vh.T.reshape(32, 128, 256).transpose(1, 0, 2))
            gg = gT[b].reshape(4, 8, S)[:, h]
            d["gt"].append(gg.reshape(4, 32, 128).transpose(2, 0, 1))
        im = {k: np.ascontiguousarray(np.stack(v_)) for k, v_ in d.items()}
        im["tri"] = tri
        ims.append(im)
    r5 = run(nc, ims)
    hn = np.zeros((B, S, D), NPBF)
    for ci in range(NCORES):
        for j, (b, h) in enumerate(pairs[2 * ci:2 * ci + 2]):
            hh = r5[ci]["hn"][j]
            hn[b][:, h * 256:(h + 1) * 256] = hh.transpose(1, 0, 2).reshape(S, 256)
    og = np.ascontiguousarray(pT[:, 6144:8192].transpose(0, 2, 1))
    nc = build_outffn("c")
    f1 = ffn_inputs(1, norm_ffn, w_up, conv_w, conv_b, w_down)
    ims = []
    for (b, q) in cores:
        d = {"aT": halo_T(hn, b, q), "oT": halo_T(og, b, q), "xT": halo_T(x1, b, q),
             "hn": pc128(h_norm_c[0], 16), "wo": np.ascontiguousarray(w_out_c[0], np.float32)}
        d.update(f1)
        ims.append(d)
    r6 = run(nc, ims)
    out = np.zeros((B, S, D), np.float32)
    for ci, (b, q) in enumerate(cores):
        out[b, q * 1024:(q + 1) * 1024] = r6[ci]["oxT"].T
    return out
```

```python
import numpy as np
import ml_dtypes
from contextlib import ExitStack
import concourse.bass as bass
import concourse.mybir as mybir
from concourse.bass_utils import run_bass_kernel_spmd

F32, BF16 = mybir.dt.float32, mybir.dt.bfloat16
AF = mybir.ActivationFunctionType
ALU = mybir.AluOpType
NPBF = ml_dtypes.bfloat16
NDMA = 12
D = 2048
S = 4096
DFF = 5632
EPS = 1e-6
NCORES = 8


PSUM_KEYS = ("pacc", "ps_ss", "ps_h", "ps_r", "ps_p", "ps_s", "ps_o", "ps_m", "psS", "psN", "psC")


class Res:
    __slots__ = ("w", "rd", "excl")

    def __init__(self, excl=False):
        self.w = None
        self.rd = []
        self.excl = excl


class Prog:
    def __init__(self):
        self.nc = bass.Bass("TRN2", target_bir_lowering=False)
        self.ops = []
        self.st = ExitStack()
        self.res = {}
        self.nm = 0

    def R(self, *key):
        r = self.res.get(key)
        if r is None:
            r = self.res[key] = Res(key[0] in PSUM_KEYS)
        return r

    def sb(self, shape, dt, name=None):
        self.nm += 1
        return self.st.enter_context(self.nc.sbuf_tensor("S_" + (name or f"sb{self.nm}"), list(shape), dt))

    def ps(self, name=None):
        self.nm += 1
        return self.st.enter_context(self.nc.psum_tensor("P_" + (name or f"ps{self.nm}"), [128, 512], F32))

    def din(self, name, shape, dt):
        return self.nc.dram_tensor(name, list(shape), dt, kind="ExternalInput").ap()

    def dout(self, name, shape, dt):
        return self.nc.dram_tensor(name, list(shape), dt, kind="ExternalOutput").ap()

    def op(self, eng, fn, rd=(), wr=()):
        i = len(self.ops)
        deps = set()
        wr = list(wr) + [r for r in rd if r.excl]
        for r in rd:
            if r.w is not None:
                deps.add(r.w)
        for r in wr:
            if r.w is not None:
                deps.add(r.w)
            deps.update(r.rd)
        for r in rd:
            r.rd.append(i)
        for r in wr:
            r.w = i
            r.rd = []
        deps.discard(i)
        self.ops.append((eng, fn, deps))
        return i

    def i(self, eng, meth, kw, rd=(), wr=()):
        return self.op(eng, lambda e: getattr(e, meth)(**kw), rd, wr)

    def dma(self, out, in_, rd=(), wr=()):
        return self.op("sp", lambda e: e.dma_start(out=out, in_=in_), rd, wr)

    def finish(self):
        nc, ops, st = self.nc, self.ops, self.st
        n = len(ops)
        signal = [False] * n
        for (_, _, deps) in ops:
            for d in deps:
                signal[d] = True
        engs = ["pe", "act", "dve", "pool"]
        tok = [None] * n
        cnt = {e: 0 for e in engs}
        ndma = 0
        idx = {e: [] for e in engs + ["sp"]}
        for i, (eng, _, _) in enumerate(ops):
            idx[eng].append(i)
            if eng == "sp":
                tok[i] = (("d", ndma % NDMA), 16 * (ndma // NDMA + 1))
                ndma += 1
            elif signal[i]:
                cnt[eng] += 1
                tok[i] = (eng, cnt[eng])
        sems = {e: st.enter_context(nc.semaphore("s_" + e)) for e in engs}
        for k in range(NDMA):
            sems[("d", k)] = st.enter_context(nc.semaphore(f"s_d{k}"))
        block = st.enter_context(nc.Block())

        def emit(engname, e):
            known = {}
            for i in idx[engname]:
                _, fn, deps = ops[i]
                need = {}
                for d in deps:
                    if engname == "pe" and ops[d][0] == "pe":
                        continue
                    s, v = tok[d]
                    if need.get(s, 0) < v:
                        need[s] = v
                if engname == "sp":
                    s, v = tok[i]
                    if v > 16:
                        need[s] = max(need.get(s, 0), v - 16)
                for s, v in need.items():
                    if known.get(s, 0) < v:
                        e.wait_ge(sems[s], v)
                        known[s] = v
                ins = fn(e)
                if tok[i] is not None:
                    ins.then_inc(sems[tok[i][0]], 16 if engname == "sp" else 1)
            if engname == "sp":
                for k in range(min(NDMA, ndma)):
                    tot = 16 * ((ndma - 1 - k) // NDMA + 1)
                    if known.get(("d", k), 0) < tot:
                        e.wait_ge(sems[("d", k)], tot)

        block.sync(lambda e: emit("sp", e))
        block.tensor(lambda e: emit("pe", e))
        block.scalar(lambda e: emit("act", e))
        block.vector(lambda e: emit("dve", e))
        block.gpsimd(lambda e: emit("pool", e))
        st.close()
        return nc


def mm_group(P, ps_ap, psR, pairs, rd, start=True, stop=True):
    pairs = list(pairs)

    def fn(e):
        n = len(pairs)
        ins = None
        for k, (l, r) in enumerate(pairs):
            ins = e.matmul(ps_ap, lhsT=l, rhs=r, start=(start and k == 0), stop=(stop and k == n - 1))
        return ins

    return P.op("pe", fn, rd=rd, wr=[psR])


class WStream:
    def __init__(self, P, slots):
        self.P = P
        self.slots = slots
        self.stage = [P.sb([128, slots * 128], F32, f"wstage{i}") for i in range(2)]
        self.wb = [P.sb([128, slots * 128], BF16, f"wbf{i}") for i in range(2)]
        self.k = 0

    def load(self, pieces, scale_aps=None):
        P = self.P
        i = self.k % 2
        self.k += 1
        sR = P.R("wstage", i)
        stage = self.stage[i]
        bR = P.R("wbf", i)
        off = 0
        views = []
        for (ap, kc, m) in pieces:
            dst = stage[:, off:off + kc * m].rearrange("p (k m) -> p k m", m=m)
            src = ap.rearrange("(k p) m -> p k m", p=128)
            P.dma(dst, src, wr=[sR])
            views.append(self.wb[i][:, off:off + kc * m].rearrange("p (k m) -> p k m", m=m))
            off += kc * m
        eng = "pool" if (self.k % 2) else "act"
        src_all = stage[:, 0:off]
        dst_all = self.wb[i][:, 0:off]
        if eng == "pool":
            P.i("pool", "tensor_copy", dict(out=dst_all, in_=src_all), rd=[sR], wr=[bR])
        else:
            P.i("act", "activation", dict(out=dst_all, in_=src_all, func=AF.Copy), rd=[sR], wr=[bR])
        return views, bR


def prefetched(ws, specs):
    nxt = ws.load(specs[0])
    for i in range(len(specs)):
        cur = nxt
        if i + 1 < len(specs):
            nxt = ws.load(specs[i + 1])
        yield cur


def consts(P):
    c = {}
    c["ones"] = P.sb([128, 128], BF16, "ones")
    c["eps"] = P.sb([128, 1], F32, "epsc")
    P.i("pool", "memset", dict(ap=c["ones"][:], constant=1.0), wr=[P.R("ones")])
    P.i("pool", "memset", dict(ap=c["eps"][:], constant=EPS), wr=[P.R("eps")])
    return c


def blocks_of(T):
    if T % 512 == 0:
        return [(i * 512, 512) for i in range(T // 512)]
    assert T % 3 == 0
    w = T // 3
    return [(i * w, w) for i in range(3)]


def rmsnorm_fm(P, c, xT, xkey, g_sb, hT, hkey, T, ps_ss, dim):
    KC = dim // 128
    blks = blocks_of(T)
    rstd = P.sb([128, T], F32, "rstd_" + hkey)
    lnv = P.sb([128, 512], F32, "lnv_" + hkey)
    for b, (t0, tw) in enumerate(blks):
        for k in range(KC):
            P.i("act", "activation", dict(out=hT[:, k, t0:t0 + tw], in_=xT[:, k, t0:t0 + tw],
                                                            func=AF.Square),
                 rd=[P.R(xkey, k, b)], wr=[P.R(hkey, k, b)])
        mm_group(P, ps_ss[:, 0:tw], P.R("ps_ss"),
                 [(c["ones"][:], hT[:, k, t0:t0 + tw]) for k in range(KC)],
                 rd=[P.R("ones")] + [P.R(hkey, k, b) for k in range(KC)])
        P.i("act", "activation", dict(out=lnv[:, 0:tw], in_=ps_ss[:, 0:tw], func=AF.Ln, bias=c["eps"][:],
                                           scale=1.0 / dim),
             rd=[P.R("ps_ss"), P.R("eps")], wr=[P.R("lnv", hkey)])
        P.i("act", "activation", dict(out=rstd[:, t0:t0 + tw], in_=lnv[:, 0:tw], func=AF.Exp, scale=-0.5),
             rd=[P.R("lnv", hkey)], wr=[P.R("rstd", hkey, b)])
        for k in range(KC):
            P.i("dve", "scalar_tensor_tensor", dict(
                out=hT[:, k, t0:t0 + tw], in0=xT[:, k, t0:t0 + tw], scalar=g_sb[:, k:k + 1],
                in1=rstd[:, t0:t0 + tw], op0=ALU.mult, op1=ALU.mult),
                 rd=[P.R(xkey, k, b), P.R("rstd", hkey, b), P.R("gsb", hkey)], wr=[P.R(hkey, k, b)])


def build_inproj0():
    T = 1024
    P = Prog()
    xT_d = P.din("xT", [D, T], F32)
    g_d = P.din("g", [128, 16], F32)
    w_d = P.din("w", [D, 3072], F32)
    qk_d = P.din("qkn", [128, 2], F32)
    cs_d = P.din("cs", [128, 2, T], F32)
    rt_d = P.din("rt", [128, 128], BF16)
    qT_o = P.dout("qT", [1536, T], BF16)
    kT_o = P.dout("kT", [512, T], BF16)
    vT_o = P.dout("vT", [512, T], BF16)
    uT_o = P.dout("uT", [512, T], BF16)
    c = consts(P)
    xT = P.sb([128, 16, T], F32, "xT")
    hT = P.sb([128, 16, T], BF16, "hT")
    g_sb = P.sb([128, 16], F32, "g_sb")
    qk_sb = P.sb([128, 2], F32, "qk_sb")
    cs = P.sb([128, 2, T], F32, "cs")
    rt = P.sb([128, 128], BF16, "rt")
    blks = blocks_of(T)
    P.dma(g_sb[:], g_d, wr=[P.R("gsb", "h")])
    P.dma(qk_sb[:], qk_d, wr=[P.R("qk")])
    P.dma(cs[:], cs_d, wr=[P.R("cs")])
    P.dma(rt[:], rt_d, wr=[P.R("rt")])
    xv = xT_d.rearrange("(k p) t -> p k t", p=128)
    for k in range(16):
        for b, (t0, tw) in enumerate(blks):
            P.dma(xT[:, k, t0:t0 + tw], xv[:, k, t0:t0 + tw], wr=[P.R("x", k, b)])
    ps_ss = P.ps("ps_ss")
    rmsnorm_fm(P, c, xT, "x", g_sb, hT, "h", T, ps_ss, D)
    ws = WStream(P, 16)
    pacc = [P.ps("pacc0"), P.ps("pacc1")]
    ps_h = P.ps("ps_h")
    ps_r = P.ps("ps_r")
    ev = [P.sb([128, 512], BF16, f"ev{i}") for i in range(2)]
    sqh = P.sb([128, 512], BF16, "sqh")
    qg = P.sb([128, 512], BF16, "qg")
    lnh = P.sb([128, 512], F32, "lnh")
    rsh = P.sb([128, 512], F32, "rsh")
    t1 = P.sb([128, 512], F32, "t1")
    t2 = P.sb([128, 512], F32, "t2")
    it = 0
    for m, ((wv,), wR) in enumerate(prefetched(ws, [[(w_d[:, m * 128:(m + 1) * 128], 16, 128)] for m in range(24)])):
        for b, (t0, tw) in enumerate(blks):
            pa = pacc[it % 2]
            paR = P.R("pacc", it % 2)
            e_ = ev[it % 2]
            eR = P.R("ev", it % 2)
            it += 1
            mm_group(P, pa[:, 0:tw], paR, [(wv[:, k, :], hT[:, k, t0:t0 + tw]) for k in range(16)],
                     rd=[wR] + [P.R("h", k, b) for k in range(16)])
            if m < 4 or m >= 20:
                dst = (uT_o[m * 128:(m + 1) * 128, t0:t0 + tw] if m < 4
                       else vT_o[(m - 20) * 128:(m - 19) * 128, t0:t0 + tw])
                P.i("act", "activation", dict(out=e_[:, 0:tw], in_=pa[:, 0:tw],
                                                                         func=AF.Copy), rd=[paR], wr=[eR])
                P.dma(dst, e_[:, 0:tw], rd=[eR])
            else:
                isq = m < 16
                gi = 0 if isq else 1
                dst = (qT_o[(m - 4) * 128:(m - 3) * 128, t0:t0 + tw] if isq
                       else kT_o[(m - 16) * 128:(m - 15) * 128, t0:t0 + tw])
                P.i("act", "activation", dict(out=sqh[:, 0:tw], in_=pa[:, 0:tw],
                                                                  func=AF.Square), rd=[paR], wr=[P.R("sqh")])
                P.i("dve", "tensor_scalar", dict(
                    out=qg[:, 0:tw], in0=pa[:, 0:tw], scalar1=qk_sb[:, gi:gi + 1], scalar2=None, op0=ALU.mult),
                     rd=[paR, P.R("qk")], wr=[P.R("qg")])
                mm_group(P, ps_h[:, 0:tw], P.R("ps_h"), [(c["ones"][:], sqh[:, 0:tw])], rd=[P.R("ones"), P.R("sqh")])
                mm_group(P, ps_r[:, 0:tw], P.R("ps_r"), [(rt[:], qg[:, 0:tw])], rd=[P.R("rt"), P.R("qg")])
                P.i("act", "activation", dict(out=lnh[:, 0:tw], in_=ps_h[:, 0:tw], func=AF.Ln, bias=c["eps"][:],
                                                   scale=1.0 / 128), rd=[P.R("ps_h"), P.R("eps")], wr=[P.R("lnh")])
                P.i("act", "activation", dict(out=rsh[:, 0:tw], in_=lnh[:, 0:tw], func=AF.Exp, scale=-0.5),
                     rd=[P.R("lnh")], wr=[P.R("rsh")])
                P.i("dve", "tensor_tensor", dict(out=t1[:, 0:tw], in0=qg[:, 0:tw],
                                                                     in1=cs[:, 0, t0:t0 + tw], op=ALU.mult),
                     rd=[P.R("qg"), P.R("cs")], wr=[P.R("t1")])
                P.i("dve", "tensor_tensor", dict(out=t2[:, 0:tw], in0=ps_r[:, 0:tw],
                                                                     in1=cs[:, 1, t0:t0 + tw], op=ALU.mult),
                     rd=[P.R("ps_r"), P.R("cs")], wr=[P.R("t2")])
                P.i("pool", "tensor_tensor", dict(out=t1[:, 0:tw], in0=t1[:, 0:tw], in1=t2[:, 0:tw], op=ALU.add),
                     rd=[P.R("t1"), P.R("t2")], wr=[P.R("t1")])
                P.i("pool", "tensor_tensor", dict(out=e_[:, 0:tw], in0=t1[:, 0:tw],
                                                                      in1=rsh[:, 0:tw], op=ALU.mult),
                     rd=[P.R("t1"), P.R("rsh")], wr=[eR])
                P.dma(dst, e_[:, 0:tw], rd=[eR])
    return P.finish()


def build_attn():
    P = Prog()
    qT_d = P.din("qT", [3, 128, S], BF16)
    kT_d = P.din("kT", [128, S], BF16)
    v_d = P.din("v", [128, 32, 128], BF16)
    uT_d = P.din("uT", [128, S], BF16)
    pw_d = P.din("pw", [128, 128], F32)
    pc_d = P.din("pc", [128, 6], F32)
    ic_d = P.din("ic", [128, S], F32)
    att_o = P.dout("attT", [3, 128, S], BF16)
    pool_o = P.dout("poolT", [128, S], BF16)
    c = consts(P)
    qT = P.sb([128, 3, S], BF16, "qT")
    kT = P.sb([128, S], BF16, "kT")
    v = P.sb([128, 32, 128], BF16, "v")
    pw = P.sb([128, 128], F32, "pw")
    pwb = P.sb([128, 128], BF16, "pwb")
    pc = P.sb([128, 6], F32, "pc")
    for h in range(3):
        P.dma(qT[:, h, :], qT_d[h], wr=[P.R("q", h)])
    P.dma(kT[:], kT_d, wr=[P.R("k")])
    P.dma(v[:], v_d, wr=[P.R("v")])
    P.dma(pw[:], pw_d, wr=[P.R("pw")])
    P.dma(pc[:], pc_d, wr=[P.R("pc")])
    P.i("act", "activation", dict(out=pwb[:], in_=pw[:], func=AF.Copy), rd=[P.R("pw")], wr=[P.R("pwb")])
    W = S + 32
    ub = P.sb([128, S], BF16, "ub")
    u = P.sb([128, W], F32, "u")
    sa = P.sb([128, W], F32, "sa")
    sb_ = P.sb([128, W], F32, "sbb")
    acc = P.sb([128, S], F32, "acc")
    ic = P.sb([128, S], F32, "ic")
    pl = P.sb([128, S], BF16, "pl")
    P.dma(ub[:], uT_d, wr=[P.R("ub")])
    P.dma(ic[:], ic_d, wr=[P.R("ic")])
    for nm, t in (("u", u), ("sa", sa), ("sbb", sb_)):
        P.i("pool", "memset", dict(ap=t[:], constant=0.0), wr=[P.R(nm)])
    P.i("act", "activation", dict(out=u[:, 16:16 + S], in_=ub[:], func=AF.Copy), rd=[P.R("ub")], wr=[P.R("u")])
    P.i("dve", "tensor_tensor", dict(out=sa[:, 1:W], in0=u[:, 0:W - 1], in1=u[:, 1:W], op=ALU.add),
         rd=[P.R("u")], wr=[P.R("sa")])
    P.i("dve", "tensor_scalar", dict(out=acc[:], in0=sa[:, 16:16 + S], scalar1=pc[:, 0:1], scalar2=None,
                                          op0=ALU.mult), rd=[P.R("sa"), P.R("pc")], wr=[P.R("acc")])
    P.i("dve", "tensor_tensor", dict(out=sb_[:, 2:W - 2], in0=sa[:, 1:W - 3], in1=sa[:, 3:W - 1], op=ALU.add),
         rd=[P.R("sa")], wr=[P.R("sbb")])
    P.i("dve", "scalar_tensor_tensor", dict(out=acc[:], in0=sb_[:, 16:16 + S], scalar=pc[:, 1:2], in1=acc[:],
                                                 op0=ALU.mult, op1=ALU.add),
         rd=[P.R("sbb"), P.R("pc"), P.R("acc")], wr=[P.R("acc")])
    P.i("dve", "tensor_tensor", dict(out=sa[:, 4:W - 4], in0=sb_[:, 2:W - 6], in1=sb_[:, 6:W - 2], op=ALU.add),
         rd=[P.R("sbb")], wr=[P.R("sa")])
    P.i("dve", "scalar_tensor_tensor", dict(out=acc[:], in0=sa[:, 16:16 + S], scalar=pc[:, 2:3], in1=acc[:],
                                                 op0=ALU.mult, op1=ALU.add),
         rd=[P.R("sa"), P.R("pc"), P.R("acc")], wr=[P.R("acc")])
    P.i("dve", "tensor_tensor", dict(out=sb_[:, 8:W - 8], in0=sa[:, 4:W - 12], in1=sa[:, 12:W - 4], op=ALU.add),
         rd=[P.R("sa")], wr=[P.R("sbb")])
    P.i("dve", "scalar_tensor_tensor", dict(out=acc[:], in0=sb_[:, 16:16 + S], scalar=pc[:, 3:4], in1=acc[:],
                                                 op0=ALU.mult, op1=ALU.add),
         rd=[P.R("sbb"), P.R("pc"), P.R("acc")], wr=[P.R("acc")])
    P.i("dve", "tensor_tensor", dict(out=acc[:], in0=acc[:], in1=ic[:], op=ALU.mult),
         rd=[P.R("acc"), P.R("ic")], wr=[P.R("acc")])
    P.i("dve", "tensor_tensor", dict(out=pl[:], in0=acc[:], in1=u[:, 16:16 + S], op=ALU.subtract),
         rd=[P.R("acc"), P.R("u")], wr=[P.R("pl")])
    ps_p = P.ps("ps_p")
    pev = [P.sb([128, 512], BF16, f"pev{i}") for i in range(2)]
    for b in range(8):
        mm_group(P, ps_p[:], P.R("ps_p"), [(pwb[:], pl[:, b * 512:(b + 1) * 512])], rd=[P.R("pwb"), P.R("pl")])
        P.i("act", "activation", dict(out=pev[b % 2][:], in_=ps_p[:], func=AF.Identity,
                                                        scale=pc[:, 4:5]),
             rd=[P.R("ps_p"), P.R("pc")], wr=[P.R("pev", b % 2)])
        P.dma(pool_o[:, b * 512:(b + 1) * 512], pev[b % 2][:], rd=[P.R("pev", b % 2)])
    ps_s = [P.ps("ps_s0"), P.ps("ps_s1")]
    ps_o = P.ps("ps_o")
    ps_m = P.ps("ps_m")
    pT = [P.sb([128, 512], BF16, f"pT{i}") for i in range(3)]
    rinv = P.sb([128, 512], F32, "rinv")
    aev = [P.sb([128, 512], BF16, f"aev{i}") for i in range(2)]
    scale = 128 ** -0.5
    it = 0
    ob = 0
    for h in range(3):
        for qb in range(8):
            q_ap = qT[:, h, qb * 512:(qb + 1) * 512]
            for kt in range(32):
                s_ap = ps_s[it % 2]
                sR = P.R("ps_s", it % 2)
                p_ap = pT[it % 3]
                pR = P.R("pT", it % 3)
                it += 1
                mm_group(P, s_ap[:], sR, [(kT[:, kt * 128:(kt + 1) * 128], q_ap)], rd=[P.R("k"), P.R("q", h)])
                P.i("act", "activation", dict(out=p_ap[:], in_=s_ap[:], func=AF.Exp,
                                                                                scale=scale), rd=[sR], wr=[pR])
                mm_group(P, ps_o[:], P.R("ps_o"), [(v[:, kt, :], p_ap[:])], rd=[P.R("v"), pR],
                         start=(kt == 0), stop=(kt == 31))
                mm_group(P, ps_m[:], P.R("ps_m"), [(c["ones"][:], p_ap[:])], rd=[P.R("ones"), pR],
                         start=(kt == 0), stop=(kt == 31))
            P.i("dve", "reciprocal", dict(out=rinv[:], in_=ps_m[:]), rd=[P.R("ps_m")], wr=[P.R("rinv")])
            a_ap = aev[ob % 2]
            aR = P.R("aev", ob % 2)
            ob += 1
            P.i("dve", "tensor_tensor", dict(out=a_ap[:], in0=ps_o[:], in1=rinv[:],
                                                                     op=ALU.mult),
                 rd=[P.R("ps_o"), P.R("rinv")], wr=[aR])
            P.dma(att_o[h, :, qb * 512:(qb + 1) * 512], a_ap[:], rd=[aR])
    return P.finish()


def build_outffn(kind):
    T = 1026
    P = Prog()
    aT_d = P.din("aT", [D, T], BF16)
    xT_d = P.din("xT", [D, T], F32)
    wo_d = P.din("wo", [D, D], F32)
    g_d = P.din("g", [128, 16], F32)
    wu_d = P.din("wu", [D, 2 * DFF], F32)
    cw_d = P.din("cw", [128, 88, 4], F32)
    wd_d = P.din("wd", [DFF, D], F32)
    if kind == "c":
        oT_d = P.din("oT", [D, T], BF16)
        hn_d = P.din("hn", [128, 16], F32)
    out_o = P.dout("oxT", [D, 1024], F32)
    xm_s = P.nc.dram_tensor("xm_s", [D, T], F32, kind="Internal").ap()
    c = consts(P)
    blks = blocks_of(T)
    big = P.sb([128, 44 * 1024], BF16, "big")
    xT = big[:, 0:16 * T * 2].bitcast(F32).rearrange("p (k t) -> p k t", t=T)
    actT = big[:].rearrange("p (j t) -> p j t", t=1024)
    A = P.sb([128, 16, T], BF16, "A")
    g_sb = P.sb([128, 16], F32, "g_sb")
    cw = P.sb([128, 88, 4], F32, "cw")
    P.dma(g_sb[:], g_d, wr=[P.R("gsb", "A")])
    P.dma(cw[:], cw_d, wr=[P.R("cw")])
    xv = xT_d.rearrange("(k p) t -> p k t", p=128)
    av = aT_d.rearrange("(k p) t -> p k t", p=128)
    for k in range(16):
        P.dma(A[:, k, :], av[:, k, :], wr=[P.R("A", k, b) for b in range(3)])
        P.dma(xT[:, k, :], xv[:, k, :], wr=[P.R("x", k, b) for b in range(3)] + [P.R("alias")])
    if kind == "c":
        O = [P.sb([128, T], BF16, f"O{i}") for i in range(2)]
        hn = P.sb([128, 16], F32, "hn")
        ov = oT_d.rearrange("(k p) t -> p k t", p=128)
        P.dma(hn[:], hn_d, wr=[P.R("hn")])
        for k in range(16):
            P.dma(O[k % 2][:], ov[:, k, :], wr=[P.R("O", k % 2)])
            P.i("dve", "scalar_tensor_tensor", dict(
                out=A[:, k, :], in0=A[:, k, :], scalar=hn[:, k:k + 1], in1=O[k % 2][:], op0=ALU.mult, op1=ALU.mult),
                 rd=[P.R("O", k % 2), P.R("hn")] + [P.R("A", k, b) for b in range(3)],
                 wr=[P.R("A", k, b) for b in range(3)])
    ws = WStream(P, 32)
    pacc = [P.ps(f"pacc{i}") for i in range(6)]
    ps_ss = P.ps("ps_ss")
    xmv = xm_s.rearrange("(k p) t -> p k t", p=128)
    it = 0
    for m, ((wv,), wR) in enumerate(prefetched(ws, [[(wo_d[:, m * 128:(m + 1) * 128], 16, 128)] for m in range(16)])):
        for b, (t0, tw) in enumerate(blks):
            pa = pacc[it % 6]
            paR = P.R("pacc", it % 6)
            it += 1
            mm_group(P, pa[:, 0:tw], paR, [(wv[:, k, :], A[:, k, t0:t0 + tw]) for k in range(16)],
                     rd=[wR] + [P.R("A", k, b) for k in range(16)])
            P.i("dve", "tensor_tensor", dict(
                out=xT[:, m, t0:t0 + tw], in0=pa[:, 0:tw], in1=xT[:, m, t0:t0 + tw], op=ALU.add),
                 rd=[paR, P.R("x", m, b)], wr=[P.R("x", m, b)])
        P.dma(xmv[:, m, :], xT[:, m, :], rd=[P.R("x", m, b) for b in range(3)] + [P.R("alias")], wr=[P.R("xm", m)])
    rmsnorm_fm(P, c, xT, "x", g_sb, A, "A", T, ps_ss, D)
    allx = [P.R("x", k, b) for k in range(16) for b in range(3)]
    P.i("pool", "memset", dict(ap=c["eps"][:], constant=EPS), rd=allx + [P.R("eps")], wr=[P.R("alias"), P.R("eps")])
    raw = [P.sb([128, T], F32, f"raw{i}") for i in range(4)]
    tg = P.sb([128, 1024], F32, "tg")
    tv = P.sb([128, 1024], F32, "tv")
    specs = [[(wu_d[:, j * 128:(j + 1) * 128], 16, 128), (wu_d[:, DFF + j * 128:DFF + (j + 1) * 128], 16, 128)]
             for j in range(44)]
    for j, ((wg, wvv), wR) in enumerate(prefetched(ws, specs)):
        for gv, wv in enumerate((wg, wvv)):
            r_ap = raw[(j % 2) * 2 + gv]
            rR = P.R("raw", (j % 2) * 2 + gv)
            for b, (t0, tw) in enumerate(blks):
                pa = pacc[it % 6]
                paR = P.R("pacc", it % 6)
                it += 1
                mm_group(P, pa[:, 0:tw], paR, [(wv[:, k, :], A[:, k, t0:t0 + tw]) for k in range(16)],
                         rd=[wR] + [P.R("A", k, b) for k in range(16)])
                P.i("act", "activation", dict(
                    out=r_ap[:, t0:t0 + tw], in_=pa[:, 0:tw], func=AF.Copy), rd=[paR], wr=[rR])
            ch = gv * 44 + j
            t_ap = tg if gv == 0 else tv
            tR = P.R("tg") if gv == 0 else P.R("tv")
            eng = "dve"
            P.i(eng, "tensor_scalar", dict(
                out=t_ap[:], in0=r_ap[:, 1:1025], scalar1=cw[:, ch, 1:2], scalar2=cw[:, ch, 3:4],
                op0=ALU.mult, op1=ALU.add), rd=[rR, P.R("cw")], wr=[tR])
            P.i(eng, "scalar_tensor_tensor", dict(
                out=t_ap[:], in0=r_ap[:, 0:1024], scalar=cw[:, ch, 0:1], in1=t_ap[:], op0=ALU.mult, op1=ALU.add),
                 rd=[rR, P.R("cw"), tR], wr=[tR])
            P.i(eng, "scalar_tensor_tensor", dict(
                out=t_ap[:], in0=r_ap[:, 2:1026], scalar=cw[:, ch, 2:3], in1=t_ap[:], op0=ALU.mult, op1=ALU.add),
                 rd=[rR, P.R("cw"), tR], wr=[tR])
        P.i("act", "activation", dict(out=tg[:], in_=tg[:], func=AF.Silu), rd=[P.R("tg")], wr=[P.R("tg")])
        P.i("dve", "tensor_tensor", dict(out=actT[:, j, :], in0=tg[:], in1=tv[:], op=ALU.mult),
             rd=[P.R("tg"), P.R("tv"), P.R("alias")], wr=[P.R("act", j)])
    xo = [raw[i][:, 0:1024] for i in range(2)]
    specs = [[(wd_d[half * 2816:(half + 1) * 2816, m * 128:(m + 1) * 128], 22, 128)]
             for m in range(16) for half in range(2)]
    wit = prefetched(ws, specs)
    for m in range(16):
        P.dma(xo[m % 2], xmv[:, m, 1:1025], rd=[P.R("xm", m)], wr=[P.R("raw", m % 2)])
        pas = []
        for half in range(2):
            (wv,), wR = next(wit)
            for b in range(2):
                if half == 0:
                    pas.append((pacc[it % 6], P.R("pacc", it % 6)))
                    it += 1
                pa, paR = pas[b]
                mm_group(P, pa[:], paR, [(wv[:, k, :], actT[:, half * 22 + k, b * 512:(b + 1) * 512]) for k in range(22)],
                         rd=[wR] + [P.R("act", half * 22 + k) for k in range(22)], start=(half == 0), stop=(half == 1))
        for b in range(2):
            pa, paR = pas[b]
            P.i("dve", "tensor_tensor", dict(
                out=xo[m % 2][:, b * 512:(b + 1) * 512], in0=pa[:], in1=xo[m % 2][:, b * 512:(b + 1) * 512],
                op=ALU.add), rd=[paR, P.R("raw", m % 2)], wr=[P.R("raw", m % 2)])
        P.dma(out_o[m * 128:(m + 1) * 128, :], xo[m % 2], rd=[P.R("raw", m % 2)])
    return P.finish()


def build_inproj1():
    T = 1024
    P = Prog()
    xT_d = P.din("xT", [D, T], F32)
    g_d = P.din("g", [128, 16], F32)
    w_d = P.din("w", [D, 8224], F32)
    bg_d = P.din("bg", [32, 1], F32)
    o_o = P.dout("pT", [8192, T], BF16)
    gt_o = P.dout("gT", [32, T], F32)
    c = consts(P)
    xT = P.sb([128, 16, T], F32, "xT")
    hT = P.sb([128, 16, T], BF16, "hT")
    g_sb = P.sb([128, 16], F32, "g_sb")
    bg = P.sb([32, 1], F32, "bg")
    blks = blocks_of(T)
    P.dma(g_sb[:], g_d, wr=[P.R("gsb", "h")])
    P.dma(bg[:], bg_d, wr=[P.R("bg")])
    xv = xT_d.rearrange("(k p) t -> p k t", p=128)
    for k in range(16):
        for b, (t0, tw) in enumerate(blks):
            P.dma(xT[:, k, t0:t0 + tw], xv[:, k, t0:t0 + tw], wr=[P.R("x", k, b)])
    ps_ss = P.ps("ps_ss")
    rmsnorm_fm(P, c, xT, "x", g_sb, hT, "h", T, ps_ss, D)
    ws = WStream(P, 16)
    pacc = [P.ps(f"pacc{i}") for i in range(4)]
    ev = [P.sb([128, 512], BF16, f"ev{i}") for i in range(4)]
    gev = P.sb([32, 512], F32, "gev")
    it = 0
    specs = [[(w_d[:, m * 128:m * 128 + (128 if m < 64 else 32)], 16, (128 if m < 64 else 32))] for m in range(65)]
    for m, ((wv,), wR) in enumerate(prefetched(ws, specs)):
        mw = 128 if m < 64 else 32
        for b, (t0, tw) in enumerate(blks):
            pa = pacc[it % 4]
            paR = P.R("pacc", it % 4)
            e_ = ev[it % 4]
            eR = P.R("ev", it % 4)
            it += 1
            mm_group(P, pa[0:mw, 0:tw], paR, [(wv[:, k, :], hT[:, k, t0:t0 + tw]) for k in range(16)],
                     rd=[wR] + [P.R("h", k, b) for k in range(16)])
            if m == 64:
                P.i("act", "activation", dict(out=gev[:, 0:tw], in_=pa[0:32, 0:tw],
                                                                  func=AF.Identity, bias=bg[:], scale=1.0),
                     rd=[paR, P.R("bg")], wr=[P.R("gev")])
                P.dma(gt_o[:, t0:t0 + tw], gev[:, 0:tw], rd=[P.R("gev")])
            else:
                if m < 16:
                    kw = dict(out=e_[:, 0:tw], in_=pa[:, 0:tw], func=AF.Identity, scale=1.0 / 16.0)
                elif m < 48:
                    kw = dict(out=e_[:, 0:tw], in_=pa[:, 0:tw], func=AF.Copy)
                else:
                    kw = dict(out=e_[:, 0:tw], in_=pa[:, 0:tw], func=AF.Sigmoid)
                P.i("act", "activation", kw, rd=[paR], wr=[eR])
                P.dma(o_o[m * 128:(m + 1) * 128, t0:t0 + tw], e_[:, 0:tw], rd=[eR])
    return P.finish()


def build_mlstm():
    NP = 2
    NCH = 32
    P = Prog()
    qT_d = P.din("qT", [NP, 2, 128, S], BF16)
    kT_d = P.din("kT", [NP, 2, 128, S], BF16)
    k_d = P.din("k", [NP, 128, NCH, 256], BF16)
    v_d = P.din("v", [NP, 128, NCH, 256], BF16)
    gt_d = P.din("gt", [NP, 128, 4, NCH], F32)
    tri_d = P.din("tri", [128, 2, 128], F32)
    hn_o = P.dout("hn", [NP, 128, NCH, 256], BF16)
    c = consts(P)
    tri = P.sb([128, 2, 128], F32, "tri")
    onesf = P.sb([128, 128], F32, "onesf")
    one1 = P.sb([128, 1], F32, "one1")
    P.dma(tri[:], tri_d, wr=[P.R("tri")])
    P.i("pool", "memset", dict(ap=onesf[:], constant=1.0), wr=[P.R("onesf")])
    P.i("pool", "memset", dict(ap=one1[:], constant=1.0), wr=[P.R("one1")])
    qT = P.sb([128, 2, S], BF16, "qT")
    kT = P.sb([128, 2, S], BF16, "kT")
    kk = P.sb([128, NCH, 256], BF16, "kk")
    vx = P.sb([128, NCH, 257], BF16, "vx")
    gt = P.sb([128, 4, NCH], F32, "gt")
    lf = P.sb([128, 2, NCH], F32, "lf")
    bc = P.sb([128, 2, NCH], F32, "bc")
    tot = P.sb([128, 2, NCH], F32, "tot")
    av = P.sb([128, 2, NCH], F32, "av")
    bv = P.sb([128, 2, NCH], F32, "bv")
    b2 = P.sb([128, 2, NCH], F32, "b2")
    dc = P.sb([128, 2, NCH], F32, "dc")
    tmp = P.sb([128, 2, NCH], F32, "tmp")
    hacc = P.sb([128, NCH, 256], F32, "hacc")
    ssq = P.sb([128, NCH], F32, "ssq")
    junk = P.sb([128, 256], F32, "junk")
    psS = [P.ps("psS0"), P.ps("psS1")]
    ps_g = psS[0]
    psN = [P.ps("psN0"), P.ps("psN1")]
    psC = [[P.ps("psC00"), P.ps("psC01")], [P.ps("psC10"), P.ps("psC11")]]
    Cst = [P.sb([128, 2, 257], F32, f"Cst{d}") for d in range(2)]
    Cbf = [P.sb([128, 2, 257], BF16, f"Cbf{d}") for d in range(2)]
    Sm = [P.sb([128, 128], BF16, f"Sm{d}") for d in range(2)]
    k2 = [P.sb([128, 256], BF16, f"k2{d}") for d in range(2)]
    dn = [P.sb([128, 4], F32, f"dn{d}") for d in range(2)]
    hev = [P.sb([128, 256], BF16, f"hev{i}") for i in range(2)]
    for p in range(NP):
        for h in range(2):
            P.dma(qT[:, h, :], qT_d[p, h], wr=[P.R("qT")])
            P.dma(kT[:, h, :], kT_d[p, h], wr=[P.R("kT")])
        P.dma(kk[:], k_d[p], wr=[P.R("kk")])
        P.dma(vx[:, :, 0:256], v_d[p], wr=[P.R("vx")])
        P.i("pool", "memset", dict(ap=vx[:, :, 256:257], constant=1.0), wr=[P.R("vx")], rd=[])
        P.dma(gt[:], gt_d[p], wr=[P.R("gt")])
        for d in range(2):
            P.i("act", "activation", dict(out=lf[:, d, :], in_=gt[:, 2 * d + 1, :], func=AF.Exp,
                                                            scale=-1.0), rd=[P.R("gt")], wr=[P.R("lf")])
        P.i("act", "activation", dict(out=lf[:], in_=lf[:], func=AF.Ln, bias=one1[:], scale=1.0),
             rd=[P.R("lf"), P.R("one1")], wr=[P.R("lf")])
        P.i("dve", "tensor_scalar", dict(out=lf[:], in0=lf[:], scalar1=-1.0, scalar2=None, op0=ALU.mult),
             rd=[P.R("lf")], wr=[P.R("lf")])
        for d in range(2):
            mm_group(P, ps_g[:, d * NCH:(d + 1) * NCH], P.R("psS", 0), [(tri[:, d, :], lf[:, d, :])],
                     rd=[P.R("tri"), P.R("lf")])
        mm_group(P, ps_g[:, 2 * NCH:4 * NCH], P.R("psS", 0), [(onesf[:], lf[:].rearrange("p d c -> p (d c)"))],
                 rd=[P.R("onesf"), P.R("lf")])
        P.i("dve", "tensor_copy", dict(out=bc[:].rearrange("p d c -> p (d c)"), in_=ps_g[:, 0:2 * NCH]),
             rd=[P.R("psS", 0)], wr=[P.R("bc")])
        P.i("dve", "tensor_copy", dict(out=tot[:].rearrange("p d c -> p (d c)"), in_=ps_g[:, 2 * NCH:4 * NCH]),
             rd=[P.R("psS", 0)], wr=[P.R("tot")])
        P.i("act", "activation", dict(out=av[:], in_=bc[:], func=AF.Exp), rd=[P.R("bc")], wr=[P.R("av")])
        P.i("act", "activation", dict(out=dc[:], in_=tot[:], func=AF.Exp), rd=[P.R("tot")], wr=[P.R("dc")])
        for d in range(2):
            P.i("dve", "tensor_tensor", dict(out=tmp[:, d, :], in0=gt[:, 2 * d, :], in1=bc[:, d, :],
                                                               op=ALU.subtract),
                 rd=[P.R("gt"), P.R("bc")], wr=[P.R("tmp")])
        P.i("act", "activation", dict(out=bv[:], in_=tmp[:], func=AF.Exp), rd=[P.R("tmp")], wr=[P.R("bv")])
        P.i("dve", "tensor_tensor", dict(out=tmp[:], in0=tmp[:], in1=tot[:], op=ALU.add),
             rd=[P.R("tmp"), P.R("tot"), P.R("bv")], wr=[P.R("tmp")])
        P.i("act", "activation", dict(out=b2[:], in_=tmp[:], func=AF.Exp), rd=[P.R("tmp")], wr=[P.R("b2")])
        gR = [P.R("av"), P.R("bv"), P.R("b2"), P.R("dc")]
        for d in range(2):
            P.i("pool", "memset", dict(ap=Cst[d][:], constant=0.0), wr=[P.R("Cst", d)])
            P.i("pool", "memset", dict(ap=Cbf[d][:], constant=0.0), wr=[P.R("Cbf", d)])
        for step in range(NCH):
            for d in range(2):
                ch = step if d == 0 else NCH - 1 - step
                cs_ = slice(ch * 128, (ch + 1) * 128)
                S_ap, SR = psS[d], P.R("psS", d)
                N_ap, NR = psN[d], P.R("psN", d)
                mm_group(P, S_ap[:, 0:128], SR, [(kT[:, h, cs_], qT[:, h, cs_]) for h in range(2)],
                         rd=[P.R("kT"), P.R("qT")])
                P.i("dve", "scalar_tensor_tensor", dict(
                    out=Sm[d][:], in0=S_ap[:, 0:128], scalar=bv[:, d, ch:ch + 1], in1=tri[:, d, :],
                    op0=ALU.mult, op1=ALU.mult), rd=[SR, P.R("tri")] + gR, wr=[P.R("Sm", d)])
                mm_group(P, N_ap[:, 0:257], NR,
                         [(Sm[d][:], vx[:, ch, :])] + [(qT[:, h, cs_], Cbf[d][:, h, :]) for h in range(2)],
                         rd=[P.R("Sm", d), P.R("vx"), P.R("qT"), P.R("Cbf", d)])
                P.i("act", "activation", dict(out=dn[d][:, 0:1], in_=N_ap[:, 256:257], func=AF.Abs,
                                              scale=av[:, d, ch:ch + 1]), rd=[NR] + gR, wr=[P.R("dn", d)])
                P.i("dve", "tensor_scalar", dict(
                    out=dn[d][:, 1:2], in0=dn[d][:, 0:1], scalar1=1.0, scalar2=None, op0=ALU.max),
                     rd=[P.R("dn", d)], wr=[P.R("dn", d)])
                P.i("dve", "reciprocal", dict(out=dn[d][:, 2:3], in_=dn[d][:, 1:2]),
                     rd=[P.R("dn", d)], wr=[P.R("dn", d)])
                P.i("dve", "tensor_tensor", dict(
                    out=dn[d][:, 3:4], in0=dn[d][:, 2:3], in1=av[:, d, ch:ch + 1], op=ALU.mult),
                     rd=[P.R("dn", d)] + gR, wr=[P.R("dn", d)])
                if step < NCH // 2:
                    P.i("act", "activation", dict(
                        out=hacc[:, ch, :], in_=N_ap[:, 0:256], func=AF.Identity, scale=dn[d][:, 3:4]),
                         rd=[NR, P.R("dn", d)], wr=[P.R("hacc", ch)])
                else:
                    P.i("dve", "scalar_tensor_tensor", dict(
                        out=hacc[:, ch, :], in0=N_ap[:, 0:256], scalar=dn[d][:, 3:4], in1=hacc[:, ch, :],
                        op0=ALU.mult, op1=ALU.add), rd=[NR, P.R("dn", d), P.R("hacc", ch)], wr=[P.R("hacc", ch)])
                P.i("pool", "tensor_scalar", dict(
                    out=k2[d][:], in0=kk[:, ch, :], scalar1=b2[:, d, ch:ch + 1], scalar2=None, op0=ALU.mult),
                     rd=[P.R("kk")] + gR, wr=[P.R("k2", d)])
                for h in range(2):
                    mm_group(P, psC[d][h][:, 0:257], P.R("psC", d, h), [(k2[d][:, h * 128:(h + 1) * 128], vx[:, ch, :])],
                             rd=[P.R("k2", d), P.R("vx")])
                    P.i("dve", "scalar_tensor_tensor", dict(
                        out=Cst[d][:, h, :], in0=Cst[d][:, h, :], scalar=dc[:, d, ch:ch + 1], in1=psC[d][h][:, 0:257],
                        op0=ALU.mult, op1=ALU.add), rd=[P.R("psC", d, h), P.R("Cst", d)] + gR, wr=[P.R("Cst", d)])
                P.i("act", "activation", dict(out=Cbf[d][:], in_=Cst[d][:], func=AF.Copy),
                     rd=[P.R("Cst", d)], wr=[P.R("Cbf", d)])
        for ch in range(NCH):
            P.i("act", "activation", dict(out=junk[:], in_=hacc[:, ch, :], func=AF.Square,
                                                              accum_out=ssq[:, ch:ch + 1]),
                 rd=[P.R("hacc", ch)], wr=[P.R("junk"), P.R("ssq")])
        P.i("act", "activation", dict(out=ssq[:], in_=ssq[:], func=AF.Ln, bias=c["eps"][:], scale=1.0 / 256),
             rd=[P.R("ssq"), P.R("eps")], wr=[P.R("ssq")])
        P.i("act", "activation", dict(out=ssq[:], in_=ssq[:], func=AF.Exp, scale=-0.5),
             rd=[P.R("ssq")], wr=[P.R("ssq")])
        for ch in range(NCH):
            P.i("dve", "tensor_scalar", dict(
                out=hev[ch % 2][:], in0=hacc[:, ch, :], scalar1=ssq[:, ch:ch + 1], scalar2=None, op0=ALU.mult),
                 rd=[P.R("hacc", ch), P.R("ssq")], wr=[P.R("hev", ch % 2)])
            P.dma(hn_o[p, :, ch, :], hev[ch % 2][:], rd=[P.R("hev", ch % 2)])
    return P.finish()


N_LAUNCH = [0]


def run(nc, in_maps):
    N_LAUNCH[0] += 1
    res = run_bass_kernel_spmd(nc, in_maps, core_ids=list(range(NCORES)))
    return res.results


def pc128(v, kc):
    return np.ascontiguousarray(np.asarray(v, np.float32).reshape(kc, 128).T)


def rope_tables():
    rows = S // 64
    row = np.repeat(np.arange(rows), 64).astype(np.float32)
    col = np.tile(np.arange(64), rows).astype(np.float32)
    inv = (1.0 / (np.float32(10000.0) ** (np.arange(0, 64, 2, dtype=np.float32) / np.float32(64)))).astype(np.float32)
    a_r = row[:, None] * inv[None, :]
    a_c = col[:, None] * inv[None, :]
    ang = np.concatenate([a_r, a_r, a_c, a_c], axis=-1)
    return np.cos(ang).astype(np.float32), np.sin(ang).astype(np.float32)


def rot_matrix_T():
    R = np.zeros((128, 128), np.float32)
    for base in (0, 64):
        for j in range(32):
            R[base + j, base + j + 32] = -1.0
            R[base + j + 32, base + j] = 1.0
    return np.ascontiguousarray(R.T).astype(NPBF)


def halo_T(a_tok, b, q):
    F_ = a_tok.shape[-1]
    out = np.zeros((F_, 1026), a_tok.dtype)
    lo, hi = q * 1024 - 1, q * 1024 + 1025
    l2, h2 = max(lo, 0), min(hi, S)
    out[:, l2 - lo:h2 - lo] = a_tok[b, l2:h2].T
    return out


def ffn_inputs(layer, norm_ffn, w_up, conv_w, conv_b, w_down):
    cw = np.zeros((128, 88, 4), np.float32)
    cwl = np.asarray(conv_w[layer], np.float32)
    cbl = np.asarray(conv_b[layer], np.float32)
    for i in range(3):
        cw[:, :, i] = cwl[i].reshape(88, 128).T
    cw[:, :, 3] = cbl.reshape(88, 128).T
    return {"g": pc128(norm_ffn[layer], 16), "wu": np.ascontiguousarray(w_up[layer], np.float32), "cw": cw,
            "wd": np.ascontiguousarray(w_down[layer], np.float32)}


def kernel(x, norm_mix, norm_ffn, w_in_ab, pool_w, pool_scale, q_norm, k_norm, w_out_ab,
           w_in_c, b_gate_c, h_norm_c, w_out_c, w_up, conv_w, conv_b, w_down):
    x = np.asarray(x, np.float32)
    B = x.shape[0]
    cores = [(c // 4, c % 4) for c in range(NCORES)]
    cos, sin = rope_tables()
    nc = build_inproj0()
    ims = []
    for (b, q) in cores:
        sl = slice(q * 1024, (q + 1) * 1024)
        cs = np.stack([cos[sl].T, sin[sl].T], axis=1)
        ims.append({"xT": np.ascontiguousarray(x[b, sl].T), "g": pc128(norm_mix[0], 16),
                    "w": np.ascontiguousarray(w_in_ab[0], np.float32),
                    "qkn": np.ascontiguousarray(np.stack([q_norm[0], k_norm[0]], axis=1), np.float32),
                    "cs": np.ascontiguousarray(cs, np.float32), "rt": rot_matrix_T()})
    r1 = run(nc, ims)
    qT = np.zeros((B, 1536, S), NPBF)
    kT = np.zeros((B, 512, S), NPBF)
    vT = np.zeros((B, 512, S), NPBF)
    uT = np.zeros((B, 512, S), NPBF)
    for ci, (b, q) in enumerate(cores):
        sl = slice(q * 1024, (q + 1) * 1024)
        qT[b][:, sl] = r1[ci]["qT"]
        kT[b][:, sl] = r1[ci]["kT"]
        vT[b][:, sl] = r1[ci]["vT"]
        uT[b][:, sl] = r1[ci]["uT"]
    nc = build_attn()
    ims = []
    t = np.arange(S)
    for (b, g) in cores:
        w = (2, 4, 8, 16)[g]
        lo = np.clip(t - w // 2, 0, S)
        hi = np.clip(t + w // 2, 0, S)
        ic = np.broadcast_to((1.0 / (hi - lo).astype(np.float32))[None, :], (128, S))
        pc = np.zeros((128, 6), np.float32)
        pc[:, g] = 1.0
        pc[:, 4] = np.asarray(pool_scale[0], np.float32)[g * 128:(g + 1) * 128]
        vg = vT[b][g * 128:(g + 1) * 128]
        v_tm = np.ascontiguousarray(vg.T.reshape(32, 128, 128).transpose(1, 0, 2))
        ims.append({"qT": np.ascontiguousarray(qT[b][g * 384:(g + 1) * 384].reshape(3, 128, S)),
                    "kT": np.ascontiguousarray(kT[b][g * 128:(g + 1) * 128]), "v": v_tm,
                    "uT": np.ascontiguousarray(uT[b][g * 128:(g + 1) * 128]),
                    "pw": np.ascontiguousarray(pool_w[0][g], np.float32), "pc": pc,
                    "ic": np.ascontiguousarray(ic, np.float32)})
    r2 = run(nc, ims)
    cat = np.zeros((B, S, D), NPBF)
    for ci, (b, g) in enumerate(cores):
        cat[b][:, g * 128:(g + 1) * 128] = r2[ci]["poolT"].T
        cat[b][:, 512 + g * 384:512 + (g + 1) * 384] = r2[ci]["attT"].reshape(384, S).T
    nc = build_outffn("ab")
    f0 = ffn_inputs(0, norm_ffn, w_up, conv_w, conv_b, w_down)
    ims = []
    for (b, q) in cores:
        d = {"aT": halo_T(cat, b, q), "xT": halo_T(x, b, q), "wo": np.ascontiguousarray(w_out_ab[0], np.float32)}
        d.update(f0)
        ims.append(d)
    r3 = run(nc, ims)
    x1 = np.zeros((B, S, D), np.float32)
    for ci, (b, q) in enumerate(cores):
        x1[b, q * 1024:(q + 1) * 1024] = r3[ci]["oxT"].T
    nc = build_inproj1()
    ims = []
    for (b, q) in cores:
        sl = slice(q * 1024, (q + 1) * 1024)
        ims.append({"xT": np.ascontiguousarray(x1[b, sl].T), "g": pc128(norm_mix[1], 16),
                    "w": np.ascontiguousarray(w_in_c[0], np.float32),
                    "bg": np.ascontiguousarray(np.asarray(b_gate_c[0], np.float32).reshape(32, 1))})
    r4 = run(nc, ims)
    pT = np.zeros((B, 8192, S), NPBF)
    gT = np.zeros((B, 32, S), np.float32)
    for ci, (b, q) in enumerate(cores):
        sl = slice(q * 1024, (q + 1) * 1024)
        pT[b][:, sl] = r4[ci]["pT"]
        gT[b][:, sl] = r4[ci]["gT"]
    nc = build_mlstm()
    tri = np.zeros((128, 2, 128), np.float32)
    ii = np.arange(128)
    tri[:, 0, :] = (ii[:, None] <= ii[None, :])
    tri[:, 1, :] = (ii[:, None] >= ii[None, :])
    ims = []
    pairs = [(p // 8, p % 8) for p in range(16)]
    for ci in range(NCORES):
        d = {k: [] for k in ("qT", "kT", "k", "v", "gt")}
        for (b, h) in pairs[2 * ci:2 * ci + 2]:
            qh = pT[b][h * 256:(h + 1) * 256]
            kh = pT[b][2048 + h * 256:2048 + (h + 1) * 256]
            vh = pT[b][4096 + h * 256:4096 + (h + 1) * 256]
            d["qT"].append(qh.reshape(2, 128, S))
            d["kT"].append(kh.reshape(2, 128, S))
            d["k"].append(kh.T.reshape(32, 128, 256).transpose(1, 0, 2))
            d["v"].append(vh.T.reshape(32, 128, 256).transpose(1, 0, 2))
            gg = gT[b].reshape(4, 8, S)[:, h]
            d["gt"].append(gg.reshape(4, 32, 128).transpose(2, 0, 1))
        im = {k: np.ascontiguousarray(np.stack(v_)) for k, v_ in d.items()}
        im["tri"] = tri
        ims.append(im)
    r5 = run(nc, ims)
    hn = np.zeros((B, S, D), NPBF)
    for ci in range(NCORES):
        for j, (b, h) in enumerate(pairs[2 * ci:2 * ci + 2]):
            hh = r5[ci]["hn"][j]
            hn[b][:, h * 256:(h + 1) * 256] = hh.transpose(1, 0, 2).reshape(S, 256)
    og = np.ascontiguousarray(pT[:, 6144:8192].transpose(0, 2, 1))
    nc = build_outffn("c")
    f1 = ffn_inputs(1, norm_ffn, w_up, conv_w, conv_b, w_down)
    ims = []
    for (b, q) in cores:
        d = {"aT": halo_T(hn, b, q), "oT": halo_T(og, b, q), "xT": halo_T(x1, b, q),
             "hn": pc128(h_norm_c[0], 16), "wo": np.ascontiguousarray(w_out_c[0], np.float32)}
        d.update(f1)
        ims.append(d)
    r6 = run(nc, ims)
    out = np.zeros((B, S, D), np.float32)
    for ci, (b, q) in enumerate(cores):
        out[b, q * 1024:(q + 1) * 1024] = r6[ci]["oxT"].T
    return out
```

```python
import numpy as np
import ml_dtypes
from contextlib import ExitStack
import concourse.bass as bass
import concourse.mybir as mybir
from concourse.bass_utils import run_bass_kernel_spmd

F32, BF16 = mybir.dt.float32, mybir.dt.bfloat16
AF = mybir.ActivationFunctionType
ALU = mybir.AluOpType
NPBF = ml_dtypes.bfloat16
NDMA = 12
D = 2048
S = 4096
DFF = 5632
EPS = 1e-6
NCORES = 8


PSUM_KEYS = ("pacc", "ps_ss", "ps_h", "ps_r", "ps_p", "ps_s", "ps_o", "ps_m", "psS", "psN", "psC")


class Res:
    __slots__ = ("w", "rd", "excl")

    def __init__(self, excl=False):
        self.w = None
        self.rd = []
        self.excl = excl


class Prog:
    def __init__(self):
        self.nc = bass.Bass("TRN2", target_bir_lowering=False)
        self.ops = []
        self.st = ExitStack()
        self.res = {}
        self.nm = 0

    def R(self, *key):
        r = self.res.get(key)
        if r is None:
            r = self.res[key] = Res(key[0] in PSUM_KEYS)
        return r

    def sb(self, shape, dt, name=None):
        self.nm += 1
        return self.st.enter_context(self.nc.sbuf_tensor("S_" + (name or f"sb{self.nm}"), list(shape), dt))

    def ps(self, name=None):
        self.nm += 1
        return self.st.enter_context(self.nc.psum_tensor("P_" + (name or f"ps{self.nm}"), [128, 512], F32))

    def din(self, name, shape, dt):
        return self.nc.dram_tensor(name, list(shape), dt, kind="ExternalInput").ap()

    def dout(self, name, shape, dt):
        return self.nc.dram_tensor(name, list(shape), dt, kind="ExternalOutput").ap()

    def op(self, eng, fn, rd=(), wr=()):
        i = len(self.ops)
        deps = set()
        wr = list(wr) + [r for r in rd if r.excl]
        for r in rd:
            if r.w is not None:
                deps.add(r.w)
        for r in wr:
            if r.w is not None:
                deps.add(r.w)
            deps.update(r.rd)
        for r in rd:
            r.rd.append(i)
        for r in wr:
            r.w = i
            r.rd = []
        deps.discard(i)
        self.ops.append((eng, fn, deps))
        return i

    def i(self, eng, meth, kw, rd=(), wr=()):
        return self.op(eng, lambda e: getattr(e, meth)(**kw), rd, wr)

    def dma(self, out, in_, rd=(), wr=(), q="sp"):
        return self.op(q, lambda e: e.dma_start(out=out, in_=in_), rd, wr)

    def finish(self):
        nc, ops, st = self.nc, self.ops, self.st
        n = len(ops)
        signal = [False] * n
        for (_, _, deps) in ops:
            for d in deps:
                signal[d] = True
        engs = ["pe", "act", "dve", "pool"]
        DMAQ = {"sp": "sp", "actq": "act"}
        tok = [None] * n
        cnt = {e: 0 for e in engs}
        ndma = 0
        idx = {e: [] for e in engs + ["sp"]}
        for i, (eng, _, _) in enumerate(ops):
            idx[DMAQ.get(eng, eng)].append(i)
            if eng in DMAQ:
                tok[i] = (("d", ndma % NDMA), 16 * (ndma // NDMA + 1))
                ndma += 1
            elif signal[i]:
                cnt[eng] += 1
                tok[i] = (eng, cnt[eng])
        sems = {e: st.enter_context(nc.semaphore("s_" + e)) for e in engs}
        for k in range(NDMA):
            sems[("d", k)] = st.enter_context(nc.semaphore(f"s_d{k}"))
        block = st.enter_context(nc.Block())

        def emit(engname, e):
            known = {}
            for i in idx[engname]:
                oeng, fn, deps = ops[i]
                isdma = oeng in DMAQ
                need = {}
                for d in deps:
                    if engname == "pe" and ops[d][0] == "pe":
                        continue
                    s, v = tok[d]
                    if need.get(s, 0) < v:
                        need[s] = v
                if isdma:
                    s, v = tok[i]
                    if v > 16:
                        need[s] = max(need.get(s, 0), v - 16)
                for s, v in need.items():
                    if known.get(s, 0) < v:
                        e.wait_ge(sems[s], v)
                        known[s] = v
                ins = fn(e)
                if tok[i] is not None:
                    ins.then_inc(sems[tok[i][0]], 16 if isdma else 1)
            if engname == "sp":
                for k in range(min(NDMA, ndma)):
                    tot = 16 * ((ndma - 1 - k) // NDMA + 1)
                    if known.get(("d", k), 0) < tot:
                        e.wait_ge(sems[("d", k)], tot)

        block.sync(lambda e: emit("sp", e))
        block.tensor(lambda e: emit("pe", e))
        block.scalar(lambda e: emit("act", e))
        block.vector(lambda e: emit("dve", e))
        block.gpsimd(lambda e: emit("pool", e))
        st.close()
        return nc


def mm_group(P, ps_ap, psR, pairs, rd, start=True, stop=True):
    pairs = list(pairs)

    def fn(e):
        n = len(pairs)
        ins = None
        for k, (l, r) in enumerate(pairs):
            ins = e.matmul(ps_ap, lhsT=l, rhs=r, start=(start and k == 0), stop=(stop and k == n - 1))
        return ins

    return P.op("pe", fn, rd=rd, wr=[psR])


class WStream:
    def __init__(self, P, slots):
        self.P = P
        self.slots = slots
        self.stage = [P.sb([128, slots * 128], F32, f"wstage{i}") for i in range(2)]
        self.wb = [P.sb([128, slots * 128], BF16, f"wbf{i}") for i in range(2)]
        self.k = 0

    def load(self, pieces, scale_aps=None):
        P = self.P
        i = self.k % 2
        self.k += 1
        sR = P.R("wstage", i)
        stage = self.stage[i]
        bR = P.R("wbf", i)
        off = 0
        views = []
        for (ap, kc, m) in pieces:
            dst = stage[:, off:off + kc * m].rearrange("p (k m) -> p k m", m=m)
            src = ap.rearrange("(k p) m -> p k m", p=128)
            P.dma(dst, src, wr=[sR])
            views.append(self.wb[i][:, off:off + kc * m].rearrange("p (k m) -> p k m", m=m))
            off += kc * m
        eng = "pool" if (self.k % 2) else "act"
        src_all = stage[:, 0:off]
        dst_all = self.wb[i][:, 0:off]
        if eng == "pool":
            P.i("pool", "tensor_copy", dict(out=dst_all, in_=src_all), rd=[sR], wr=[bR])
        else:
            P.i("act", "activation", dict(out=dst_all, in_=src_all, func=AF.Copy), rd=[sR], wr=[bR])
        return views, bR


def prefetched(ws, specs):
    nxt = ws.load(specs[0])
    for i in range(len(specs)):
        cur = nxt
        if i + 1 < len(specs):
            nxt = ws.load(specs[i + 1])
        yield cur


def consts(P):
    c = {}
    c["ones"] = P.sb([128, 128], BF16, "ones")
    c["eps"] = P.sb([128, 1], F32, "epsc")
    P.i("pool", "memset", dict(ap=c["ones"][:], constant=1.0), wr=[P.R("ones")])
    P.i("pool", "memset", dict(ap=c["eps"][:], constant=EPS), wr=[P.R("eps")])
    return c


def blocks_of(T):
    if T % 512 == 0:
        return [(i * 512, 512) for i in range(T // 512)]
    assert T % 3 == 0
    w = T // 3
    return [(i * w, w) for i in range(3)]


def rmsnorm_fm(P, c, xT, xkey, g_sb, hT, hkey, T, ps_ss, dim):
    KC = dim // 128
    blks = blocks_of(T)
    rstd = P.sb([128, T], F32, "rstd_" + hkey)
    lnv = P.sb([128, 512], F32, "lnv_" + hkey)
    for b, (t0, tw) in enumerate(blks):
        for k in range(KC):
            P.i("act", "activation", dict(out=hT[:, k, t0:t0 + tw], in_=xT[:, k, t0:t0 + tw],
                                                            func=AF.Square),
                 rd=[P.R(xkey, k, b)], wr=[P.R(hkey, k, b)])
        mm_group(P, ps_ss[:, 0:tw], P.R("ps_ss"),
                 [(c["ones"][:], hT[:, k, t0:t0 + tw]) for k in range(KC)],
                 rd=[P.R("ones")] + [P.R(hkey, k, b) for k in range(KC)])
        P.i("act", "activation", dict(out=lnv[:, 0:tw], in_=ps_ss[:, 0:tw], func=AF.Ln, bias=c["eps"][:],
                                           scale=1.0 / dim),
             rd=[P.R("ps_ss"), P.R("eps")], wr=[P.R("lnv", hkey)])
        P.i("act", "activation", dict(out=rstd[:, t0:t0 + tw], in_=lnv[:, 0:tw], func=AF.Exp, scale=-0.5),
             rd=[P.R("lnv", hkey)], wr=[P.R("rstd", hkey, b)])
        for k in range(KC):
            P.i("dve", "scalar_tensor_tensor", dict(
                out=hT[:, k, t0:t0 + tw], in0=xT[:, k, t0:t0 + tw], scalar=g_sb[:, k:k + 1],
                in1=rstd[:, t0:t0 + tw], op0=ALU.mult, op1=ALU.mult),
                 rd=[P.R(xkey, k, b), P.R("rstd", hkey, b), P.R("gsb", hkey)], wr=[P.R(hkey, k, b)])


def build_inproj0():
    T = 1024
    P = Prog()
    xT_d = P.din("xT", [D, T], F32)
    g_d = P.din("g", [128, 16], F32)
    w_d = P.din("w", [D, 3072], F32)
    qk_d = P.din("qkn", [128, 2], F32)
    cs_d = P.din("cs", [128, 2, T], F32)
    rt_d = P.din("rt", [128, 128], BF16)
    qT_o = P.dout("qT", [1536, T], BF16)
    kT_o = P.dout("kT", [512, T], BF16)
    vT_o = P.dout("vT", [512, T], BF16)
    uT_o = P.dout("uT", [512, T], BF16)
    c = consts(P)
    xT = P.sb([128, 16, T], F32, "xT")
    hT = P.sb([128, 16, T], BF16, "hT")
    g_sb = P.sb([128, 16], F32, "g_sb")
    qk_sb = P.sb([128, 2], F32, "qk_sb")
    cs = P.sb([128, 2, T], F32, "cs")
    rt = P.sb([128, 128], BF16, "rt")
    blks = blocks_of(T)
    P.dma(g_sb[:], g_d, wr=[P.R("gsb", "h")])
    P.dma(qk_sb[:], qk_d, wr=[P.R("qk")])
    P.dma(cs[:], cs_d, wr=[P.R("cs")])
    P.dma(rt[:], rt_d, wr=[P.R("rt")])
    xv = xT_d.rearrange("(k p) t -> p k t", p=128)
    for k in range(16):
        for b, (t0, tw) in enumerate(blks):
            P.dma(xT[:, k, t0:t0 + tw], xv[:, k, t0:t0 + tw], wr=[P.R("x", k, b)])
    ps_ss = P.ps("ps_ss")
    rmsnorm_fm(P, c, xT, "x", g_sb, hT, "h", T, ps_ss, D)
    ws = WStream(P, 16)
    pacc = [P.ps("pacc0"), P.ps("pacc1")]
    ps_h = P.ps("ps_h")
    ps_r = P.ps("ps_r")
    ev = [P.sb([128, 512], BF16, f"ev{i}") for i in range(2)]
    sqh = P.sb([128, 512], BF16, "sqh")
    qg = P.sb([128, 512], BF16, "qg")
    lnh = P.sb([128, 512], F32, "lnh")
    rsh = P.sb([128, 512], F32, "rsh")
    t1 = P.sb([128, 512], F32, "t1")
    t2 = P.sb([128, 512], F32, "t2")
    it = 0
    for m, ((wv,), wR) in enumerate(prefetched(ws, [[(w_d[:, m * 128:(m + 1) * 128], 16, 128)] for m in range(24)])):
        for b, (t0, tw) in enumerate(blks):
            pa = pacc[it % 2]
            paR = P.R("pacc", it % 2)
            e_ = ev[it % 2]
            eR = P.R("ev", it % 2)
            it += 1
            mm_group(P, pa[:, 0:tw], paR, [(wv[:, k, :], hT[:, k, t0:t0 + tw]) for k in range(16)],
                     rd=[wR] + [P.R("h", k, b) for k in range(16)])
            if m < 4 or m >= 20:
                dst = (uT_o[m * 128:(m + 1) * 128, t0:t0 + tw] if m < 4
                       else vT_o[(m - 20) * 128:(m - 19) * 128, t0:t0 + tw])
                P.i("act", "activation", dict(out=e_[:, 0:tw], in_=pa[:, 0:tw],
                                                                         func=AF.Copy), rd=[paR], wr=[eR])
                P.dma(dst, e_[:, 0:tw], rd=[eR], q="actq")
            else:
                isq = m < 16
                gi = 0 if isq else 1
                dst = (qT_o[(m - 4) * 128:(m - 3) * 128, t0:t0 + tw] if isq
                       else kT_o[(m - 16) * 128:(m - 15) * 128, t0:t0 + tw])
                P.i("act", "activation", dict(out=sqh[:, 0:tw], in_=pa[:, 0:tw],
                                                                  func=AF.Square), rd=[paR], wr=[P.R("sqh")])
                P.i("dve", "tensor_scalar", dict(
                    out=qg[:, 0:tw], in0=pa[:, 0:tw], scalar1=qk_sb[:, gi:gi + 1], scalar2=None, op0=ALU.mult),
                     rd=[paR, P.R("qk")], wr=[P.R("qg")])
                mm_group(P, ps_h[:, 0:tw], P.R("ps_h"), [(c["ones"][:], sqh[:, 0:tw])], rd=[P.R("ones"), P.R("sqh")])
                mm_group(P, ps_r[:, 0:tw], P.R("ps_r"), [(rt[:], qg[:, 0:tw])], rd=[P.R("rt"), P.R("qg")])
                P.i("act", "activation", dict(out=lnh[:, 0:tw], in_=ps_h[:, 0:tw], func=AF.Ln, bias=c["eps"][:],
                                                   scale=1.0 / 128), rd=[P.R("ps_h"), P.R("eps")], wr=[P.R("lnh")])
                P.i("act", "activation", dict(out=rsh[:, 0:tw], in_=lnh[:, 0:tw], func=AF.Exp, scale=-0.5),
                     rd=[P.R("lnh")], wr=[P.R("rsh")])
                P.i("dve", "tensor_tensor", dict(out=t1[:, 0:tw], in0=qg[:, 0:tw],
                                                                     in1=cs[:, 0, t0:t0 + tw], op=ALU.mult),
                     rd=[P.R("qg"), P.R("cs")], wr=[P.R("t1")])
                P.i("dve", "tensor_tensor", dict(out=t2[:, 0:tw], in0=ps_r[:, 0:tw],
                                                                     in1=cs[:, 1, t0:t0 + tw], op=ALU.mult),
                     rd=[P.R("ps_r"), P.R("cs")], wr=[P.R("t2")])
                P.i("pool", "tensor_tensor", dict(out=t1[:, 0:tw], in0=t1[:, 0:tw], in1=t2[:, 0:tw], op=ALU.add),
                     rd=[P.R("t1"), P.R("t2")], wr=[P.R("t1")])
                P.i("pool", "tensor_tensor", dict(out=e_[:, 0:tw], in0=t1[:, 0:tw],
                                                                      in1=rsh[:, 0:tw], op=ALU.mult),
                     rd=[P.R("t1"), P.R("rsh")], wr=[eR])
                P.dma(dst, e_[:, 0:tw], rd=[eR], q="actq")
    return P.finish()


def build_attn():
    P = Prog()
    qT_d = P.din("qT", [3, 128, S], BF16)
    kT_d = P.din("kT", [128, S], BF16)
    v_d = P.din("v", [128, 32, 128], BF16)
    uT_d = P.din("uT", [128, S], BF16)
    pw_d = P.din("pw", [128, 128], F32)
    pc_d = P.din("pc", [128, 6], F32)
    ic_d = P.din("ic", [128, S], F32)
    att_o = P.dout("attT", [3, 128, S], BF16)
    pool_o = P.dout("poolT", [128, S], BF16)
    c = consts(P)
    qT = P.sb([128, 3, S], BF16, "qT")
    kT = P.sb([128, S], BF16, "kT")
    v = P.sb([128, 32, 128], BF16, "v")
    pw = P.sb([128, 128], F32, "pw")
    pwb = P.sb([128, 128], BF16, "pwb")
    pc = P.sb([128, 6], F32, "pc")
    for h in range(3):
        P.dma(qT[:, h, :], qT_d[h], wr=[P.R("q", h)])
    P.dma(kT[:], kT_d, wr=[P.R("k")])
    P.dma(v[:], v_d, wr=[P.R("v")])
    P.dma(pw[:], pw_d, wr=[P.R("pw")])
    P.dma(pc[:], pc_d, wr=[P.R("pc")])
    P.i("act", "activation", dict(out=pwb[:], in_=pw[:], func=AF.Copy), rd=[P.R("pw")], wr=[P.R("pwb")])
    W = S + 32
    ub = P.sb([128, S], BF16, "ub")
    u = P.sb([128, W], F32, "u")
    sa = P.sb([128, W], F32, "sa")
    sb_ = P.sb([128, W], F32, "sbb")
    acc = P.sb([128, S], F32, "acc")
    ic = P.sb([128, S], F32, "ic")
    pl = P.sb([128, S], BF16, "pl")
    P.dma(ub[:], uT_d, wr=[P.R("ub")])
    P.dma(ic[:], ic_d, wr=[P.R("ic")])
    for nm, t in (("u", u), ("sa", sa), ("sbb", sb_)):
        P.i("pool", "memset", dict(ap=t[:], constant=0.0), wr=[P.R(nm)])
    P.i("act", "activation", dict(out=u[:, 16:16 + S], in_=ub[:], func=AF.Copy), rd=[P.R("ub")], wr=[P.R("u")])
    P.i("dve", "tensor_tensor", dict(out=sa[:, 1:W], in0=u[:, 0:W - 1], in1=u[:, 1:W], op=ALU.add),
         rd=[P.R("u")], wr=[P.R("sa")])
    P.i("dve", "tensor_scalar", dict(out=acc[:], in0=sa[:, 16:16 + S], scalar1=pc[:, 0:1], scalar2=None,
                                          op0=ALU.mult), rd=[P.R("sa"), P.R("pc")], wr=[P.R("acc")])
    P.i("dve", "tensor_tensor", dict(out=sb_[:, 2:W - 2], in0=sa[:, 1:W - 3], in1=sa[:, 3:W - 1], op=ALU.add),
         rd=[P.R("sa")], wr=[P.R("sbb")])
    P.i("dve", "scalar_tensor_tensor", dict(out=acc[:], in0=sb_[:, 16:16 + S], scalar=pc[:, 1:2], in1=acc[:],
                                                 op0=ALU.mult, op1=ALU.add),
         rd=[P.R("sbb"), P.R("pc"), P.R("acc")], wr=[P.R("acc")])
    P.i("dve", "tensor_tensor", dict(out=sa[:, 4:W - 4], in0=sb_[:, 2:W - 6], in1=sb_[:, 6:W - 2], op=ALU.add),
         rd=[P.R("sbb")], wr=[P.R("sa")])
    P.i("dve", "scalar_tensor_tensor", dict(out=acc[:], in0=sa[:, 16:16 + S], scalar=pc[:, 2:3], in1=acc[:],
                                                 op0=ALU.mult, op1=ALU.add),
         rd=[P.R("sa"), P.R("pc"), P.R("acc")], wr=[P.R("acc")])
    P.i("dve", "tensor_tensor", dict(out=sb_[:, 8:W - 8], in0=sa[:, 4:W - 12], in1=sa[:, 12:W - 4], op=ALU.add),
         rd=[P.R("sa")], wr=[P.R("sbb")])
    P.i("dve", "scalar_tensor_tensor", dict(out=acc[:], in0=sb_[:, 16:16 + S], scalar=pc[:, 3:4], in1=acc[:],
                                                 op0=ALU.mult, op1=ALU.add),
         rd=[P.R("sbb"), P.R("pc"), P.R("acc")], wr=[P.R("acc")])
    P.i("dve", "tensor_tensor", dict(out=acc[:], in0=acc[:], in1=ic[:], op=ALU.mult),
         rd=[P.R("acc"), P.R("ic")], wr=[P.R("acc")])
    P.i("dve", "tensor_tensor", dict(out=pl[:], in0=acc[:], in1=u[:, 16:16 + S], op=ALU.subtract),
         rd=[P.R("acc"), P.R("u")], wr=[P.R("pl")])
    ps_p = P.ps("ps_s0")
    pev = [P.sb([128, 512], BF16, f"pev{i}") for i in range(2)]
    for b in range(8):
        mm_group(P, ps_p[:], P.R("ps_s", 0), [(pwb[:], pl[:, b * 512:(b + 1) * 512])], rd=[P.R("pwb"), P.R("pl")])
        P.i("act", "activation", dict(out=pev[b % 2][:], in_=ps_p[:], func=AF.Identity,
                                                        scale=pc[:, 4:5]),
             rd=[P.R("ps_s", 0), P.R("pc")], wr=[P.R("pev", b % 2)])
        P.dma(pool_o[:, b * 512:(b + 1) * 512], pev[b % 2][:], rd=[P.R("pev", b % 2)])
    ps_s = [ps_p, P.ps("ps_s1"), P.ps("ps_s2")]
    ps_o = [P.ps("ps_o0"), P.ps("ps_o1")]
    ps_m = [P.ps("ps_m0"), P.ps("ps_m1")]
    pT = [P.sb([128, 512], BF16, f"pT{i}") for i in range(4)]
    rinv = [P.sb([128, 512], F32, f"rinv{i}") for i in range(2)]
    aev = [P.sb([128, 512], BF16, f"aev{i}") for i in range(2)]
    scale = 128 ** -0.5
    steps = [(h, qb, kt) for h in range(3) for qb in range(8) for kt in range(32)]
    NS = len(steps)

    def emit_s(i):
        h, qb, kt = steps[i]
        mm_group(P, ps_s[i % 3][:], P.R("ps_s", i % 3),
                 [(kT[:, kt * 128:(kt + 1) * 128], qT[:, h, qb * 512:(qb + 1) * 512])], rd=[P.R("k"), P.R("q", h)])
        P.i("act", "activation", dict(out=pT[i % 4][:], in_=ps_s[i % 3][:], func=AF.Exp, scale=scale),
            rd=[P.R("ps_s", i % 3)], wr=[P.R("pT", i % 4)])

    emit_s(0)
    emit_s(1)
    for i in range(NS):
        if i + 2 < NS:
            emit_s(i + 2)
        h, qb, kt = steps[i]
        ob = (i // 32) % 2
        mm_group(P, ps_o[ob][:], P.R("ps_o", ob), [(v[:, kt, :], pT[i % 4][:])], rd=[P.R("v"), P.R("pT", i % 4)],
                 start=(kt == 0), stop=(kt == 31))
        mm_group(P, ps_m[ob][:], P.R("ps_m", ob), [(c["ones"][:], pT[i % 4][:])], rd=[P.R("ones"), P.R("pT", i % 4)],
                 start=(kt == 0), stop=(kt == 31))
        if kt == 31:
            P.i("dve", "reciprocal", dict(out=rinv[ob][:], in_=ps_m[ob][:]), rd=[P.R("ps_m", ob)], wr=[P.R("rinv", ob)])
            P.i("dve", "tensor_tensor", dict(out=aev[ob][:], in0=ps_o[ob][:], in1=rinv[ob][:], op=ALU.mult),
                rd=[P.R("ps_o", ob), P.R("rinv", ob)], wr=[P.R("aev", ob)])
            P.dma(att_o[h, :, qb * 512:(qb + 1) * 512], aev[ob][:], rd=[P.R("aev", ob)])
    return P.finish()


def build_outffn(kind):
    T = 1026
    P = Prog()
    aT_d = P.din("aT", [D, T], BF16)
    xT_d = P.din("xT", [D, T], F32)
    wo_d = P.din("wo", [D, D], F32)
    g_d = P.din("g", [128, 16], F32)
    wu_d = P.din("wu", [D, 2 * DFF], F32)
    cw_d = P.din("cw", [128, 88, 4], F32)
    wd_d = P.din("wd", [DFF, D], F32)
    if kind == "c":
        oT_d = P.din("oT", [D, T], BF16)
        hn_d = P.din("hn", [128, 16], F32)
    out_o = P.dout("oxT", [D, 1024], F32)
    xm_s = P.nc.dram_tensor("xm_s", [D, T], F32, kind="Internal").ap()
    c = consts(P)
    blks = blocks_of(T)
    big = P.sb([128, 44 * 1024], BF16, "big")
    xT = big[:, 0:16 * T * 2].bitcast(F32).rearrange("p (k t) -> p k t", t=T)
    actT = big[:].rearrange("p (j t) -> p j t", t=1024)
    A = P.sb([128, 16, T], BF16, "A")
    g_sb = P.sb([128, 16], F32, "g_sb")
    cw = P.sb([128, 88, 4], F32, "cw")
    P.dma(g_sb[:], g_d, wr=[P.R("gsb", "A")])
    P.dma(cw[:], cw_d, wr=[P.R("cw")])
    xv = xT_d.rearrange("(k p) t -> p k t", p=128)
    av = aT_d.rearrange("(k p) t -> p k t", p=128)
    for k in range(16):
        P.dma(A[:, k, :], av[:, k, :], wr=[P.R("A", k, b) for b in range(3)])
        P.dma(xT[:, k, :], xv[:, k, :], wr=[P.R("x", k, b) for b in range(3)] + [P.R("alias")])
    if kind == "c":
        O = [P.sb([128, T], BF16, f"O{i}") for i in range(2)]
        hn = P.sb([128, 16], F32, "hn")
        ov = oT_d.rearrange("(k p) t -> p k t", p=128)
        P.dma(hn[:], hn_d, wr=[P.R("hn")])
        for k in range(16):
            P.dma(O[k % 2][:], ov[:, k, :], wr=[P.R("O", k % 2)])
            P.i("dve", "scalar_tensor_tensor", dict(
                out=A[:, k, :], in0=A[:, k, :], scalar=hn[:, k:k + 1], in1=O[k % 2][:], op0=ALU.mult, op1=ALU.mult),
                 rd=[P.R("O", k % 2), P.R("hn")] + [P.R("A", k, b) for b in range(3)],
                 wr=[P.R("A", k, b) for b in range(3)])
    ws = WStream(P, 32)
    pacc = [P.ps(f"pacc{i}") for i in range(6)]
    ps_ss = P.ps("ps_ss")
    xmv = xm_s.rearrange("(k p) t -> p k t", p=128)
    it = 0
    for m, ((wv,), wR) in enumerate(prefetched(ws, [[(wo_d[:, m * 128:(m + 1) * 128], 16, 128)] for m in range(16)])):
        for b, (t0, tw) in enumerate(blks):
            pa = pacc[it % 6]
            paR = P.R("pacc", it % 6)
            it += 1
            mm_group(P, pa[:, 0:tw], paR, [(wv[:, k, :], A[:, k, t0:t0 + tw]) for k in range(16)],
                     rd=[wR] + [P.R("A", k, b) for k in range(16)])
            P.i("dve", "tensor_tensor", dict(
                out=xT[:, m, t0:t0 + tw], in0=pa[:, 0:tw], in1=xT[:, m, t0:t0 + tw], op=ALU.add),
                 rd=[paR, P.R("x", m, b)], wr=[P.R("x", m, b)])
        P.dma(xmv[:, m, :], xT[:, m, :], rd=[P.R("x", m, b) for b in range(3)] + [P.R("alias")], wr=[P.R("xm", m)])
    rmsnorm_fm(P, c, xT, "x", g_sb, A, "A", T, ps_ss, D)
    allx = [P.R("x", k, b) for k in range(16) for b in range(3)]
    P.i("pool", "memset", dict(ap=c["eps"][:], constant=EPS), rd=allx + [P.R("eps")], wr=[P.R("alias"), P.R("eps")])
    raw = [P.sb([128, T], F32, f"raw{i}") for i in range(4)]
    tg = P.sb([128, 1024], F32, "tg")
    tv = P.sb([128, 1024], F32, "tv")
    specs = [[(wu_d[:, j * 128:(j + 1) * 128], 16, 128), (wu_d[:, DFF + j * 128:DFF + (j + 1) * 128], 16, 128)]
             for j in range(44)]
    for j, ((wg, wvv), wR) in enumerate(prefetched(ws, specs)):
        for gv, wv in enumerate((wg, wvv)):
            r_ap = raw[(j % 2) * 2 + gv]
            rR = P.R("raw", (j % 2) * 2 + gv)
            for b, (t0, tw) in enumerate(blks):
                pa = pacc[it % 6]
                paR = P.R("pacc", it % 6)
                it += 1
                mm_group(P, pa[:, 0:tw], paR, [(wv[:, k, :], A[:, k, t0:t0 + tw]) for k in range(16)],
                         rd=[wR] + [P.R("A", k, b) for k in range(16)])
                P.i("act", "activation", dict(
                    out=r_ap[:, t0:t0 + tw], in_=pa[:, 0:tw], func=AF.Copy), rd=[paR], wr=[rR])
            ch = gv * 44 + j
            t_ap = tg if gv == 0 else tv
            tR = P.R("tg") if gv == 0 else P.R("tv")
            eng = "dve"
            P.i(eng, "tensor_scalar", dict(
                out=t_ap[:], in0=r_ap[:, 1:1025], scalar1=cw[:, ch, 1:2], scalar2=cw[:, ch, 3:4],
                op0=ALU.mult, op1=ALU.add), rd=[rR, P.R("cw")], wr=[tR])
            P.i(eng, "scalar_tensor_tensor", dict(
                out=t_ap[:], in0=r_ap[:, 0:1024], scalar=cw[:, ch, 0:1], in1=t_ap[:], op0=ALU.mult, op1=ALU.add),
                 rd=[rR, P.R("cw"), tR], wr=[tR])
            P.i(eng, "scalar_tensor_tensor", dict(
                out=t_ap[:], in0=r_ap[:, 2:1026], scalar=cw[:, ch, 2:3], in1=t_ap[:], op0=ALU.mult, op1=ALU.add),
                 rd=[rR, P.R("cw"), tR], wr=[tR])
        P.i("act", "activation", dict(out=tg[:], in_=tg[:], func=AF.Silu), rd=[P.R("tg")], wr=[P.R("tg")])
        P.i("dve", "tensor_tensor", dict(out=actT[:, j, :], in0=tg[:], in1=tv[:], op=ALU.mult),
             rd=[P.R("tg"), P.R("tv"), P.R("alias")], wr=[P.R("act", j)])
    xo = [raw[i][:, 0:1024] for i in range(2)]
    specs = [[(wd_d[half * 2816:(half + 1) * 2816, m * 128:(m + 1) * 128], 22, 128)]
             for m in range(16) for half in range(2)]
    wit = prefetched(ws, specs)
    for m in range(16):
        P.dma(xo[m % 2], xmv[:, m, 1:1025], rd=[P.R("xm", m)], wr=[P.R("raw", m % 2)])
        pas = []
        for half in range(2):
            (wv,), wR = next(wit)
            for b in range(2):
                if half == 0:
                    pas.append((pacc[it % 6], P.R("pacc", it % 6)))
                    it += 1
                pa, paR = pas[b]
                mm_group(P, pa[:], paR, [(wv[:, k, :], actT[:, half * 22 + k, b * 512:(b + 1) * 512]) for k in range(22)],
                         rd=[wR] + [P.R("act", half * 22 + k) for k in range(22)], start=(half == 0), stop=(half == 1))
        for b in range(2):
            pa, paR = pas[b]
            P.i("dve", "tensor_tensor", dict(
                out=xo[m % 2][:, b * 512:(b + 1) * 512], in0=pa[:], in1=xo[m % 2][:, b * 512:(b + 1) * 512],
                op=ALU.add), rd=[paR, P.R("raw", m % 2)], wr=[P.R("raw", m % 2)])
        P.dma(out_o[m * 128:(m + 1) * 128, :], xo[m % 2], rd=[P.R("raw", m % 2)])
    return P.finish()


def build_inproj1():
    T = 1024
    P = Prog()
    xT_d = P.din("xT", [D, T], F32)
    g_d = P.din("g", [128, 16], F32)
    w_d = P.din("w", [D, 8224], F32)
    bg_d = P.din("bg", [32, 1], F32)
    o_o = P.dout("pT", [8192, T], BF16)
    gt_o = P.dout("gT", [32, T], F32)
    c = consts(P)
    xT = P.sb([128, 16, T], F32, "xT")
    hT = P.sb([128, 16, T], BF16, "hT")
    g_sb = P.sb([128, 16], F32, "g_sb")
    bg = P.sb([32, 1], F32, "bg")
    blks = blocks_of(T)
    P.dma(g_sb[:], g_d, wr=[P.R("gsb", "h")])
    P.dma(bg[:], bg_d, wr=[P.R("bg")])
    xv = xT_d.rearrange("(k p) t -> p k t", p=128)
    for k in range(16):
        for b, (t0, tw) in enumerate(blks):
            P.dma(xT[:, k, t0:t0 + tw], xv[:, k, t0:t0 + tw], wr=[P.R("x", k, b)])
    ps_ss = P.ps("ps_ss")
    rmsnorm_fm(P, c, xT, "x", g_sb, hT, "h", T, ps_ss, D)
    ws = WStream(P, 16)
    pacc = [P.ps(f"pacc{i}") for i in range(4)]
    ev = [P.sb([128, 512], BF16, f"ev{i}") for i in range(4)]
    gev = P.sb([32, 512], F32, "gev")
    it = 0
    specs = [[(w_d[:, m * 128:m * 128 + (128 if m < 64 else 32)], 16, (128 if m < 64 else 32))] for m in range(65)]
    for m, ((wv,), wR) in enumerate(prefetched(ws, specs)):
        mw = 128 if m < 64 else 32
        for b, (t0, tw) in enumerate(blks):
            pa = pacc[it % 4]
            paR = P.R("pacc", it % 4)
            e_ = ev[it % 4]
            eR = P.R("ev", it % 4)
            it += 1
            mm_group(P, pa[0:mw, 0:tw], paR, [(wv[:, k, :], hT[:, k, t0:t0 + tw]) for k in range(16)],
                     rd=[wR] + [P.R("h", k, b) for k in range(16)])
            if m == 64:
                P.i("act", "activation", dict(out=gev[:, 0:tw], in_=pa[0:32, 0:tw],
                                                                  func=AF.Identity, bias=bg[:], scale=1.0),
                     rd=[paR, P.R("bg")], wr=[P.R("gev")])
                P.dma(gt_o[:, t0:t0 + tw], gev[:, 0:tw], rd=[P.R("gev")], q="actq")
            else:
                if m < 16:
                    kw = dict(out=e_[:, 0:tw], in_=pa[:, 0:tw], func=AF.Identity, scale=1.0 / 16.0)
                elif m < 48:
                    kw = dict(out=e_[:, 0:tw], in_=pa[:, 0:tw], func=AF.Copy)
                else:
                    kw = dict(out=e_[:, 0:tw], in_=pa[:, 0:tw], func=AF.Sigmoid)
                P.i("act", "activation", kw, rd=[paR], wr=[eR])
                P.dma(o_o[m * 128:(m + 1) * 128, t0:t0 + tw], e_[:, 0:tw], rd=[eR], q="actq")
    return P.finish()


def build_mlstm():
    NP = 2
    NCH = 32
    P = Prog()
    qT_d = P.din("qT", [NP, 2, 128, S], BF16)
    kT_d = P.din("kT", [NP, 2, 128, S], BF16)
    k_d = P.din("k", [NP, 128, NCH, 256], BF16)
    v_d = P.din("v", [NP, 128, NCH, 256], BF16)
    gt_d = P.din("gt", [NP, 128, 4, NCH], F32)
    tri_d = P.din("tri", [128, 2, 128], F32)
    hn_o = P.dout("hn", [NP, 128, NCH, 256], BF16)
    c = consts(P)
    tri = P.sb([128, 2, 128], F32, "tri")
    onesf = P.sb([128, 128], F32, "onesf")
    one1 = P.sb([128, 1], F32, "one1")
    P.dma(tri[:], tri_d, wr=[P.R("tri")])
    P.i("pool", "memset", dict(ap=onesf[:], constant=1.0), wr=[P.R("onesf")])
    P.i("pool", "memset", dict(ap=one1[:], constant=1.0), wr=[P.R("one1")])
    qT = P.sb([128, 2, S], BF16, "qT")
    kT = P.sb([128, 2, S], BF16, "kT")
    kk = P.sb([128, NCH, 256], BF16, "kk")
    vx = P.sb([128, NCH, 257], BF16, "vx")
    gt = P.sb([128, 4, NCH], F32, "gt")
    lf = P.sb([128, 2, NCH], F32, "lf")
    bc = P.sb([128, 2, NCH], F32, "bc")
    tot = P.sb([128, 2, NCH], F32, "tot")
    av = P.sb([128, 2, NCH], F32, "av")
    bv = P.sb([128, 2, NCH], F32, "bv")
    b2 = P.sb([128, 2, NCH], F32, "b2")
    dc = P.sb([128, 2, NCH], F32, "dc")
    tmp = P.sb([128, 2, NCH], F32, "tmp")
    hacc = P.sb([128, NCH, 256], F32, "hacc")
    ssq = P.sb([128, NCH], F32, "ssq")
    junk = P.sb([128, 256], F32, "junk")
    psS = [P.ps("psS0"), P.ps("psS1")]
    ps_g = psS[0]
    psN = [P.ps("psN0"), P.ps("psN1")]
    psC = [[P.ps("psC00"), P.ps("psC01")], [P.ps("psC10"), P.ps("psC11")]]
    Cst = [P.sb([128, 2, 257], F32, f"Cst{d}") for d in range(2)]
    Cbf = [P.sb([128, 2, 257], BF16, f"Cbf{d}") for d in range(2)]
    Sm = [P.sb([128, 128], BF16, f"Sm{d}") for d in range(2)]
    k2 = [P.sb([128, 256], BF16, f"k2{d}") for d in range(2)]
    dn = [P.sb([128, 4], F32, f"dn{d}") for d in range(2)]
    hev = [P.sb([128, 256], BF16, f"hev{i}") for i in range(2)]
    for p in range(NP):
        for h in range(2):
            P.dma(qT[:, h, :], qT_d[p, h], wr=[P.R("qT")])
            P.dma(kT[:, h, :], kT_d[p, h], wr=[P.R("kT")])
        P.dma(kk[:], k_d[p], wr=[P.R("kk")])
        P.dma(vx[:, :, 0:256], v_d[p], wr=[P.R("vx")])
        P.i("pool", "memset", dict(ap=vx[:, :, 256:257], constant=1.0), wr=[P.R("vx")], rd=[])
        P.dma(gt[:], gt_d[p], wr=[P.R("gt")])
        for d in range(2):
            P.i("act", "activation", dict(out=lf[:, d, :], in_=gt[:, 2 * d + 1, :], func=AF.Exp,
                                                            scale=-1.0), rd=[P.R("gt")], wr=[P.R("lf")])
        P.i("act", "activation", dict(out=lf[:], in_=lf[:], func=AF.Ln, bias=one1[:], scale=1.0),
             rd=[P.R("lf"), P.R("one1")], wr=[P.R("lf")])
        P.i("dve", "tensor_scalar", dict(out=lf[:], in0=lf[:], scalar1=-1.0, scalar2=None, op0=ALU.mult),
             rd=[P.R("lf")], wr=[P.R("lf")])
        for d in range(2):
            mm_group(P, ps_g[:, d * NCH:(d + 1) * NCH], P.R("psS", 0), [(tri[:, d, :], lf[:, d, :])],
                     rd=[P.R("tri"), P.R("lf")])
        mm_group(P, ps_g[:, 2 * NCH:4 * NCH], P.R("psS", 0), [(onesf[:], lf[:].rearrange("p d c -> p (d c)"))],
                 rd=[P.R("onesf"), P.R("lf")])
        P.i("dve", "tensor_copy", dict(out=bc[:].rearrange("p d c -> p (d c)"), in_=ps_g[:, 0:2 * NCH]),
             rd=[P.R("psS", 0)], wr=[P.R("bc")])
        P.i("dve", "tensor_copy", dict(out=tot[:].rearrange("p d c -> p (d c)"), in_=ps_g[:, 2 * NCH:4 * NCH]),
             rd=[P.R("psS", 0)], wr=[P.R("tot")])
        P.i("act", "activation", dict(out=av[:], in_=bc[:], func=AF.Exp), rd=[P.R("bc")], wr=[P.R("av")])
        P.i("act", "activation", dict(out=dc[:], in_=tot[:], func=AF.Exp), rd=[P.R("tot")], wr=[P.R("dc")])
        for d in range(2):
            P.i("dve", "tensor_tensor", dict(out=tmp[:, d, :], in0=gt[:, 2 * d, :], in1=bc[:, d, :],
                                                               op=ALU.subtract),
                 rd=[P.R("gt"), P.R("bc")], wr=[P.R("tmp")])
        P.i("act", "activation", dict(out=bv[:], in_=tmp[:], func=AF.Exp), rd=[P.R("tmp")], wr=[P.R("bv")])
        P.i("dve", "tensor_tensor", dict(out=tmp[:], in0=tmp[:], in1=tot[:], op=ALU.add),
             rd=[P.R("tmp"), P.R("tot"), P.R("bv")], wr=[P.R("tmp")])
        P.i("act", "activation", dict(out=b2[:], in_=tmp[:], func=AF.Exp), rd=[P.R("tmp")], wr=[P.R("b2")])
        gR = [P.R("av"), P.R("bv"), P.R("b2"), P.R("dc")]
        for d in range(2):
            P.i("pool", "memset", dict(ap=Cst[d][:], constant=0.0), wr=[P.R("Cst", d)])
            P.i("pool", "memset", dict(ap=Cbf[d][:], constant=0.0), wr=[P.R("Cbf", d)])
        for step in range(NCH):
            for d in range(2):
                ch = step if d == 0 else NCH - 1 - step
                cs_ = slice(ch * 128, (ch + 1) * 128)
                S_ap, SR = psS[d], P.R("psS", d)
                N_ap, NR = psN[d], P.R("psN", d)
                mm_group(P, S_ap[:, 0:128], SR, [(kT[:, h, cs_], qT[:, h, cs_]) for h in range(2)],
                         rd=[P.R("kT"), P.R("qT")])
                P.i("dve", "scalar_tensor_tensor", dict(
                    out=Sm[d][:], in0=S_ap[:, 0:128], scalar=bv[:, d, ch:ch + 1], in1=tri[:, d, :],
                    op0=ALU.mult, op1=ALU.mult), rd=[SR, P.R("tri")] + gR, wr=[P.R("Sm", d)])
                mm_group(P, N_ap[:, 0:257], NR,
                         [(Sm[d][:], vx[:, ch, :])] + [(qT[:, h, cs_], Cbf[d][:, h, :]) for h in range(2)],
                         rd=[P.R("Sm", d), P.R("vx"), P.R("qT"), P.R("Cbf", d)])
                P.i("act", "activation", dict(out=dn[d][:, 0:1], in_=N_ap[:, 256:257], func=AF.Abs,
                                              scale=av[:, d, ch:ch + 1]), rd=[NR] + gR, wr=[P.R("dn", d)])
                P.i("dve", "tensor_scalar", dict(
                    out=dn[d][:, 1:2], in0=dn[d][:, 0:1], scalar1=1.0, scalar2=None, op0=ALU.max),
                     rd=[P.R("dn", d)], wr=[P.R("dn", d)])
                P.i("dve", "reciprocal", dict(out=dn[d][:, 2:3], in_=dn[d][:, 1:2]),
                     rd=[P.R("dn", d)], wr=[P.R("dn", d)])
                P.i("dve", "tensor_tensor", dict(
                    out=dn[d][:, 3:4], in0=dn[d][:, 2:3], in1=av[:, d, ch:ch + 1], op=ALU.mult),
                     rd=[P.R("dn", d)] + gR, wr=[P.R("dn", d)])
                if step < NCH // 2:
                    P.i("act", "activation", dict(
                        out=hacc[:, ch, :], in_=N_ap[:, 0:256], func=AF.Identity, scale=dn[d][:, 3:4]),
                         rd=[NR, P.R("dn", d)], wr=[P.R("hacc", ch)])
                else:
                    P.i("dve", "scalar_tensor_tensor", dict(
                        out=hacc[:, ch, :], in0=N_ap[:, 0:256], scalar=dn[d][:, 3:4], in1=hacc[:, ch, :],
                        op0=ALU.mult, op1=ALU.add), rd=[NR, P.R("dn", d), P.R("hacc", ch)], wr=[P.R("hacc", ch)])
                P.i("pool", "tensor_scalar", dict(
                    out=k2[d][:], in0=kk[:, ch, :], scalar1=b2[:, d, ch:ch + 1], scalar2=None, op0=ALU.mult),
                     rd=[P.R("kk")] + gR, wr=[P.R("k2", d)])
                for h in range(2):
                    mm_group(P, psC[d][h][:, 0:257], P.R("psC", d, h), [(k2[d][:, h * 128:(h + 1) * 128], vx[:, ch, :])],
                             rd=[P.R("k2", d), P.R("vx")])
                    P.i("dve", "scalar_tensor_tensor", dict(
                        out=Cst[d][:, h, :], in0=Cst[d][:, h, :], scalar=dc[:, d, ch:ch + 1], in1=psC[d][h][:, 0:257],
                        op0=ALU.mult, op1=ALU.add), rd=[P.R("psC", d, h), P.R("Cst", d)] + gR, wr=[P.R("Cst", d)])
                P.i("act", "activation", dict(out=Cbf[d][:], in_=Cst[d][:], func=AF.Copy),
                     rd=[P.R("Cst", d)], wr=[P.R("Cbf", d)])
        for ch in range(NCH):
            P.i("act", "activation", dict(out=junk[:], in_=hacc[:, ch, :], func=AF.Square,
                                                              accum_out=ssq[:, ch:ch + 1]),
                 rd=[P.R("hacc", ch)], wr=[P.R("junk"), P.R("ssq")])
        P.i("act", "activation", dict(out=ssq[:], in_=ssq[:], func=AF.Ln, bias=c["eps"][:], scale=1.0 / 256),
             rd=[P.R("ssq"), P.R("eps")], wr=[P.R("ssq")])
        P.i("act", "activation", dict(out=ssq[:], in_=ssq[:], func=AF.Exp, scale=-0.5),
             rd=[P.R("ssq")], wr=[P.R("ssq")])
        for ch in range(NCH):
            P.i("dve", "tensor_scalar", dict(
                out=hev[ch % 2][:], in0=hacc[:, ch, :], scalar1=ssq[:, ch:ch + 1], scalar2=None, op0=ALU.mult),
                 rd=[P.R("hacc", ch), P.R("ssq")], wr=[P.R("hev", ch % 2)])
            P.dma(hn_o[p, :, ch, :], hev[ch % 2][:], rd=[P.R("hev", ch % 2)])
    return P.finish()


N_LAUNCH = [0]


def run(nc, in_maps):
    N_LAUNCH[0] += 1
    res = run_bass_kernel_spmd(nc, in_maps, core_ids=list(range(NCORES)))
    return res.results


def pc128(v, kc):
    return np.ascontiguousarray(np.asarray(v, np.float32).reshape(kc, 128).T)


def rope_tables():
    rows = S // 64
    row = np.repeat(np.arange(rows), 64).astype(np.float32)
    col = np.tile(np.arange(64), rows).astype(np.float32)
    inv = (1.0 / (np.float32(10000.0) ** (np.arange(0, 64, 2, dtype=np.float32) / np.float32(64)))).astype(np.float32)
    a_r = row[:, None] * inv[None, :]
    a_c = col[:, None] * inv[None, :]
    ang = np.concatenate([a_r, a_r, a_c, a_c], axis=-1)
    return np.cos(ang).astype(np.float32), np.sin(ang).astype(np.float32)


def rot_matrix_T():
    R = np.zeros((128, 128), np.float32)
    for base in (0, 64):
        for j in range(32):
            R[base + j, base + j + 32] = -1.0
            R[base + j + 32, base + j] = 1.0
    return np.ascontiguousarray(R.T).astype(NPBF)


def halo_T(a_tok, b, q):
    F_ = a_tok.shape[-1]
    out = np.zeros((F_, 1026), a_tok.dtype)
    lo, hi = q * 1024 - 1, q * 1024 + 1025
    l2, h2 = max(lo, 0), min(hi, S)
    out[:, l2 - lo:h2 - lo] = a_tok[b, l2:h2].T
    return out


def ffn_inputs(layer, norm_ffn, w_up, conv_w, conv_b, w_down):
    cw = np.zeros((128, 88, 4), np.float32)
    cwl = np.asarray(conv_w[layer], np.float32)
    cbl = np.asarray(conv_b[layer], np.float32)
    for i in range(3):
        cw[:, :, i] = cwl[i].reshape(88, 128).T
    cw[:, :, 3] = cbl.reshape(88, 128).T
    return {"g": pc128(norm_ffn[layer], 16), "wu": np.ascontiguousarray(w_up[layer], np.float32), "cw": cw,
            "wd": np.ascontiguousarray(w_down[layer], np.float32)}


def kernel(x, norm_mix, norm_ffn, w_in_ab, pool_w, pool_scale, q_norm, k_norm, w_out_ab,
           w_in_c, b_gate_c, h_norm_c, w_out_c, w_up, conv_w, conv_b, w_down):
    x = np.asarray(x, np.float32)
    B = x.shape[0]
    cores = [(c // 4, c % 4) for c in range(NCORES)]
    cos, sin = rope_tables()
    nc = build_inproj0()
    ims = []
    for (b, q) in cores:
        sl = slice(q * 1024, (q + 1) * 1024)
        cs = np.stack([cos[sl].T, sin[sl].T], axis=1)
        ims.append({"xT": np.ascontiguousarray(x[b, sl].T), "g": pc128(norm_mix[0], 16),
                    "w": np.ascontiguousarray(w_in_ab[0], np.float32),
                    "qkn": np.ascontiguousarray(np.stack([q_norm[0], k_norm[0]], axis=1), np.float32),
                    "cs": np.ascontiguousarray(cs, np.float32), "rt": rot_matrix_T()})
    r1 = run(nc, ims)
    qT = np.zeros((B, 1536, S), NPBF)
    kT = np.zeros((B, 512, S), NPBF)
    vT = np.zeros((B, 512, S), NPBF)
    uT = np.zeros((B, 512, S), NPBF)
    for ci, (b, q) in enumerate(cores):
        sl = slice(q * 1024, (q + 1) * 1024)
        qT[b][:, sl] = r1[ci]["qT"]
        kT[b][:, sl] = r1[ci]["kT"]
        vT[b][:, sl] = r1[ci]["vT"]
        uT[b][:, sl] = r1[ci]["uT"]
    nc = build_attn()
    ims = []
    t = np.arange(S)
    for (b, g) in cores:
        w = (2, 4, 8, 16)[g]
        lo = np.clip(t - w // 2, 0, S)
        hi = np.clip(t + w // 2, 0, S)
        ic = np.broadcast_to((1.0 / (hi - lo).astype(np.float32))[None, :], (128, S))
        pc = np.zeros((128, 6), np.float32)
        pc[:, g] = 1.0
        pc[:, 4] = np.asarray(pool_scale[0], np.float32)[g * 128:(g + 1) * 128]
        vg = vT[b][g * 128:(g + 1) * 128]
        v_tm = np.ascontiguousarray(vg.T.reshape(32, 128, 128).transpose(1, 0, 2))
        ims.append({"qT": np.ascontiguousarray(qT[b][g * 384:(g + 1) * 384].reshape(3, 128, S)),
                    "kT": np.ascontiguousarray(kT[b][g * 128:(g + 1) * 128]), "v": v_tm,
                    "uT": np.ascontiguousarray(uT[b][g * 128:(g + 1) * 128]),
                    "pw": np.ascontiguousarray(pool_w[0][g], np.float32), "pc": pc,
                    "ic": np.ascontiguousarray(ic, np.float32)})
    r2 = run(nc, ims)
    cat = np.zeros((B, S, D), NPBF)
    for ci, (b, g) in enumerate(cores):
        cat[b][:, g * 128:(g + 1) * 128] = r2[ci]["poolT"].T
        cat[b][:, 512 + g * 384:512 + (g + 1) * 384] = r2[ci]["attT"].reshape(384, S).T
    nc = build_outffn("ab")
    f0 = ffn_inputs(0, norm_ffn, w_up, conv_w, conv_b, w_down)
    ims = []
    for (b, q) in cores:
        d = {"aT": halo_T(cat, b, q), "xT": halo_T(x, b, q), "wo": np.ascontiguousarray(w_out_ab[0], np.float32)}
        d.update(f0)
        ims.append(d)
    r3 = run(nc, ims)
    x1 = np.zeros((B, S, D), np.float32)
    for ci, (b, q) in enumerate(cores):
        x1[b, q * 1024:(q + 1) * 1024] = r3[ci]["oxT"].T
    nc = build_inproj1()
    ims = []
    for (b, q) in cores:
        sl = slice(q * 1024, (q + 1) * 1024)
        ims.append({"xT": np.ascontiguousarray(x1[b, sl].T), "g": pc128(norm_mix[1], 16),
                    "w": np.ascontiguousarray(w_in_c[0], np.float32),
                    "bg": np.ascontiguousarray(np.asarray(b_gate_c[0], np.float32).reshape(32, 1))})
    r4 = run(nc, ims)
    pT = np.zeros((B, 8192, S), NPBF)
    gT = np.zeros((B, 32, S), np.float32)
    for ci, (b, q) in enumerate(cores):
        sl = slice(q * 1024, (q + 1) * 1024)
        pT[b][:, sl] = r4[ci]["pT"]
        gT[b][:, sl] = r4[ci]["gT"]
    nc = build_mlstm()
    tri = np.zeros((128, 2, 128), np.float32)
    ii = np.arange(128)
    tri[:, 0, :] = (ii[:, None] <= ii[None, :])
    tri[:, 1, :] = (ii[:, None] >= ii[None, :])
    ims = []
    pairs = [(p // 8, p % 8) for p in range(16)]
    for ci in range(NCORES):
        d = {k: [] for k in ("qT", "kT", "k", "v", "gt")}
        for (b, h) in pairs[2 * ci:2 * ci + 2]:
            qh = pT[b][h * 256:(h + 1) * 256]
            kh = pT[b][2048 + h * 256:2048 + (h + 1) * 256]
            vh = pT[b][4096 + h * 256:4096 + (h + 1) * 256]
            d["qT"].append(qh.reshape(2, 128, S))
            d["kT"].append(kh.reshape(2, 128, S))
            d["k"].append(kh.T.reshape(32, 128, 256).transpose(1, 0, 2))
            d["v"].append(vh.T.reshape(32, 128, 256).transpose(1, 0, 2))
            gg = gT[b].reshape(4, 8, S)[:, h]
            d["gt"].append(gg.reshape(4, 32, 128).transpose(2, 0, 1))
        im = {k: np.ascontiguousarray(np.stack(v_)) for k, v_ in d.items()}
        im["tri"] = tri
        ims.append(im)
    r5 = run(nc, ims)
    hn = np.zeros((B, S, D), NPBF)
    for ci in range(NCORES):
        for j, (b, h) in enumerate(pairs[2 * ci:2 * ci + 2]):
            hh = r5[ci]["hn"][j]
            hn[b][:, h * 256:(h + 1) * 256] = hh.transpose(1, 0, 2).reshape(S, 256)
    og = np.ascontiguousarray(pT[:, 6144:8192].transpose(0, 2, 1))
    nc = build_outffn("c")
    f1 = ffn_inputs(1, norm_ffn, w_up, conv_w, conv_b, w_down)
    ims = []
    for (b, q) in cores:
        d = {"aT": halo_T(hn, b, q), "oT": halo_T(og, b, q), "xT": halo_T(x1, b, q),
             "hn": pc128(h_norm_c[0], 16), "wo": np.ascontiguousarray(w_out_c[0], np.float32)}
        d.update(f1)
        ims.append(d)
    r6 = run(nc, ims)
    out = np.zeros((B, S, D), np.float32)
    for ci, (b, q) in enumerate(cores):
        out[b, q * 1024:(q + 1) * 1024] = r6[ci]["oxT"].T
    return out
```

```python
import numpy as np
import ml_dtypes
from contextlib import ExitStack
import concourse.bass as bass
import concourse.mybir as mybir
from concourse.bass_utils import run_bass_kernel_spmd

F32, BF16 = mybir.dt.float32, mybir.dt.bfloat16
AF = mybir.ActivationFunctionType
ALU = mybir.AluOpType
NPBF = ml_dtypes.bfloat16
NDMA = 12
D = 2048
S = 4096
DFF = 5632
EPS = 1e-6
NCORES = 8


PSUM_KEYS = ("pacc", "ps_ss", "ps_h", "ps_r", "ps_p", "ps_s", "ps_o", "ps_m", "psS", "psN", "psC")


class Res:
    __slots__ = ("w", "rd", "excl")

    def __init__(self, excl=False):
        self.w = None
        self.rd = []
        self.excl = excl


class Prog:
    def __init__(self):
        self.nc = bass.Bass("TRN2", target_bir_lowering=False)
        self.ops = []
        self.st = ExitStack()
        self.res = {}
        self.nm = 0

    def R(self, *key):
        r = self.res.get(key)
        if r is None:
            r = self.res[key] = Res(key[0] in PSUM_KEYS)
        return r

    def sb(self, shape, dt, name=None):
        self.nm += 1
        return self.st.enter_context(self.nc.sbuf_tensor("S_" + (name or f"sb{self.nm}"), list(shape), dt))

    def ps(self, name=None):
        self.nm += 1
        return self.st.enter_context(self.nc.psum_tensor("P_" + (name or f"ps{self.nm}"), [128, 512], F32))

    def din(self, name, shape, dt):
        return self.nc.dram_tensor(name, list(shape), dt, kind="ExternalInput").ap()

    def dout(self, name, shape, dt):
        return self.nc.dram_tensor(name, list(shape), dt, kind="ExternalOutput").ap()

    def op(self, eng, fn, rd=(), wr=()):
        i = len(self.ops)
        deps = set()
        wr = list(wr) + [r for r in rd if r.excl]
        for r in rd:
            if r.w is not None:
                deps.add(r.w)
        for r in wr:
            if r.w is not None:
                deps.add(r.w)
            deps.update(r.rd)
        for r in rd:
            r.rd.append(i)
        for r in wr:
            r.w = i
            r.rd = []
        deps.discard(i)
        self.ops.append((eng, fn, deps))
        return i

    def i(self, eng, meth, kw, rd=(), wr=()):
        return self.op(eng, lambda e: getattr(e, meth)(**kw), rd, wr)

    def dma(self, out, in_, rd=(), wr=(), q="sp"):
        return self.op(q, lambda e: e.dma_start(out=out, in_=in_), rd, wr)

    def finish(self):
        nc, ops, st = self.nc, self.ops, self.st
        n = len(ops)
        signal = [False] * n
        for (_, _, deps) in ops:
            for d in deps:
                signal[d] = True
        engs = ["pe", "act", "dve", "pool"]
        DMAQ = {"sp": "sp", "actq": "act"}
        tok = [None] * n
        cnt = {e: 0 for e in engs}
        ndma = 0
        idx = {e: [] for e in engs + ["sp"]}
        for i, (eng, _, _) in enumerate(ops):
            idx[DMAQ.get(eng, eng)].append(i)
            if eng in DMAQ:
                tok[i] = (("d", ndma % NDMA), 16 * (ndma // NDMA + 1))
                ndma += 1
            elif signal[i]:
                cnt[eng] += 1
                tok[i] = (eng, cnt[eng])
        sems = {e: st.enter_context(nc.semaphore("s_" + e)) for e in engs}
        for k in range(NDMA):
            sems[("d", k)] = st.enter_context(nc.semaphore(f"s_d{k}"))
        block = st.enter_context(nc.Block())

        def emit(engname, e):
            known = {}
            for i in idx[engname]:
                oeng, fn, deps = ops[i]
                isdma = oeng in DMAQ
                need = {}
                for d in deps:
                    if engname == "pe" and ops[d][0] == "pe":
                        continue
                    s, v = tok[d]
                    if need.get(s, 0) < v:
                        need[s] = v
                if isdma:
                    s, v = tok[i]
                    if v > 16:
                        need[s] = max(need.get(s, 0), v - 16)
                for s, v in need.items():
                    if known.get(s, 0) < v:
                        e.wait_ge(sems[s], v)
                        known[s] = v
                ins = fn(e)
                if tok[i] is not None:
                    ins.then_inc(sems[tok[i][0]], 16 if isdma else 1)
            if engname == "sp":
                for k in range(min(NDMA, ndma)):
                    tot = 16 * ((ndma - 1 - k) // NDMA + 1)
                    if known.get(("d", k), 0) < tot:
                        e.wait_ge(sems[("d", k)], tot)

        block.sync(lambda e: emit("sp", e))
        block.tensor(lambda e: emit("pe", e))
        block.scalar(lambda e: emit("act", e))
        block.vector(lambda e: emit("dve", e))
        block.gpsimd(lambda e: emit("pool", e))
        st.close()
        return nc


def mm_group(P, ps_ap, psR, pairs, rd, start=True, stop=True):
    pairs = list(pairs)

    def fn(e):
        n = len(pairs)
        ins = None
        for k, (l, r) in enumerate(pairs):
            ins = e.matmul(ps_ap, lhsT=l, rhs=r, start=(start and k == 0), stop=(stop and k == n - 1))
        return ins

    return P.op("pe", fn, rd=rd, wr=[psR])


class WStream:
    def __init__(self, P, slots):
        self.P = P
        self.slots = slots
        self.stage = [P.sb([128, slots * 128], F32, f"wstage{i}") for i in range(2)]
        self.wb = [P.sb([128, slots * 128], BF16, f"wbf{i}") for i in range(2)]
        self.k = 0

    def load(self, pieces, scale_aps=None):
        P = self.P
        i = self.k % 2
        self.k += 1
        sR = P.R("wstage", i)
        stage = self.stage[i]
        bR = P.R("wbf", i)
        off = 0
        views = []
        for (ap, kc, m) in pieces:
            dst = stage[:, off:off + kc * m].rearrange("p (k m) -> p k m", m=m)
            src = ap.rearrange("(k p) m -> p k m", p=128)
            P.dma(dst, src, wr=[sR])
            views.append(self.wb[i][:, off:off + kc * m].rearrange("p (k m) -> p k m", m=m))
            off += kc * m
        eng = "pool" if (self.k % 2) else "act"
        src_all = stage[:, 0:off]
        dst_all = self.wb[i][:, 0:off]
        if eng == "pool":
            P.i("pool", "tensor_copy", dict(out=dst_all, in_=src_all), rd=[sR], wr=[bR])
        else:
            P.i("act", "activation", dict(out=dst_all, in_=src_all, func=AF.Copy), rd=[sR], wr=[bR])
        return views, bR


def prefetched(ws, specs):
    nxt = ws.load(specs[0])
    for i in range(len(specs)):
        cur = nxt
        if i + 1 < len(specs):
            nxt = ws.load(specs[i + 1])
        yield cur


def consts(P):
    c = {}
    c["ones"] = P.sb([128, 128], BF16, "ones")
    c["eps"] = P.sb([128, 1], F32, "epsc")
    P.i("pool", "memset", dict(ap=c["ones"][:], constant=1.0), wr=[P.R("ones")])
    P.i("pool", "memset", dict(ap=c["eps"][:], constant=EPS), wr=[P.R("eps")])
    return c


def blocks_of(T):
    if T % 512 == 0:
        return [(i * 512, 512) for i in range(T // 512)]
    assert T % 3 == 0
    w = T // 3
    return [(i * w, w) for i in range(3)]


def rmsnorm_fm(P, c, xT, xkey, g_sb, hT, hkey, T, ps_ss, dim):
    KC = dim // 128
    blks = blocks_of(T)
    rstd = P.sb([128, T], F32, "rstd_" + hkey)
    lnv = P.sb([128, 512], F32, "lnv_" + hkey)
    for b, (t0, tw) in enumerate(blks):
        for k in range(KC):
            P.i("act", "activation", dict(out=hT[:, k, t0:t0 + tw], in_=xT[:, k, t0:t0 + tw],
                                                            func=AF.Square),
                 rd=[P.R(xkey, k, b)], wr=[P.R(hkey, k, b)])
        mm_group(P, ps_ss[:, 0:tw], P.R("ps_ss"),
                 [(c["ones"][:], hT[:, k, t0:t0 + tw]) for k in range(KC)],
                 rd=[P.R("ones")] + [P.R(hkey, k, b) for k in range(KC)])
        P.i("act", "activation", dict(out=lnv[:, 0:tw], in_=ps_ss[:, 0:tw], func=AF.Ln, bias=c["eps"][:],
                                           scale=1.0 / dim),
             rd=[P.R("ps_ss"), P.R("eps")], wr=[P.R("lnv", hkey)])
        P.i("act", "activation", dict(out=rstd[:, t0:t0 + tw], in_=lnv[:, 0:tw], func=AF.Exp, scale=-0.5),
             rd=[P.R("lnv", hkey)], wr=[P.R("rstd", hkey, b)])
        for k in range(KC):
            P.i("dve", "scalar_tensor_tensor", dict(
                out=hT[:, k, t0:t0 + tw], in0=xT[:, k, t0:t0 + tw], scalar=g_sb[:, k:k + 1],
                in1=rstd[:, t0:t0 + tw], op0=ALU.mult, op1=ALU.mult),
                 rd=[P.R(xkey, k, b), P.R("rstd", hkey, b), P.R("gsb", hkey)], wr=[P.R(hkey, k, b)])


def build_inproj0():
    T = 1024
    P = Prog()
    xT_d = P.din("xT", [D, T], F32)
    g_d = P.din("g", [128, 16], F32)
    w_d = P.din("w", [D, 3072], F32)
    qk_d = P.din("qkn", [128, 2], F32)
    cs_d = P.din("cs", [128, 2, T], F32)
    rt_d = P.din("rt", [128, 128], BF16)
    qT_o = P.dout("qT", [1536, T], BF16)
    kT_o = P.dout("kT", [512, T], BF16)
    vT_o = P.dout("vT", [512, T], BF16)
    uT_o = P.dout("uT", [512, T], BF16)
    c = consts(P)
    xT = P.sb([128, 16, T], F32, "xT")
    hT = P.sb([128, 16, T], BF16, "hT")
    g_sb = P.sb([128, 16], F32, "g_sb")
    qk_sb = P.sb([128, 2], F32, "qk_sb")
    cs = P.sb([128, 2, T], F32, "cs")
    rt = P.sb([128, 128], BF16, "rt")
    blks = blocks_of(T)
    P.dma(g_sb[:], g_d, wr=[P.R("gsb", "h")])
    P.dma(qk_sb[:], qk_d, wr=[P.R("qk")])
    P.dma(cs[:], cs_d, wr=[P.R("cs")])
    P.dma(rt[:], rt_d, wr=[P.R("rt")])
    xv = xT_d.rearrange("(k p) t -> p k t", p=128)
    for k in range(16):
        for b, (t0, tw) in enumerate(blks):
            P.dma(xT[:, k, t0:t0 + tw], xv[:, k, t0:t0 + tw], wr=[P.R("x", k, b)])
    ps_ss = P.ps("ps_ss")
    rmsnorm_fm(P, c, xT, "x", g_sb, hT, "h", T, ps_ss, D)
    ws = WStream(P, 16)
    pacc = [P.ps("pacc0"), P.ps("pacc1")]
    ps_h = P.ps("ps_h")
    ps_r = P.ps("ps_r")
    ev = [P.sb([128, 512], BF16, f"ev{i}") for i in range(2)]
    sqh = P.sb([128, 512], BF16, "sqh")
    qg = P.sb([128, 512], BF16, "qg")
    lnh = P.sb([128, 512], F32, "lnh")
    rsh = P.sb([128, 512], F32, "rsh")
    t1 = P.sb([128, 512], F32, "t1")
    t2 = P.sb([128, 512], F32, "t2")
    it = 0
    for m, ((wv,), wR) in enumerate(prefetched(ws, [[(w_d[:, m * 128:(m + 1) * 128], 16, 128)] for m in range(24)])):
        for b, (t0, tw) in enumerate(blks):
            pa = pacc[it % 2]
            paR = P.R("pacc", it % 2)
            e_ = ev[it % 2]
            eR = P.R("ev", it % 2)
            it += 1
            mm_group(P, pa[:, 0:tw], paR, [(wv[:, k, :], hT[:, k, t0:t0 + tw]) for k in range(16)],
                     rd=[wR] + [P.R("h", k, b) for k in range(16)])
            if m < 4 or m >= 20:
                dst = (uT_o[m * 128:(m + 1) * 128, t0:t0 + tw] if m < 4
                       else vT_o[(m - 20) * 128:(m - 19) * 128, t0:t0 + tw])
                P.i("act", "activation", dict(out=e_[:, 0:tw], in_=pa[:, 0:tw],
                                                                         func=AF.Copy), rd=[paR], wr=[eR])
                P.dma(dst, e_[:, 0:tw], rd=[eR], q="actq")
            else:
                isq = m < 16
                gi = 0 if isq else 1
                dst = (qT_o[(m - 4) * 128:(m - 3) * 128, t0:t0 + tw] if isq
                       else kT_o[(m - 16) * 128:(m - 15) * 128, t0:t0 + tw])
                P.i("act", "activation", dict(out=sqh[:, 0:tw], in_=pa[:, 0:tw],
                                                                  func=AF.Square), rd=[paR], wr=[P.R("sqh")])
                P.i("dve", "tensor_scalar", dict(
                    out=qg[:, 0:tw], in0=pa[:, 0:tw], scalar1=qk_sb[:, gi:gi + 1], scalar2=None, op0=ALU.mult),
                     rd=[paR, P.R("qk")], wr=[P.R("qg")])
                mm_group(P, ps_h[:, 0:tw], P.R("ps_h"), [(c["ones"][:], sqh[:, 0:tw])], rd=[P.R("ones"), P.R("sqh")])
                mm_group(P, ps_r[:, 0:tw], P.R("ps_r"), [(rt[:], qg[:, 0:tw])], rd=[P.R("rt"), P.R("qg")])
                P.i("act", "activation", dict(out=lnh[:, 0:tw], in_=ps_h[:, 0:tw], func=AF.Ln, bias=c["eps"][:],
                                                   scale=1.0 / 128), rd=[P.R("ps_h"), P.R("eps")], wr=[P.R("lnh")])
                P.i("act", "activation", dict(out=rsh[:, 0:tw], in_=lnh[:, 0:tw], func=AF.Exp, scale=-0.5),
                     rd=[P.R("lnh")], wr=[P.R("rsh")])
                P.i("dve", "tensor_tensor", dict(out=t1[:, 0:tw], in0=qg[:, 0:tw],
                                                                     in1=cs[:, 0, t0:t0 + tw], op=ALU.mult),
                     rd=[P.R("qg"), P.R("cs")], wr=[P.R("t1")])
                P.i("dve", "tensor_tensor", dict(out=t2[:, 0:tw], in0=ps_r[:, 0:tw],
                                                                     in1=cs[:, 1, t0:t0 + tw], op=ALU.mult),
                     rd=[P.R("ps_r"), P.R("cs")], wr=[P.R("t2")])
                P.i("pool", "tensor_tensor", dict(out=t1[:, 0:tw], in0=t1[:, 0:tw], in1=t2[:, 0:tw], op=ALU.add),
                     rd=[P.R("t1"), P.R("t2")], wr=[P.R("t1")])
                P.i("pool", "tensor_tensor", dict(out=e_[:, 0:tw], in0=t1[:, 0:tw],
                                                                      in1=rsh[:, 0:tw], op=ALU.mult),
                     rd=[P.R("t1"), P.R("rsh")], wr=[eR])
                P.dma(dst, e_[:, 0:tw], rd=[eR], q="actq")
    return P.finish()


def build_attn():
    P = Prog()
    qT_d = P.din("qT", [3, 128, S], BF16)
    kT_d = P.din("kT", [128, S], BF16)
    v_d = P.din("v", [128, 32, 128], BF16)
    uT_d = P.din("uT", [128, S], BF16)
    pw_d = P.din("pw", [128, 128], F32)
    pc_d = P.din("pc", [128, 6], F32)
    ic_d = P.din("ic", [128, S], F32)
    att_o = P.dout("attT", [3, 128, S], BF16)
    pool_o = P.dout("poolT", [128, S], BF16)
    c = consts(P)
    qT = P.sb([128, 3, S], BF16, "qT")
    kT = P.sb([128, S], BF16, "kT")
    v = P.sb([128, 32, 128], BF16, "v")
    pw = P.sb([128, 128], F32, "pw")
    pwb = P.sb([128, 128], BF16, "pwb")
    pc = P.sb([128, 6], F32, "pc")
    for h in range(3):
        P.dma(qT[:, h, :], qT_d[h], wr=[P.R("q", h)])
    P.dma(kT[:], kT_d, wr=[P.R("k")])
    P.dma(v[:], v_d, wr=[P.R("v")])
    P.dma(pw[:], pw_d, wr=[P.R("pw")])
    P.dma(pc[:], pc_d, wr=[P.R("pc")])
    P.i("act", "activation", dict(out=pwb[:], in_=pw[:], func=AF.Copy), rd=[P.R("pw")], wr=[P.R("pwb")])
    W = S + 32
    ub = P.sb([128, S], BF16, "ub")
    u = P.sb([128, W], F32, "u")
    sa = P.sb([128, W], F32, "sa")
    sb_ = P.sb([128, W], F32, "sbb")
    acc = P.sb([128, S], F32, "acc")
    ic = P.sb([128, S], F32, "ic")
    pl = P.sb([128, S], BF16, "pl")
    P.dma(ub[:], uT_d, wr=[P.R("ub")])
    P.dma(ic[:], ic_d, wr=[P.R("ic")])
    for nm, t in (("u", u), ("sa", sa), ("sbb", sb_)):
        P.i("pool", "memset", dict(ap=t[:], constant=0.0), wr=[P.R(nm)])
    P.i("act", "activation", dict(out=u[:, 16:16 + S], in_=ub[:], func=AF.Copy), rd=[P.R("ub")], wr=[P.R("u")])
    P.i("dve", "tensor_tensor", dict(out=sa[:, 1:W], in0=u[:, 0:W - 1], in1=u[:, 1:W], op=ALU.add),
         rd=[P.R("u")], wr=[P.R("sa")])
    P.i("dve", "tensor_scalar", dict(out=acc[:], in0=sa[:, 16:16 + S], scalar1=pc[:, 0:1], scalar2=None,
                                          op0=ALU.mult), rd=[P.R("sa"), P.R("pc")], wr=[P.R("acc")])
    P.i("dve", "tensor_tensor", dict(out=sb_[:, 2:W - 2], in0=sa[:, 1:W - 3], in1=sa[:, 3:W - 1], op=ALU.add),
         rd=[P.R("sa")], wr=[P.R("sbb")])
    P.i("dve", "scalar_tensor_tensor", dict(out=acc[:], in0=sb_[:, 16:16 + S], scalar=pc[:, 1:2], in1=acc[:],
                                                 op0=ALU.mult, op1=ALU.add),
         rd=[P.R("sbb"), P.R("pc"), P.R("acc")], wr=[P.R("acc")])
    P.i("dve", "tensor_tensor", dict(out=sa[:, 4:W - 4], in0=sb_[:, 2:W - 6], in1=sb_[:, 6:W - 2], op=ALU.add),
         rd=[P.R("sbb")], wr=[P.R("sa")])
    P.i("dve", "scalar_tensor_tensor", dict(out=acc[:], in0=sa[:, 16:16 + S], scalar=pc[:, 2:3], in1=acc[:],
                                                 op0=ALU.mult, op1=ALU.add),
         rd=[P.R("sa"), P.R("pc"), P.R("acc")], wr=[P.R("acc")])
    P.i("dve", "tensor_tensor", dict(out=sb_[:, 8:W - 8], in0=sa[:, 4:W - 12], in1=sa[:, 12:W - 4], op=ALU.add),
         rd=[P.R("sa")], wr=[P.R("sbb")])
    P.i("dve", "scalar_tensor_tensor", dict(out=acc[:], in0=sb_[:, 16:16 + S], scalar=pc[:, 3:4], in1=acc[:],
                                                 op0=ALU.mult, op1=ALU.add),
         rd=[P.R("sbb"), P.R("pc"), P.R("acc")], wr=[P.R("acc")])
    P.i("dve", "tensor_tensor", dict(out=acc[:], in0=acc[:], in1=ic[:], op=ALU.mult),
         rd=[P.R("acc"), P.R("ic")], wr=[P.R("acc")])
    P.i("dve", "tensor_tensor", dict(out=pl[:], in0=acc[:], in1=u[:, 16:16 + S], op=ALU.subtract),
         rd=[P.R("acc"), P.R("u")], wr=[P.R("pl")])
    ps_p = P.ps("ps_s0")
    pev = [P.sb([128, 512], BF16, f"pev{i}") for i in range(2)]
    for b in range(8):
        mm_group(P, ps_p[:], P.R("ps_s", 0), [(pwb[:], pl[:, b * 512:(b + 1) * 512])], rd=[P.R("pwb"), P.R("pl")])
        P.i("act", "activation", dict(out=pev[b % 2][:], in_=ps_p[:], func=AF.Identity,
                                                        scale=pc[:, 4:5]),
             rd=[P.R("ps_s", 0), P.R("pc")], wr=[P.R("pev", b % 2)])
        P.dma(pool_o[:, b * 512:(b + 1) * 512], pev[b % 2][:], rd=[P.R("pev", b % 2)])
    ps_s = [ps_p, P.ps("ps_s1"), P.ps("ps_s2")]
    ps_o = [P.ps("ps_o0"), P.ps("ps_o1")]
    ps_m = [P.ps("ps_m0"), P.ps("ps_m1")]
    pT = [P.sb([128, 512], BF16, f"pT{i}") for i in range(4)]
    rinv = [P.sb([128, 512], F32, f"rinv{i}") for i in range(2)]
    aev = [P.sb([128, 512], BF16, f"aev{i}") for i in range(2)]
    scale = 128 ** -0.5
    steps = [(h, qb, kt) for h in range(3) for qb in range(8) for kt in range(32)]
    NS = len(steps)

    def emit_s(i):
        h, qb, kt = steps[i]
        mm_group(P, ps_s[i % 3][:], P.R("ps_s", i % 3),
                 [(kT[:, kt * 128:(kt + 1) * 128], qT[:, h, qb * 512:(qb + 1) * 512])], rd=[P.R("k"), P.R("q", h)])
        P.i("act", "activation", dict(out=pT[i % 4][:], in_=ps_s[i % 3][:], func=AF.Exp, scale=scale),
            rd=[P.R("ps_s", i % 3)], wr=[P.R("pT", i % 4)])

    emit_s(0)
    emit_s(1)
    for i in range(NS):
        if i + 2 < NS:
            emit_s(i + 2)
        h, qb, kt = steps[i]
        ob = (i // 32) % 2
        mm_group(P, ps_o[ob][:], P.R("ps_o", ob), [(v[:, kt, :], pT[i % 4][:])], rd=[P.R("v"), P.R("pT", i % 4)],
                 start=(kt == 0), stop=(kt == 31))
        mm_group(P, ps_m[ob][:], P.R("ps_m", ob), [(c["ones"][:], pT[i % 4][:])], rd=[P.R("ones"), P.R("pT", i % 4)],
                 start=(kt == 0), stop=(kt == 31))
        if kt == 31:
            P.i("dve", "reciprocal", dict(out=rinv[ob][:], in_=ps_m[ob][:]), rd=[P.R("ps_m", ob)], wr=[P.R("rinv", ob)])
            P.i("dve", "tensor_tensor", dict(out=aev[ob][:], in0=ps_o[ob][:], in1=rinv[ob][:], op=ALU.mult),
                rd=[P.R("ps_o", ob), P.R("rinv", ob)], wr=[P.R("aev", ob)])
            P.dma(att_o[h, :, qb * 512:(qb + 1) * 512], aev[ob][:], rd=[P.R("aev", ob)])
    return P.finish()


def build_outffn(kind):
    T = 1026
    P = Prog()
    aT_d = P.din("aT", [D, T], BF16)
    xT_d = P.din("xT", [D, T], F32)
    wo_d = P.din("wo", [D, D], F32)
    g_d = P.din("g", [128, 16], F32)
    wu_d = P.din("wu", [D, 2 * DFF], F32)
    cw_d = P.din("cw", [128, 88, 4], F32)
    wd_d = P.din("wd", [DFF, D], F32)
    if kind == "c":
        oT_d = P.din("oT", [D, T], BF16)
        hn_d = P.din("hn", [128, 16], F32)
    out_o = P.dout("oxT", [D, 1024], F32)
    xm_s = P.nc.dram_tensor("xm_s", [D, T], F32, kind="Internal").ap()
    c = consts(P)
    blks = blocks_of(T)
    big = P.sb([128, 44 * 1024], BF16, "big")
    xT = big[:, 0:16 * T * 2].bitcast(F32).rearrange("p (k t) -> p k t", t=T)
    actT = big[:].rearrange("p (j t) -> p j t", t=1024)
    A = P.sb([128, 16, T], BF16, "A")
    g_sb = P.sb([128, 16], F32, "g_sb")
    cw = P.sb([128, 88, 4], F32, "cw")
    P.dma(g_sb[:], g_d, wr=[P.R("gsb", "A")])
    P.dma(cw[:], cw_d, wr=[P.R("cw")])
    xv = xT_d.rearrange("(k p) t -> p k t", p=128)
    av = aT_d.rearrange("(k p) t -> p k t", p=128)
    for k in range(16):
        P.dma(A[:, k, :], av[:, k, :], wr=[P.R("A", k, b) for b in range(3)])
        P.dma(xT[:, k, :], xv[:, k, :], wr=[P.R("x", k, b) for b in range(3)] + [P.R("alias")])
    if kind == "c":
        O = [P.sb([128, T], BF16, f"O{i}") for i in range(2)]
        hn = P.sb([128, 16], F32, "hn")
        ov = oT_d.rearrange("(k p) t -> p k t", p=128)
        P.dma(hn[:], hn_d, wr=[P.R("hn")])
        for k in range(16):
            P.dma(O[k % 2][:], ov[:, k, :], wr=[P.R("O", k % 2)])
            P.i("dve", "scalar_tensor_tensor", dict(
                out=A[:, k, :], in0=A[:, k, :], scalar=hn[:, k:k + 1], in1=O[k % 2][:], op0=ALU.mult, op1=ALU.mult),
                 rd=[P.R("O", k % 2), P.R("hn")] + [P.R("A", k, b) for b in range(3)],
                 wr=[P.R("A", k, b) for b in range(3)])
    ws = WStream(P, 32)
    pacc = [P.ps(f"pacc{i}") for i in range(6)]
    ps_ss = P.ps("ps_ss")
    xmv = xm_s.rearrange("(k p) t -> p k t", p=128)
    it = 0
    for m, ((wv,), wR) in enumerate(prefetched(ws, [[(wo_d[:, m * 128:(m + 1) * 128], 16, 128)] for m in range(16)])):
        for b, (t0, tw) in enumerate(blks):
            pa = pacc[it % 6]
            paR = P.R("pacc", it % 6)
            it += 1
            mm_group(P, pa[:, 0:tw], paR, [(wv[:, k, :], A[:, k, t0:t0 + tw]) for k in range(16)],
                     rd=[wR] + [P.R("A", k, b) for k in range(16)])
            P.i("dve", "tensor_tensor", dict(
                out=xT[:, m, t0:t0 + tw], in0=pa[:, 0:tw], in1=xT[:, m, t0:t0 + tw], op=ALU.add),
                 rd=[paR, P.R("x", m, b)], wr=[P.R("x", m, b)])
        P.dma(xmv[:, m, :], xT[:, m, :], rd=[P.R("x", m, b) for b in range(3)] + [P.R("alias")], wr=[P.R("xm", m)])
    rmsnorm_fm(P, c, xT, "x", g_sb, A, "A", T, ps_ss, D)
    allx = [P.R("x", k, b) for k in range(16) for b in range(3)]
    P.i("pool", "memset", dict(ap=c["eps"][:], constant=EPS), rd=allx + [P.R("eps")], wr=[P.R("alias"), P.R("eps")])
    raw = [P.sb([128, T], F32, f"raw{i}") for i in range(4)]
    tg = P.sb([128, 1024], F32, "tg")
    tv = P.sb([128, 1024], F32, "tv")
    specs = [[(wu_d[:, j * 128:(j + 1) * 128], 16, 128), (wu_d[:, DFF + j * 128:DFF + (j + 1) * 128], 16, 128)]
             for j in range(44)]
    for j, ((wg, wvv), wR) in enumerate(prefetched(ws, specs)):
        for gv, wv in enumerate((wg, wvv)):
            r_ap = raw[(j % 2) * 2 + gv]
            rR = P.R("raw", (j % 2) * 2 + gv)
            for b, (t0, tw) in enumerate(blks):
                pa = pacc[it % 6]
                paR = P.R("pacc", it % 6)
                it += 1
                mm_group(P, pa[:, 0:tw], paR, [(wv[:, k, :], A[:, k, t0:t0 + tw]) for k in range(16)],
                         rd=[wR] + [P.R("A", k, b) for k in range(16)])
                P.i("act", "activation", dict(
                    out=r_ap[:, t0:t0 + tw], in_=pa[:, 0:tw], func=AF.Copy), rd=[paR], wr=[rR])
            ch = gv * 44 + j
            t_ap = tg if gv == 0 else tv
            tR = P.R("tg") if gv == 0 else P.R("tv")
            eng = "dve"
            P.i(eng, "tensor_scalar", dict(
                out=t_ap[:], in0=r_ap[:, 1:1025], scalar1=cw[:, ch, 1:2], scalar2=cw[:, ch, 3:4],
                op0=ALU.mult, op1=ALU.add), rd=[rR, P.R("cw")], wr=[tR])
            P.i(eng, "scalar_tensor_tensor", dict(
                out=t_ap[:], in0=r_ap[:, 0:1024], scalar=cw[:, ch, 0:1], in1=t_ap[:], op0=ALU.mult, op1=ALU.add),
                 rd=[rR, P.R("cw"), tR], wr=[tR])
            P.i(eng, "scalar_tensor_tensor", dict(
                out=t_ap[:], in0=r_ap[:, 2:1026], scalar=cw[:, ch, 2:3], in1=t_ap[:], op0=ALU.mult, op1=ALU.add),
                 rd=[rR, P.R("cw"), tR], wr=[tR])
        P.i("act", "activation", dict(out=tg[:], in_=tg[:], func=AF.Silu), rd=[P.R("tg")], wr=[P.R("tg")])
        P.i("dve", "tensor_tensor", dict(out=actT[:, j, :], in0=tg[:], in1=tv[:], op=ALU.mult),
             rd=[P.R("tg"), P.R("tv"), P.R("alias")], wr=[P.R("act", j)])
    xo = [raw[i][:, 0:1024] for i in range(2)]
    specs = [[(wd_d[half * 2816:(half + 1) * 2816, m * 128:(m + 1) * 128], 22, 128)]
             for m in range(16) for half in range(2)]
    wit = prefetched(ws, specs)
    for m in range(16):
        P.dma(xo[m % 2], xmv[:, m, 1:1025], rd=[P.R("xm", m)], wr=[P.R("raw", m % 2)], q="actq")
        pas = []
        for half in range(2):
            (wv,), wR = next(wit)
            for b in range(2):
                if half == 0:
                    pas.append((pacc[it % 6], P.R("pacc", it % 6)))
                    it += 1
                pa, paR = pas[b]
                mm_group(P, pa[:], paR, [(wv[:, k, :], actT[:, half * 22 + k, b * 512:(b + 1) * 512]) for k in range(22)],
                         rd=[wR] + [P.R("act", half * 22 + k) for k in range(22)], start=(half == 0), stop=(half == 1))
        for b in range(2):
            pa, paR = pas[b]
            P.i("dve", "tensor_tensor", dict(
                out=xo[m % 2][:, b * 512:(b + 1) * 512], in0=pa[:], in1=xo[m % 2][:, b * 512:(b + 1) * 512],
                op=ALU.add), rd=[paR, P.R("raw", m % 2)], wr=[P.R("raw", m % 2)])
        P.dma(out_o[m * 128:(m + 1) * 128, :], xo[m % 2], rd=[P.R("raw", m % 2)], q="actq")
    return P.finish()


def build_inproj1():
    T = 1024
    P = Prog()
    xT_d = P.din("xT", [D, T], F32)
    g_d = P.din("g", [128, 16], F32)
    w_d = P.din("w", [D, 8224], F32)
    bg_d = P.din("bg", [32, 1], F32)
    o_o = P.dout("pT", [8192, T], BF16)
    gt_o = P.dout("gT", [32, T], F32)
    c = consts(P)
    xT = P.sb([128, 16, T], F32, "xT")
    hT = P.sb([128, 16, T], BF16, "hT")
    g_sb = P.sb([128, 16], F32, "g_sb")
    bg = P.sb([32, 1], F32, "bg")
    blks = blocks_of(T)
    P.dma(g_sb[:], g_d, wr=[P.R("gsb", "h")])
    P.dma(bg[:], bg_d, wr=[P.R("bg")])
    xv = xT_d.rearrange("(k p) t -> p k t", p=128)
    for k in range(16):
        for b, (t0, tw) in enumerate(blks):
            P.dma(xT[:, k, t0:t0 + tw], xv[:, k, t0:t0 + tw], wr=[P.R("x", k, b)])
    ps_ss = P.ps("ps_ss")
    rmsnorm_fm(P, c, xT, "x", g_sb, hT, "h", T, ps_ss, D)
    ws = WStream(P, 16)
    pacc = [P.ps(f"pacc{i}") for i in range(4)]
    ev = [P.sb([128, 512], BF16, f"ev{i}") for i in range(4)]
    gev = P.sb([32, 512], F32, "gev")
    it = 0
    specs = [[(w_d[:, m * 128:m * 128 + (128 if m < 64 else 32)], 16, (128 if m < 64 else 32))] for m in range(65)]
    for m, ((wv,), wR) in enumerate(prefetched(ws, specs)):
        mw = 128 if m < 64 else 32
        for b, (t0, tw) in enumerate(blks):
            pa = pacc[it % 4]
            paR = P.R("pacc", it % 4)
            e_ = ev[it % 4]
            eR = P.R("ev", it % 4)
            it += 1
            mm_group(P, pa[0:mw, 0:tw], paR, [(wv[:, k, :], hT[:, k, t0:t0 + tw]) for k in range(16)],
                     rd=[wR] + [P.R("h", k, b) for k in range(16)])
            if m == 64:
                P.i("act", "activation", dict(out=gev[:, 0:tw], in_=pa[0:32, 0:tw],
                                                                  func=AF.Identity, bias=bg[:], scale=1.0),
                     rd=[paR, P.R("bg")], wr=[P.R("gev")])
                P.dma(gt_o[:, t0:t0 + tw], gev[:, 0:tw], rd=[P.R("gev")], q="actq")
            else:
                if m < 16:
                    kw = dict(out=e_[:, 0:tw], in_=pa[:, 0:tw], func=AF.Identity, scale=1.0 / 16.0)
                elif m < 48:
                    kw = dict(out=e_[:, 0:tw], in_=pa[:, 0:tw], func=AF.Copy)
                else:
                    kw = dict(out=e_[:, 0:tw], in_=pa[:, 0:tw], func=AF.Sigmoid)
                P.i("act", "activation", kw, rd=[paR], wr=[eR])
                P.dma(o_o[m * 128:(m + 1) * 128, t0:t0 + tw], e_[:, 0:tw], rd=[eR], q="actq")
    return P.finish()


def build_mlstm():
    NP = 2
    NCH = 32
    P = Prog()
    qT_d = P.din("qT", [NP, 2, 128, S], BF16)
    kT_d = P.din("kT", [NP, 2, 128, S], BF16)
    k_d = P.din("k", [NP, 128, NCH, 256], BF16)
    v_d = P.din("v", [NP, 128, NCH, 256], BF16)
    gt_d = P.din("gt", [NP, 128, 4, NCH], F32)
    tri_d = P.din("tri", [128, 2, 128], F32)
    hn_o = P.dout("hn", [NP, 128, NCH, 256], BF16)
    c = consts(P)
    tri = P.sb([128, 2, 128], F32, "tri")
    onesf = P.sb([128, 128], F32, "onesf")
    one1 = P.sb([128, 1], F32, "one1")
    P.dma(tri[:], tri_d, wr=[P.R("tri")])
    P.i("pool", "memset", dict(ap=onesf[:], constant=1.0), wr=[P.R("onesf")])
    P.i("pool", "memset", dict(ap=one1[:], constant=1.0), wr=[P.R("one1")])
    qT = P.sb([128, 2, S], BF16, "qT")
    kT = P.sb([128, 2, S], BF16, "kT")
    kk = P.sb([128, NCH, 256], BF16, "kk")
    vx = P.sb([128, NCH, 257], BF16, "vx")
    gt = P.sb([128, 4, NCH], F32, "gt")
    lf = P.sb([128, 2, NCH], F32, "lf")
    bc = P.sb([128, 2, NCH], F32, "bc")
    tot = P.sb([128, 2, NCH], F32, "tot")
    av = P.sb([128, 2, NCH], F32, "av")
    bv = P.sb([128, 2, NCH], F32, "bv")
    b2 = P.sb([128, 2, NCH], F32, "b2")
    dc = P.sb([128, 2, NCH], F32, "dc")
    tmp = P.sb([128, 2, NCH], F32, "tmp")
    hacc = P.sb([128, NCH, 256], F32, "hacc")
    ssq = P.sb([128, NCH], F32, "ssq")
    junk = P.sb([128, 256], F32, "junk")
    psS = [P.ps("psS0"), P.ps("psS1")]
    ps_g = psS[0]
    psN = [P.ps("psN0"), P.ps("psN1")]
    psC = [[P.ps("psC00"), P.ps("psC01")], [P.ps("psC10"), P.ps("psC11")]]
    Cst = [P.sb([128, 2, 257], F32, f"Cst{d}") for d in range(2)]
    Cbf = [P.sb([128, 2, 257], BF16, f"Cbf{d}") for d in range(2)]
    Sm = [P.sb([128, 128], BF16, f"Sm{d}") for d in range(2)]
    k2 = [P.sb([128, 256], BF16, f"k2{d}") for d in range(2)]
    dn = [P.sb([128, 4], F32, f"dn{d}") for d in range(2)]
    hev = [P.sb([128, 256], BF16, f"hev{i}") for i in range(2)]
    for p in range(NP):
        for h in range(2):
            P.dma(qT[:, h, :], qT_d[p, h], wr=[P.R("qT")])
            P.dma(kT[:, h, :], kT_d[p, h], wr=[P.R("kT")])
        P.dma(kk[:], k_d[p], wr=[P.R("kk")])
        P.dma(vx[:, :, 0:256], v_d[p], wr=[P.R("vx")])
        P.i("pool", "memset", dict(ap=vx[:, :, 256:257], constant=1.0), wr=[P.R("vx")], rd=[])
        P.dma(gt[:], gt_d[p], wr=[P.R("gt")])
        for d in range(2):
            P.i("act", "activation", dict(out=lf[:, d, :], in_=gt[:, 2 * d + 1, :], func=AF.Exp,
                                                            scale=-1.0), rd=[P.R("gt")], wr=[P.R("lf")])
        P.i("act", "activation", dict(out=lf[:], in_=lf[:], func=AF.Ln, bias=one1[:], scale=1.0),
             rd=[P.R("lf"), P.R("one1")], wr=[P.R("lf")])
        P.i("dve", "tensor_scalar", dict(out=lf[:], in0=lf[:], scalar1=-1.0, scalar2=None, op0=ALU.mult),
             rd=[P.R("lf")], wr=[P.R("lf")])
        for d in range(2):
            mm_group(P, ps_g[:, d * NCH:(d + 1) * NCH], P.R("psS", 0), [(tri[:, d, :], lf[:, d, :])],
                     rd=[P.R("tri"), P.R("lf")])
        mm_group(P, ps_g[:, 2 * NCH:4 * NCH], P.R("psS", 0), [(onesf[:], lf[:].rearrange("p d c -> p (d c)"))],
                 rd=[P.R("onesf"), P.R("lf")])
        P.i("dve", "tensor_copy", dict(out=bc[:].rearrange("p d c -> p (d c)"), in_=ps_g[:, 0:2 * NCH]),
             rd=[P.R("psS", 0)], wr=[P.R("bc")])
        P.i("dve", "tensor_copy", dict(out=tot[:].rearrange("p d c -> p (d c)"), in_=ps_g[:, 2 * NCH:4 * NCH]),
             rd=[P.R("psS", 0)], wr=[P.R("tot")])
        P.i("act", "activation", dict(out=av[:], in_=bc[:], func=AF.Exp), rd=[P.R("bc")], wr=[P.R("av")])
        P.i("act", "activation", dict(out=dc[:], in_=tot[:], func=AF.Exp), rd=[P.R("tot")], wr=[P.R("dc")])
        for d in range(2):
            P.i("dve", "tensor_tensor", dict(out=tmp[:, d, :], in0=gt[:, 2 * d, :], in1=bc[:, d, :],
                                                               op=ALU.subtract),
                 rd=[P.R("gt"), P.R("bc")], wr=[P.R("tmp")])
        P.i("act", "activation", dict(out=bv[:], in_=tmp[:], func=AF.Exp), rd=[P.R("tmp")], wr=[P.R("bv")])
        P.i("dve", "tensor_tensor", dict(out=tmp[:], in0=tmp[:], in1=tot[:], op=ALU.add),
             rd=[P.R("tmp"), P.R("tot"), P.R("bv")], wr=[P.R("tmp")])
        P.i("act", "activation", dict(out=b2[:], in_=tmp[:], func=AF.Exp), rd=[P.R("tmp")], wr=[P.R("b2")])
        gR = [P.R("av"), P.R("bv"), P.R("b2"), P.R("dc")]
        for d in range(2):
            P.i("pool", "memset", dict(ap=Cst[d][:], constant=0.0), wr=[P.R("Cst", d)])
            P.i("pool", "memset", dict(ap=Cbf[d][:], constant=0.0), wr=[P.R("Cbf", d)])
        for step in range(NCH):
            for d in range(2):
                ch = step if d == 0 else NCH - 1 - step
                cs_ = slice(ch * 128, (ch + 1) * 128)
                S_ap, SR = psS[d], P.R("psS", d)
                N_ap, NR = psN[d], P.R("psN", d)
                mm_group(P, S_ap[:, 0:128], SR, [(kT[:, h, cs_], qT[:, h, cs_]) for h in range(2)],
                         rd=[P.R("kT"), P.R("qT")])
                P.i("dve", "scalar_tensor_tensor", dict(
                    out=Sm[d][:], in0=S_ap[:, 0:128], scalar=bv[:, d, ch:ch + 1], in1=tri[:, d, :],
                    op0=ALU.mult, op1=ALU.mult), rd=[SR, P.R("tri")] + gR, wr=[P.R("Sm", d)])
                P.i("pool", "tensor_scalar", dict(
                    out=k2[d][:], in0=kk[:, ch, :], scalar1=b2[:, d, ch:ch + 1], scalar2=None, op0=ALU.mult),
                     rd=[P.R("kk")] + gR, wr=[P.R("k2", d)])
                for h in range(2):
                    mm_group(P, psC[d][h][:, 0:257], P.R("psC", d, h), [(k2[d][:, h * 128:(h + 1) * 128], vx[:, ch, :])],
                             rd=[P.R("k2", d), P.R("vx")])
            for d in range(2):
                ch = step if d == 0 else NCH - 1 - step
                cs_ = slice(ch * 128, (ch + 1) * 128)
                S_ap, SR = psS[d], P.R("psS", d)
                N_ap, NR = psN[d], P.R("psN", d)
                mm_group(P, N_ap[:, 0:257], NR,
                         [(Sm[d][:], vx[:, ch, :])] + [(qT[:, h, cs_], Cbf[d][:, h, :]) for h in range(2)],
                         rd=[P.R("Sm", d), P.R("vx"), P.R("qT"), P.R("Cbf", d)])
                for h in range(2):
                    P.i("dve", "scalar_tensor_tensor", dict(
                        out=Cst[d][:, h, :], in0=Cst[d][:, h, :], scalar=dc[:, d, ch:ch + 1], in1=psC[d][h][:, 0:257],
                        op0=ALU.mult, op1=ALU.add), rd=[P.R("psC", d, h), P.R("Cst", d)] + gR, wr=[P.R("Cst", d)])
                P.i("act", "activation", dict(out=Cbf[d][:], in_=Cst[d][:], func=AF.Copy),
                     rd=[P.R("Cst", d)], wr=[P.R("Cbf", d)])
                P.i("act", "activation", dict(out=dn[d][:, 0:1], in_=N_ap[:, 256:257], func=AF.Abs,
                                              scale=av[:, d, ch:ch + 1]), rd=[NR] + gR, wr=[P.R("dn", d)])
                P.i("dve", "tensor_scalar", dict(
                    out=dn[d][:, 1:2], in0=dn[d][:, 0:1], scalar1=1.0, scalar2=None, op0=ALU.max),
                     rd=[P.R("dn", d)], wr=[P.R("dn", d)])
                P.i("dve", "reciprocal", dict(out=dn[d][:, 2:3], in_=dn[d][:, 1:2]),
                     rd=[P.R("dn", d)], wr=[P.R("dn", d)])
                P.i("dve", "tensor_tensor", dict(
                    out=dn[d][:, 3:4], in0=dn[d][:, 2:3], in1=av[:, d, ch:ch + 1], op=ALU.mult),
                     rd=[P.R("dn", d)] + gR, wr=[P.R("dn", d)])
                if step < NCH // 2:
                    P.i("act", "activation", dict(
                        out=hacc[:, ch, :], in_=N_ap[:, 0:256], func=AF.Identity, scale=dn[d][:, 3:4]),
                         rd=[NR, P.R("dn", d)], wr=[P.R("hacc", ch)])
                else:
                    P.i("dve", "scalar_tensor_tensor", dict(
                        out=hacc[:, ch, :], in0=N_ap[:, 0:256], scalar=dn[d][:, 3:4], in1=hacc[:, ch, :],
                        op0=ALU.mult, op1=ALU.add), rd=[NR, P.R("dn", d), P.R("hacc", ch)], wr=[P.R("hacc", ch)])
        for ch in range(NCH):
            P.i("act", "activation", dict(out=junk[:], in_=hacc[:, ch, :], func=AF.Square,
                                                              accum_out=ssq[:, ch:ch + 1]),
                 rd=[P.R("hacc", ch)], wr=[P.R("junk"), P.R("ssq")])
        P.i("act", "activation", dict(out=ssq[:], in_=ssq[:], func=AF.Ln, bias=c["eps"][:], scale=1.0 / 256),
             rd=[P.R("ssq"), P.R("eps")], wr=[P.R("ssq")])
        P.i("act", "activation", dict(out=ssq[:], in_=ssq[:], func=AF.Exp, scale=-0.5),
             rd=[P.R("ssq")], wr=[P.R("ssq")])
        for ch in range(NCH):
            P.i("dve", "tensor_scalar", dict(
                out=hev[ch % 2][:], in0=hacc[:, ch, :], scalar1=ssq[:, ch:ch + 1], scalar2=None, op0=ALU.mult),
                 rd=[P.R("hacc", ch), P.R("ssq")], wr=[P.R("hev", ch % 2)])
            P.dma(hn_o[p, :, ch, :], hev[ch % 2][:], rd=[P.R("hev", ch % 2)])
    return P.finish()


N_LAUNCH = [0]


def run(nc, in_maps):
    N_LAUNCH[0] += 1
    res = run_bass_kernel_spmd(nc, in_maps, core_ids=list(range(NCORES)))
    return res.results


def pc128(v, kc):
    return np.ascontiguousarray(np.asarray(v, np.float32).reshape(kc, 128).T)


def rope_tables():
    rows = S // 64
    row = np.repeat(np.arange(rows), 64).astype(np.float32)
    col = np.tile(np.arange(64), rows).astype(np.float32)
    inv = (1.0 / (np.float32(10000.0) ** (np.arange(0, 64, 2, dtype=np.float32) / np.float32(64)))).astype(np.float32)
    a_r = row[:, None] * inv[None, :]
    a_c = col[:, None] * inv[None, :]
    ang = np.concatenate([a_r, a_r, a_c, a_c], axis=-1)
    return np.cos(ang).astype(np.float32), np.sin(ang).astype(np.float32)


def rot_matrix_T():
    R = np.zeros((128, 128), np.float32)
    for base in (0, 64):
        for j in range(32):
            R[base + j, base + j + 32] = -1.0
            R[base + j + 32, base + j] = 1.0
    return np.ascontiguousarray(R.T).astype(NPBF)


def halo_T(a_tok, b, q):
    F_ = a_tok.shape[-1]
    out = np.zeros((F_, 1026), a_tok.dtype)
    lo, hi = q * 1024 - 1, q * 1024 + 1025
    l2, h2 = max(lo, 0), min(hi, S)
    out[:, l2 - lo:h2 - lo] = a_tok[b, l2:h2].T
    return out


def ffn_inputs(layer, norm_ffn, w_up, conv_w, conv_b, w_down):
    cw = np.zeros((128, 88, 4), np.float32)
    cwl = np.asarray(conv_w[layer], np.float32)
    cbl = np.asarray(conv_b[layer], np.float32)
    for i in range(3):
        cw[:, :, i] = cwl[i].reshape(88, 128).T
    cw[:, :, 3] = cbl.reshape(88, 128).T
    return {"g": pc128(norm_ffn[layer], 16), "wu": np.ascontiguousarray(w_up[layer], np.float32), "cw": cw,
            "wd": np.ascontiguousarray(w_down[layer], np.float32)}


def kernel(x, norm_mix, norm_ffn, w_in_ab, pool_w, pool_scale, q_norm, k_norm, w_out_ab,
           w_in_c, b_gate_c, h_norm_c, w_out_c, w_up, conv_w, conv_b, w_down):
    x = np.asarray(x, np.float32)
    B = x.shape[0]
    cores = [(c // 4, c % 4) for c in range(NCORES)]
    cos, sin = rope_tables()
    nc = build_inproj0()
    ims = []
    for (b, q) in cores:
        sl = slice(q * 1024, (q + 1) * 1024)
        cs = np.stack([cos[sl].T, sin[sl].T], axis=1)
        ims.append({"xT": np.ascontiguousarray(x[b, sl].T), "g": pc128(norm_mix[0], 16),
                    "w": np.ascontiguousarray(w_in_ab[0], np.float32),
                    "qkn": np.ascontiguousarray(np.stack([q_norm[0], k_norm[0]], axis=1), np.float32),
                    "cs": np.ascontiguousarray(cs, np.float32), "rt": rot_matrix_T()})
    r1 = run(nc, ims)
    qT = np.zeros((B, 1536, S), NPBF)
    kT = np.zeros((B, 512, S), NPBF)
    vT = np.zeros((B, 512, S), NPBF)
    uT = np.zeros((B, 512, S), NPBF)
    for ci, (b, q) in enumerate(cores):
        sl = slice(q * 1024, (q + 1) * 1024)
        qT[b][:, sl] = r1[ci]["qT"]
        kT[b][:, sl] = r1[ci]["kT"]
        vT[b][:, sl] = r1[ci]["vT"]
        uT[b][:, sl] = r1[ci]["uT"]
    nc = build_attn()
    ims = []
    t = np.arange(S)
    for (b, g) in cores:
        w = (2, 4, 8, 16)[g]
        lo = np.clip(t - w // 2, 0, S)
        hi = np.clip(t + w // 2, 0, S)
        ic = np.broadcast_to((1.0 / (hi - lo).astype(np.float32))[None, :], (128, S))
        pc = np.zeros((128, 6), np.float32)
        pc[:, g] = 1.0
        pc[:, 4] = np.asarray(pool_scale[0], np.float32)[g * 128:(g + 1) * 128]
        vg = vT[b][g * 128:(g + 1) * 128]
        v_tm = np.ascontiguousarray(vg.T.reshape(32, 128, 128).transpose(1, 0, 2))
        ims.append({"qT": np.ascontiguousarray(qT[b][g * 384:(g + 1) * 384].reshape(3, 128, S)),
                    "kT": np.ascontiguousarray(kT[b][g * 128:(g + 1) * 128]), "v": v_tm,
                    "uT": np.ascontiguousarray(uT[b][g * 128:(g + 1) * 128]),
                    "pw": np.ascontiguousarray(pool_w[0][g], np.float32), "pc": pc,
                    "ic": np.ascontiguousarray(ic, np.float32)})
    r2 = run(nc, ims)
    cat = np.zeros((B, S, D), NPBF)
    for ci, (b, g) in enumerate(cores):
        cat[b][:, g * 128:(g + 1) * 128] = r2[ci]["poolT"].T
        cat[b][:, 512 + g * 384:512 + (g + 1) * 384] = r2[ci]["attT"].reshape(384, S).T
    nc = build_outffn("ab")
    f0 = ffn_inputs(0, norm_ffn, w_up, conv_w, conv_b, w_down)
    ims = []
    for (b, q) in cores:
        d = {"aT": halo_T(cat, b, q), "xT": halo_T(x, b, q), "wo": np.ascontiguousarray(w_out_ab[0], np.float32)}
        d.update(f0)
        ims.append(d)
    r3 = run(nc, ims)
    x1 = np.zeros((B, S, D), np.float32)
    for ci, (b, q) in enumerate(cores):
        x1[b, q * 1024:(q + 1) * 1024] = r3[ci]["oxT"].T
    nc = build_inproj1()
    ims = []
    for (b, q) in cores:
        sl = slice(q * 1024, (q + 1) * 1024)
        ims.append({"xT": np.ascontiguousarray(x1[b, sl].T), "g": pc128(norm_mix[1], 16),
                    "w": np.ascontiguousarray(w_in_c[0], np.float32),
                    "bg": np.ascontiguousarray(np.asarray(b_gate_c[0], np.float32).reshape(32, 1))})
    r4 = run(nc, ims)
    pT = np.zeros((B, 8192, S), NPBF)
    gT = np.zeros((B, 32, S), np.float32)
    for ci, (b, q) in enumerate(cores):
        sl = slice(q * 1024, (q + 1) * 1024)
        pT[b][:, sl] = r4[ci]["pT"]
        gT[b][:, sl] = r4[ci]["gT"]
    nc = build_mlstm()
    tri = np.zeros((128, 2, 128), np.float32)
    ii = np.arange(128)
    tri[:, 0, :] = (ii[:, None] <= ii[None, :])
    tri[:, 1, :] = (ii[:, None] >= ii[None, :])
    ims = []
    pairs = [(p // 8, p % 8) for p in range(16)]
    for ci in range(NCORES):
        d = {k: [] for k in ("qT", "kT", "k", "v", "gt")}
        for (b, h) in pairs[2 * ci:2 * ci + 2]:
            qh = pT[b][h * 256:(h + 1) * 256]
            kh = pT[b][2048 + h * 256:2048 + (h + 1) * 256]
            vh = pT[b][4096 + h * 256:4096 + (h + 1) * 256]
            d["qT"].append(qh.reshape(2, 128, S))
            d["kT"].append(kh.reshape(2, 128, S))
            d["k"].append(kh.T.reshape(32, 128, 256).transpose(1, 0, 2))
            d["v"].append(vh.T.reshape(32, 128, 256).transpose(1, 0, 2))
            gg = gT[b].reshape(4, 8, S)[:, h]
            d["gt"].append(gg.reshape(4, 32, 128).transpose(2, 0, 1))
        im = {k: np.ascontiguousarray(np.stack(v_)) for k, v_ in d.items()}
        im["tri"] = tri
        ims.append(im)
    r5 = run(nc, ims)
    hn = np.zeros((B, S, D), NPBF)
    for ci in range(NCORES):
        for j, (b, h) in enumerate(pairs[2 * ci:2 * ci + 2]):
            hh = r5[ci]["hn"][j]
            hn[b][:, h * 256:(h + 1) * 256] = hh.transpose(1, 0, 2).reshape(S, 256)
    og = np.ascontiguousarray(pT[:, 6144:8192].transpose(0, 2, 1))
    nc = build_outffn("c")
    f1 = ffn_inputs(1, norm_ffn, w_up, conv_w, conv_b, w_down)
    ims = []
    for (b, q) in cores:
        d = {"aT": halo_T(hn, b, q), "oT": halo_T(og, b, q), "xT": halo_T(x1, b, q),
             "hn": pc128(h_norm_c[0], 16), "wo": np.ascontiguousarray(w_out_c[0], np.float32)}
        d.update(f1)
        ims.append(d)
    r6 = run(nc, ims)
    out = np.zeros((B, S, D), np.float32)
    for ci, (b, q) in enumerate(cores):
        out[b, q * 1024:(q + 1) * 1024] = r6[ci]["oxT"].T
    return out
```

```python
import numpy as np
import ml_dtypes
from contextlib import ExitStack
import concourse.bass as bass
import concourse.mybir as mybir
from concourse.bass_utils import run_bass_kernel_spmd

F32, BF16 = mybir.dt.float32, mybir.dt.bfloat16
AF = mybir.ActivationFunctionType
ALU = mybir.AluOpType
NPBF = ml_dtypes.bfloat16
NDMA = 12
D = 2048
S = 4096
DFF = 5632
EPS = 1e-6
NCORES = 8


PSUM_KEYS = ("pacc", "ps_ss", "ps_h", "ps_r", "ps_p", "ps_s", "ps_o", "ps_m", "psS", "psN", "psC")


class Res:
    __slots__ = ("w", "rd", "excl")

    def __init__(self, excl=False):
        self.w = None
        self.rd = []
        self.excl = excl


class Prog:
    def __init__(self):
        self.nc = bass.Bass("TRN2", target_bir_lowering=False)
        self.ops = []
        self.st = ExitStack()
        self.res = {}
        self.nm = 0

    def R(self, *key):
        r = self.res.get(key)
        if r is None:
            r = self.res[key] = Res(key[0] in PSUM_KEYS)
        return r

    def sb(self, shape, dt, name=None):
        self.nm += 1
        return self.st.enter_context(self.nc.sbuf_tensor("S_" + (name or f"sb{self.nm}"), list(shape), dt))

    def ps(self, name=None):
        self.nm += 1
        return self.st.enter_context(self.nc.psum_tensor("P_" + (name or f"ps{self.nm}"), [128, 512], F32))

    def din(self, name, shape, dt):
        return self.nc.dram_tensor(name, list(shape), dt, kind="ExternalInput").ap()

    def dout(self, name, shape, dt):
        return self.nc.dram_tensor(name, list(shape), dt, kind="ExternalOutput").ap()

    def op(self, eng, fn, rd=(), wr=()):
        i = len(self.ops)
        deps = set()
        wr = list(wr) + [r for r in rd if r.excl]
        for r in rd:
            if r.w is not None:
                deps.add(r.w)
        for r in wr:
            if r.w is not None:
                deps.add(r.w)
            deps.update(r.rd)
        for r in rd:
            r.rd.append(i)
        for r in wr:
            r.w = i
            r.rd = []
        deps.discard(i)
        self.ops.append((eng, fn, deps))
        return i

    def i(self, eng, meth, kw, rd=(), wr=()):
        return self.op(eng, lambda e: getattr(e, meth)(**kw), rd, wr)

    def dma(self, out, in_, rd=(), wr=(), q="sp"):
        return self.op(q, lambda e: e.dma_start(out=out, in_=in_), rd, wr)

    def finish(self):
        nc, ops, st = self.nc, self.ops, self.st
        n = len(ops)
        signal = [False] * n
        for (_, _, deps) in ops:
            for d in deps:
                signal[d] = True
        engs = ["pe", "act", "dve", "pool"]
        DMAQ = {"sp": "sp", "actq": "act"}
        tok = [None] * n
        cnt = {e: 0 for e in engs}
        ndma = 0
        idx = {e: [] for e in engs + ["sp"]}
        for i, (eng, _, _) in enumerate(ops):
            idx[DMAQ.get(eng, eng)].append(i)
            if eng in DMAQ:
                tok[i] = (("d", ndma % NDMA), 16 * (ndma // NDMA + 1))
                ndma += 1
            elif signal[i]:
                cnt[eng] += 1
                tok[i] = (eng, cnt[eng])
        sems = {e: st.enter_context(nc.semaphore("s_" + e)) for e in engs}
        for k in range(NDMA):
            sems[("d", k)] = st.enter_context(nc.semaphore(f"s_d{k}"))
        block = st.enter_context(nc.Block())

        def emit(engname, e):
            known = {}
            for i in idx[engname]:
                oeng, fn, deps = ops[i]
                isdma = oeng in DMAQ
                need = {}
                for d in deps:
                    if engname == "pe" and ops[d][0] == "pe":
                        continue
                    s, v = tok[d]
                    if need.get(s, 0) < v:
                        need[s] = v
                if isdma:
                    s, v = tok[i]
                    if v > 16:
                        need[s] = max(need.get(s, 0), v - 16)
                for s, v in need.items():
                    if known.get(s, 0) < v:
                        e.wait_ge(sems[s], v)
                        known[s] = v
                ins = fn(e)
                if tok[i] is not None:
                    ins.then_inc(sems[tok[i][0]], 16 if isdma else 1)
            if engname == "sp":
                for k in range(min(NDMA, ndma)):
                    tot = 16 * ((ndma - 1 - k) // NDMA + 1)
                    if known.get(("d", k), 0) < tot:
                        e.wait_ge(sems[("d", k)], tot)

        block.sync(lambda e: emit("sp", e))
        block.tensor(lambda e: emit("pe", e))
        block.scalar(lambda e: emit("act", e))
        block.vector(lambda e: emit("dve", e))
        block.gpsimd(lambda e: emit("pool", e))
        st.close()
        return nc


def mm_group(P, ps_ap, psR, pairs, rd, start=True, stop=True):
    pairs = list(pairs)

    def fn(e):
        n = len(pairs)
        ins = None
        for k, (l, r) in enumerate(pairs):
            ins = e.matmul(ps_ap, lhsT=l, rhs=r, start=(start and k == 0), stop=(stop and k == n - 1))
        return ins

    return P.op("pe", fn, rd=rd, wr=[psR])


class WStream:
    def __init__(self, P, slots):
        self.P = P
        self.slots = slots
        self.stage = [P.sb([128, slots * 128], F32, f"wstage{i}") for i in range(2)]
        self.wb = [P.sb([128, slots * 128], BF16, f"wbf{i}") for i in range(2)]
        self.k = 0

    def load(self, pieces, scale_aps=None):
        P = self.P
        i = self.k % 2
        self.k += 1
        stage = self.stage[i]
        off = 0
        views = []
        parts = []
        for (ap, kc, m) in pieces:
            if len(pieces) == 1 and kc % 2 == 0:
                hk = kc // 2
                subs = [(ap[0:hk * 128, :], hk), (ap[hk * 128:kc * 128, :], hk)]
            else:
                subs = [(ap, kc)]
            o2 = off
            for (sap, skc) in subs:
                dst = stage[:, o2:o2 + skc * m].rearrange("p (k m) -> p k m", m=m)
                src = sap.rearrange("(k p) m -> p k m", p=128)
                sR = P.R("wstage", i, len(parts))
                P.dma(dst, src, wr=[sR])
                parts.append((o2, o2 + skc * m, sR))
                o2 += skc * m
            views.append(self.wb[i][:, off:off + kc * m].rearrange("p (k m) -> p k m", m=m))
            off += kc * m
        rs = []
        for j, (a, b, sR) in enumerate(parts):
            bR = P.R("wbf", i, j)
            src_ap = stage[:, a:b]
            dst_ap = self.wb[i][:, a:b]
            if j % 2 == 0:
                P.i("act", "activation", dict(out=dst_ap, in_=src_ap, func=AF.Copy), rd=[sR], wr=[bR])
            else:
                P.i("pool", "tensor_copy", dict(out=dst_ap, in_=src_ap), rd=[sR], wr=[bR])
            rs.append(bR)
        return views, rs


def prefetched(ws, specs):
    nxt = ws.load(specs[0])
    for i in range(len(specs)):
        cur = nxt
        if i + 1 < len(specs):
            nxt = ws.load(specs[i + 1])
        yield cur


def consts(P):
    c = {}
    c["ones"] = P.sb([128, 128], BF16, "ones")
    c["eps"] = P.sb([128, 1], F32, "epsc")
    P.i("pool", "memset", dict(ap=c["ones"][:], constant=1.0), wr=[P.R("ones")])
    P.i("pool", "memset", dict(ap=c["eps"][:], constant=EPS), wr=[P.R("eps")])
    return c


def blocks_of(T):
    if T % 512 == 0:
        return [(i * 512, 512) for i in range(T // 512)]
    assert T % 3 == 0
    w = T // 3
    return [(i * w, w) for i in range(3)]


def rmsnorm_fm(P, c, xT, xkey, g_sb, hT, hkey, T, ps_ss, dim):
    KC = dim // 128
    blks = blocks_of(T)
    rstd = P.sb([128, T], F32, "rstd_" + hkey)
    lnv = P.sb([128, 512], F32, "lnv_" + hkey)
    for b, (t0, tw) in enumerate(blks):
        for k in range(KC):
            P.i("act", "activation", dict(out=hT[:, k, t0:t0 + tw], in_=xT[:, k, t0:t0 + tw],
                                                            func=AF.Square),
                 rd=[P.R(xkey, k, b)], wr=[P.R(hkey, k, b)])
        mm_group(P, ps_ss[:, 0:tw], P.R("ps_ss"),
                 [(c["ones"][:], hT[:, k, t0:t0 + tw]) for k in range(KC)],
                 rd=[P.R("ones")] + [P.R(hkey, k, b) for k in range(KC)])
        P.i("act", "activation", dict(out=lnv[:, 0:tw], in_=ps_ss[:, 0:tw], func=AF.Ln, bias=c["eps"][:],
                                           scale=1.0 / dim),
             rd=[P.R("ps_ss"), P.R("eps")], wr=[P.R("lnv", hkey)])
        P.i("act", "activation", dict(out=rstd[:, t0:t0 + tw], in_=lnv[:, 0:tw], func=AF.Exp, scale=-0.5),
             rd=[P.R("lnv", hkey)], wr=[P.R("rstd", hkey, b)])
        for k in range(KC):
            P.i("dve", "scalar_tensor_tensor", dict(
                out=hT[:, k, t0:t0 + tw], in0=xT[:, k, t0:t0 + tw], scalar=g_sb[:, k:k + 1],
                in1=rstd[:, t0:t0 + tw], op0=ALU.mult, op1=ALU.mult),
                 rd=[P.R(xkey, k, b), P.R("rstd", hkey, b), P.R("gsb", hkey)], wr=[P.R(hkey, k, b)])


def build_inproj0():
    T = 1024
    P = Prog()
    xT_d = P.din("xT", [D, T], F32)
    g_d = P.din("g", [128, 16], F32)
    w_d = P.din("w", [D, 3072], F32)
    qk_d = P.din("qkn", [128, 2], F32)
    cs_d = P.din("cs", [128, 2, T], F32)
    rt_d = P.din("rt", [128, 128], BF16)
    qT_o = P.dout("qT", [1536, T], BF16)
    kT_o = P.dout("kT", [512, T], BF16)
    vT_o = P.dout("vT", [512, T], BF16)
    uT_o = P.dout("uT", [512, T], BF16)
    c = consts(P)
    xT = P.sb([128, 16, T], F32, "xT")
    hT = P.sb([128, 16, T], BF16, "hT")
    g_sb = P.sb([128, 16], F32, "g_sb")
    qk_sb = P.sb([128, 2], F32, "qk_sb")
    cs = P.sb([128, 2, T], F32, "cs")
    rt = P.sb([128, 128], BF16, "rt")
    blks = blocks_of(T)
    P.dma(g_sb[:], g_d, wr=[P.R("gsb", "h")])
    P.dma(qk_sb[:], qk_d, wr=[P.R("qk")])
    P.dma(cs[:], cs_d, wr=[P.R("cs")])
    P.dma(rt[:], rt_d, wr=[P.R("rt")])
    xv = xT_d.rearrange("(k p) t -> p k t", p=128)
    for k in range(16):
        for b, (t0, tw) in enumerate(blks):
            P.dma(xT[:, k, t0:t0 + tw], xv[:, k, t0:t0 + tw], wr=[P.R("x", k, b)])
    ps_ss = P.ps("ps_ss")
    rmsnorm_fm(P, c, xT, "x", g_sb, hT, "h", T, ps_ss, D)
    ws = WStream(P, 16)
    pacc = [P.ps("pacc0"), P.ps("pacc1")]
    ps_h = P.ps("ps_h")
    ps_r = P.ps("ps_r")
    ev = [P.sb([128, 512], BF16, f"ev{i}") for i in range(2)]
    sqh = P.sb([128, 512], BF16, "sqh")
    qg = P.sb([128, 512], BF16, "qg")
    lnh = P.sb([128, 512], F32, "lnh")
    rsh = P.sb([128, 512], F32, "rsh")
    t1 = P.sb([128, 512], F32, "t1")
    t2 = P.sb([128, 512], F32, "t2")
    it = 0
    for m, ((wv,), wR) in enumerate(prefetched(ws, [[(w_d[:, m * 128:(m + 1) * 128], 16, 128)] for m in range(24)])):
        for b, (t0, tw) in enumerate(blks):
            pa = pacc[it % 2]
            paR = P.R("pacc", it % 2)
            e_ = ev[it % 2]
            eR = P.R("ev", it % 2)
            it += 1
            mm_group(P, pa[:, 0:tw], paR, [(wv[:, k, :], hT[:, k, t0:t0 + tw]) for k in range(16)],
                     rd=wR + [P.R("h", k, b) for k in range(16)])
            if m < 4 or m >= 20:
                dst = (uT_o[m * 128:(m + 1) * 128, t0:t0 + tw] if m < 4
                       else vT_o[(m - 20) * 128:(m - 19) * 128, t0:t0 + tw])
                P.i("act", "activation", dict(out=e_[:, 0:tw], in_=pa[:, 0:tw],
                                                                         func=AF.Copy), rd=[paR], wr=[eR])
                P.dma(dst, e_[:, 0:tw], rd=[eR], q="actq")
            else:
                isq = m < 16
                gi = 0 if isq else 1
                dst = (qT_o[(m - 4) * 128:(m - 3) * 128, t0:t0 + tw] if isq
                       else kT_o[(m - 16) * 128:(m - 15) * 128, t0:t0 + tw])
                P.i("act", "activation", dict(out=sqh[:, 0:tw], in_=pa[:, 0:tw],
                                                                  func=AF.Square), rd=[paR], wr=[P.R("sqh")])
                P.i("dve", "tensor_scalar", dict(
                    out=qg[:, 0:tw], in0=pa[:, 0:tw], scalar1=qk_sb[:, gi:gi + 1], scalar2=None, op0=ALU.mult),
                     rd=[paR, P.R("qk")], wr=[P.R("qg")])
                mm_group(P, ps_h[:, 0:tw], P.R("ps_h"), [(c["ones"][:], sqh[:, 0:tw])], rd=[P.R("ones"), P.R("sqh")])
                mm_group(P, ps_r[:, 0:tw], P.R("ps_r"), [(rt[:], qg[:, 0:tw])], rd=[P.R("rt"), P.R("qg")])
                P.i("act", "activation", dict(out=lnh[:, 0:tw], in_=ps_h[:, 0:tw], func=AF.Ln, bias=c["eps"][:],
                                                   scale=1.0 / 128), rd=[P.R("ps_h"), P.R("eps")], wr=[P.R("lnh")])
                P.i("act", "activation", dict(out=rsh[:, 0:tw], in_=lnh[:, 0:tw], func=AF.Exp, scale=-0.5),
                     rd=[P.R("lnh")], wr=[P.R("rsh")])
                P.i("dve", "tensor_tensor", dict(out=t1[:, 0:tw], in0=qg[:, 0:tw],
                                                                     in1=cs[:, 0, t0:t0 + tw], op=ALU.mult),
                     rd=[P.R("qg"), P.R("cs")], wr=[P.R("t1")])
                P.i("dve", "tensor_tensor", dict(out=t2[:, 0:tw], in0=ps_r[:, 0:tw],
                                                                     in1=cs[:, 1, t0:t0 + tw], op=ALU.mult),
                     rd=[P.R("ps_r"), P.R("cs")], wr=[P.R("t2")])
                P.i("pool", "tensor_tensor", dict(out=t1[:, 0:tw], in0=t1[:, 0:tw], in1=t2[:, 0:tw], op=ALU.add),
                     rd=[P.R("t1"), P.R("t2")], wr=[P.R("t1")])
                P.i("pool", "tensor_tensor", dict(out=e_[:, 0:tw], in0=t1[:, 0:tw],
                                                                      in1=rsh[:, 0:tw], op=ALU.mult),
                     rd=[P.R("t1"), P.R("rsh")], wr=[eR])
                P.dma(dst, e_[:, 0:tw], rd=[eR], q="actq")
    return P.finish()


def build_attn():
    P = Prog()
    qT_d = P.din("qT", [3, 128, S], BF16)
    kT_d = P.din("kT", [128, S], BF16)
    v_d = P.din("v", [128, 32, 128], BF16)
    uT_d = P.din("uT", [128, S], BF16)
    pw_d = P.din("pw", [128, 128], F32)
    pc_d = P.din("pc", [128, 6], F32)
    ic_d = P.din("ic", [128, S], F32)
    att_o = P.dout("attT", [3, 128, S], BF16)
    pool_o = P.dout("poolT", [128, S], BF16)
    c = consts(P)
    qT = P.sb([128, 3, S], BF16, "qT")
    kT = P.sb([128, S], BF16, "kT")
    v = P.sb([128, 32, 128], BF16, "v")
    pw = P.sb([128, 128], F32, "pw")
    pwb = P.sb([128, 128], BF16, "pwb")
    pc = P.sb([128, 6], F32, "pc")
    for h in range(3):
        P.dma(qT[:, h, :], qT_d[h], wr=[P.R("q", h)])
    P.dma(kT[:], kT_d, wr=[P.R("k")])
    P.dma(v[:], v_d, wr=[P.R("v")])
    P.dma(pw[:], pw_d, wr=[P.R("pw")])
    P.dma(pc[:], pc_d, wr=[P.R("pc")])
    P.i("act", "activation", dict(out=pwb[:], in_=pw[:], func=AF.Copy), rd=[P.R("pw")], wr=[P.R("pwb")])
    W = S + 32
    ub = P.sb([128, S], BF16, "ub")
    u = P.sb([128, W], F32, "u")
    sa = P.sb([128, W], F32, "sa")
    sb_ = P.sb([128, W], F32, "sbb")
    acc = P.sb([128, S], F32, "acc")
    ic = P.sb([128, S], F32, "ic")
    pl = P.sb([128, S], BF16, "pl")
    P.dma(ub[:], uT_d, wr=[P.R("ub")])
    P.dma(ic[:], ic_d, wr=[P.R("ic")])
    for nm, t in (("u", u), ("sa", sa), ("sbb", sb_)):
        P.i("pool", "memset", dict(ap=t[:], constant=0.0), wr=[P.R(nm)])
    P.i("act", "activation", dict(out=u[:, 16:16 + S], in_=ub[:], func=AF.Copy), rd=[P.R("ub")], wr=[P.R("u")])
    P.i("dve", "tensor_tensor", dict(out=sa[:, 1:W], in0=u[:, 0:W - 1], in1=u[:, 1:W], op=ALU.add),
         rd=[P.R("u")], wr=[P.R("sa")])
    P.i("dve", "tensor_scalar", dict(out=acc[:], in0=sa[:, 16:16 + S], scalar1=pc[:, 0:1], scalar2=None,
                                          op0=ALU.mult), rd=[P.R("sa"), P.R("pc")], wr=[P.R("acc")])
    P.i("dve", "tensor_tensor", dict(out=sb_[:, 2:W - 2], in0=sa[:, 1:W - 3], in1=sa[:, 3:W - 1], op=ALU.add),
         rd=[P.R("sa")], wr=[P.R("sbb")])
    P.i("dve", "scalar_tensor_tensor", dict(out=acc[:], in0=sb_[:, 16:16 + S], scalar=pc[:, 1:2], in1=acc[:],
                                                 op0=ALU.mult, op1=ALU.add),
         rd=[P.R("sbb"), P.R("pc"), P.R("acc")], wr=[P.R("acc")])
    P.i("dve", "tensor_tensor", dict(out=sa[:, 4:W - 4], in0=sb_[:, 2:W - 6], in1=sb_[:, 6:W - 2], op=ALU.add),
         rd=[P.R("sbb")], wr=[P.R("sa")])
    P.i("dve", "scalar_tensor_tensor", dict(out=acc[:], in0=sa[:, 16:16 + S], scalar=pc[:, 2:3], in1=acc[:],
                                                 op0=ALU.mult, op1=ALU.add),
         rd=[P.R("sa"), P.R("pc"), P.R("acc")], wr=[P.R("acc")])
    P.i("dve", "tensor_tensor", dict(out=sb_[:, 8:W - 8], in0=sa[:, 4:W - 12], in1=sa[:, 12:W - 4], op=ALU.add),
         rd=[P.R("sa")], wr=[P.R("sbb")])
    P.i("dve", "scalar_tensor_tensor", dict(out=acc[:], in0=sb_[:, 16:16 + S], scalar=pc[:, 3:4], in1=acc[:],
                                                 op0=ALU.mult, op1=ALU.add),
         rd=[P.R("sbb"), P.R("pc"), P.R("acc")], wr=[P.R("acc")])
    P.i("dve", "tensor_tensor", dict(out=acc[:], in0=acc[:], in1=ic[:], op=ALU.mult),
         rd=[P.R("acc"), P.R("ic")], wr=[P.R("acc")])
    P.i("dve", "tensor_tensor", dict(out=pl[:], in0=acc[:], in1=u[:, 16:16 + S], op=ALU.subtract),
         rd=[P.R("acc"), P.R("u")], wr=[P.R("pl")])
    ps_p = P.ps("ps_s0")
    pev = [P.sb([128, 512], BF16, f"pev{i}") for i in range(2)]
    for b in range(8):
        mm_group(P, ps_p[:], P.R("ps_s", 0), [(pwb[:], pl[:, b * 512:(b + 1) * 512])], rd=[P.R("pwb"), P.R("pl")])
        P.i("act", "activation", dict(out=pev[b % 2][:], in_=ps_p[:], func=AF.Identity,
                                                        scale=pc[:, 4:5]),
             rd=[P.R("ps_s", 0), P.R("pc")], wr=[P.R("pev", b % 2)])
        P.dma(pool_o[:, b * 512:(b + 1) * 512], pev[b % 2][:], rd=[P.R("pev", b % 2)])
    ps_s = [ps_p, P.ps("ps_s1"), P.ps("ps_s2")]
    ps_o = [P.ps("ps_o0"), P.ps("ps_o1")]
    ps_m = [P.ps("ps_m0"), P.ps("ps_m1")]
    pT = [P.sb([128, 512], BF16, f"pT{i}") for i in range(4)]
    rinv = [P.sb([128, 512], F32, f"rinv{i}") for i in range(2)]
    aev = [P.sb([128, 512], BF16, f"aev{i}") for i in range(2)]
    scale = 128 ** -0.5
    steps = [(h, qb, kt) for h in range(3) for qb in range(8) for kt in range(32)]
    NS = len(steps)

    def emit_s(i):
        h, qb, kt = steps[i]
        mm_group(P, ps_s[i % 3][:], P.R("ps_s", i % 3),
                 [(kT[:, kt * 128:(kt + 1) * 128], qT[:, h, qb * 512:(qb + 1) * 512])], rd=[P.R("k"), P.R("q", h)])
        P.i("act", "activation", dict(out=pT[i % 4][:], in_=ps_s[i % 3][:], func=AF.Exp, scale=scale),
            rd=[P.R("ps_s", i % 3)], wr=[P.R("pT", i % 4)])

    emit_s(0)
    emit_s(1)
    for i in range(NS):
        if i + 2 < NS:
            emit_s(i + 2)
        h, qb, kt = steps[i]
        ob = (i // 32) % 2
        mm_group(P, ps_o[ob][:], P.R("ps_o", ob), [(v[:, kt, :], pT[i % 4][:])], rd=[P.R("v"), P.R("pT", i % 4)],
                 start=(kt == 0), stop=(kt == 31))
        mm_group(P, ps_m[ob][:], P.R("ps_m", ob), [(c["ones"][:], pT[i % 4][:])], rd=[P.R("ones"), P.R("pT", i % 4)],
                 start=(kt == 0), stop=(kt == 31))
        if kt == 31:
            P.i("dve", "reciprocal", dict(out=rinv[ob][:], in_=ps_m[ob][:]), rd=[P.R("ps_m", ob)], wr=[P.R("rinv", ob)])
            P.i("dve", "tensor_tensor", dict(out=aev[ob][:], in0=ps_o[ob][:], in1=rinv[ob][:], op=ALU.mult),
                rd=[P.R("ps_o", ob), P.R("rinv", ob)], wr=[P.R("aev", ob)])
            P.dma(att_o[h, :, qb * 512:(qb + 1) * 512], aev[ob][:], rd=[P.R("aev", ob)])
    return P.finish()


def build_outffn(kind):
    T = 1026
    P = Prog()
    aT_d = P.din("aT", [D, T], BF16)
    xT_d = P.din("xT", [D, T], F32)
    wo_d = P.din("wo", [D, D], F32)
    g_d = P.din("g", [128, 16], F32)
    wu_d = P.din("wu", [D, 2 * DFF], F32)
    cw_d = P.din("cw", [128, 88, 4], F32)
    wd_d = P.din("wd", [DFF, D], F32)
    if kind == "c":
        oT_d = P.din("oT", [D, T], BF16)
        hn_d = P.din("hn", [128, 16], F32)
    out_o = P.dout("oxT", [D, 1024], F32)
    xm_s = P.nc.dram_tensor("xm_s", [D, T], F32, kind="Internal").ap()
    c = consts(P)
    blks = blocks_of(T)
    big = P.sb([128, 44 * 1024], BF16, "big")
    xT = big[:, 0:16 * T * 2].bitcast(F32).rearrange("p (k t) -> p k t", t=T)
    actT = big[:].rearrange("p (j t) -> p j t", t=1024)
    A = P.sb([128, 16, T], BF16, "A")
    g_sb = P.sb([128, 16], F32, "g_sb")
    cw = P.sb([128, 88, 4], F32, "cw")
    P.dma(g_sb[:], g_d, wr=[P.R("gsb", "A")])
    P.dma(cw[:], cw_d, wr=[P.R("cw")])
    xv = xT_d.rearrange("(k p) t -> p k t", p=128)
    av = aT_d.rearrange("(k p) t -> p k t", p=128)
    for k in range(16):
        P.dma(A[:, k, :], av[:, k, :], wr=[P.R("A", k, b) for b in range(3)])
    for k in range(16):
        P.dma(xT[:, k, :], xv[:, k, :], wr=[P.R("x", k, b) for b in range(3)] + [P.R("alias")])
    if kind == "c":
        O = [P.sb([128, T], BF16, f"O{i}") for i in range(2)]
        hn = P.sb([128, 16], F32, "hn")
        ov = oT_d.rearrange("(k p) t -> p k t", p=128)
        P.dma(hn[:], hn_d, wr=[P.R("hn")])
        for k in range(16):
            P.dma(O[k % 2][:], ov[:, k, :], wr=[P.R("O", k % 2)])
            P.i("dve", "scalar_tensor_tensor", dict(
                out=A[:, k, :], in0=A[:, k, :], scalar=hn[:, k:k + 1], in1=O[k % 2][:], op0=ALU.mult, op1=ALU.mult),
                 rd=[P.R("O", k % 2), P.R("hn")] + [P.R("A", k, b) for b in range(3)],
                 wr=[P.R("A", k, b) for b in range(3)])
    ws = WStream(P, 32)
    pacc = [P.ps(f"pacc{i}") for i in range(6)]
    ps_ss = P.ps("ps_ss")
    xmv = xm_s.rearrange("(k p) t -> p k t", p=128)
    it = 0
    for m, ((wv,), wR) in enumerate(prefetched(ws, [[(wo_d[:, m * 128:(m + 1) * 128], 16, 128)] for m in range(16)])):
        for b, (t0, tw) in enumerate(blks):
            pa = pacc[it % 6]
            paR = P.R("pacc", it % 6)
            it += 1
            mm_group(P, pa[:, 0:tw], paR, [(wv[:, k, :], A[:, k, t0:t0 + tw]) for k in range(16)],
                     rd=wR + [P.R("A", k, b) for k in range(16)])
            P.i("dve", "tensor_tensor", dict(
                out=xT[:, m, t0:t0 + tw], in0=pa[:, 0:tw], in1=xT[:, m, t0:t0 + tw], op=ALU.add),
                 rd=[paR, P.R("x", m, b)], wr=[P.R("x", m, b)])
        P.dma(xmv[:, m, :], xT[:, m, :], rd=[P.R("x", m, b) for b in range(3)] + [P.R("alias")], wr=[P.R("xm", m)])
    rmsnorm_fm(P, c, xT, "x", g_sb, A, "A", T, ps_ss, D)
    allx = [P.R("x", k, b) for k in range(16) for b in range(3)]
    P.i("pool", "memset", dict(ap=c["eps"][:], constant=EPS), rd=allx + [P.R("eps")], wr=[P.R("alias"), P.R("eps")])
    raw = [P.sb([128, T], F32, f"raw{i}") for i in range(4)]
    tg = P.sb([128, 1024], F32, "tg")
    tv = P.sb([128, 1024], F32, "tv")
    specs = [[(wu_d[:, j * 128:(j + 1) * 128], 16, 128), (wu_d[:, DFF + j * 128:DFF + (j + 1) * 128], 16, 128)]
             for j in range(44)]
    for j, ((wg, wvv), wR) in enumerate(prefetched(ws, specs)):
        for gv, wv in enumerate((wg, wvv)):
            r_ap = raw[(j % 2) * 2 + gv]
            rR = P.R("raw", (j % 2) * 2 + gv)
            for b, (t0, tw) in enumerate(blks):
                pa = pacc[it % 6]
                paR = P.R("pacc", it % 6)
                it += 1
                mm_group(P, pa[:, 0:tw], paR, [(wv[:, k, :], A[:, k, t0:t0 + tw]) for k in range(16)],
                         rd=[wR[gv]] + [P.R("A", k, b) for k in range(16)])
                P.i("act", "activation", dict(
                    out=r_ap[:, t0:t0 + tw], in_=pa[:, 0:tw], func=AF.Copy), rd=[paR], wr=[rR])
            ch = gv * 44 + j
            t_ap = tg if gv == 0 else tv
            tR = P.R("tg") if gv == 0 else P.R("tv")
            eng = "dve"
            P.i(eng, "tensor_scalar", dict(
                out=t_ap[:], in0=r_ap[:, 1:1025], scalar1=cw[:, ch, 1:2], scalar2=cw[:, ch, 3:4],
                op0=ALU.mult, op1=ALU.add), rd=[rR, P.R("cw")], wr=[tR])
            P.i(eng, "scalar_tensor_tensor", dict(
                out=t_ap[:], in0=r_ap[:, 0:1024], scalar=cw[:, ch, 0:1], in1=t_ap[:], op0=ALU.mult, op1=ALU.add),
                 rd=[rR, P.R("cw"), tR], wr=[tR])
            P.i(eng, "scalar_tensor_tensor", dict(
                out=t_ap[:], in0=r_ap[:, 2:1026], scalar=cw[:, ch, 2:3], in1=t_ap[:], op0=ALU.mult, op1=ALU.add),
                 rd=[rR, P.R("cw"), tR], wr=[tR])
        P.i("act", "activation", dict(out=tg[:], in_=tg[:], func=AF.Silu), rd=[P.R("tg")], wr=[P.R("tg")])
        P.i("dve", "tensor_tensor", dict(out=actT[:, j, :], in0=tg[:], in1=tv[:], op=ALU.mult),
             rd=[P.R("tg"), P.R("tv"), P.R("alias")], wr=[P.R("act", j)])
    xo = [raw[i][:, 0:1024] for i in range(2)]
    specs = [[(wd_d[half * 2816:(half + 1) * 2816, m * 128:(m + 1) * 128], 22, 128)]
             for m in range(16) for half in range(2)]
    wit = prefetched(ws, specs)
    for m in range(16):
        P.dma(xo[m % 2], xmv[:, m, 1:1025], rd=[P.R("xm", m)], wr=[P.R("raw", m % 2)], q="actq")
        pas = []
        for half in range(2):
            (wv,), wR = next(wit)
            for b in range(2):
                if half == 0:
                    pas.append((pacc[it % 6], P.R("pacc", it % 6)))
                    it += 1
                pa, paR = pas[b]
                mm_group(P, pa[:], paR, [(wv[:, k, :], actT[:, half * 22 + k, b * 512:(b + 1) * 512]) for k in range(22)],
                         rd=wR + [P.R("act", half * 22 + k) for k in range(22)], start=(half == 0), stop=(half == 1))
        for b in range(2):
            pa, paR = pas[b]
            P.i("dve", "tensor_tensor", dict(
                out=xo[m % 2][:, b * 512:(b + 1) * 512], in0=pa[:], in1=xo[m % 2][:, b * 512:(b + 1) * 512],
                op=ALU.add), rd=[paR, P.R("raw", m % 2)], wr=[P.R("raw", m % 2)])
        P.dma(out_o[m * 128:(m + 1) * 128, :], xo[m % 2], rd=[P.R("raw", m % 2)], q="actq")
    return P.finish()


def build_inproj1():
    T = 1024
    P = Prog()
    xT_d = P.din("xT", [D, T], F32)
    g_d = P.din("g", [128, 16], F32)
    w_d = P.din("w", [D, 8224], F32)
    bg_d = P.din("bg", [32, 1], F32)
    o_o = P.dout("pT", [8192, T], BF16)
    gt_o = P.dout("gT", [32, T], F32)
    c = consts(P)
    xT = P.sb([128, 16, T], F32, "xT")
    hT = P.sb([128, 16, T], BF16, "hT")
    g_sb = P.sb([128, 16], F32, "g_sb")
    bg = P.sb([32, 1], F32, "bg")
    blks = blocks_of(T)
    P.dma(g_sb[:], g_d, wr=[P.R("gsb", "h")])
    P.dma(bg[:], bg_d, wr=[P.R("bg")])
    xv = xT_d.rearrange("(k p) t -> p k t", p=128)
    for k in range(16):
        for b, (t0, tw) in enumerate(blks):
            P.dma(xT[:, k, t0:t0 + tw], xv[:, k, t0:t0 + tw], wr=[P.R("x", k, b)])
    ps_ss = P.ps("ps_ss")
    rmsnorm_fm(P, c, xT, "x", g_sb, hT, "h", T, ps_ss, D)
    ws = WStream(P, 16)
    pacc = [P.ps(f"pacc{i}") for i in range(4)]
    ev = [P.sb([128, 512], BF16, f"ev{i}") for i in range(4)]
    gev = P.sb([32, 512], F32, "gev")
    it = 0
    specs = [[(w_d[:, m * 128:m * 128 + (128 if m < 64 else 32)], 16, (128 if m < 64 else 32))] for m in range(65)]
    for m, ((wv,), wR) in enumerate(prefetched(ws, specs)):
        mw = 128 if m < 64 else 32
        for b, (t0, tw) in enumerate(blks):
            pa = pacc[it % 4]
            paR = P.R("pacc", it % 4)
            e_ = ev[it % 4]
            eR = P.R("ev", it % 4)
            it += 1
            mm_group(P, pa[0:mw, 0:tw], paR, [(wv[:, k, :], hT[:, k, t0:t0 + tw]) for k in range(16)],
                     rd=wR + [P.R("h", k, b) for k in range(16)])
            if m == 64:
                P.i("act", "activation", dict(out=gev[:, 0:tw], in_=pa[0:32, 0:tw],
                                                                  func=AF.Identity, bias=bg[:], scale=1.0),
                     rd=[paR, P.R("bg")], wr=[P.R("gev")])
                P.dma(gt_o[:, t0:t0 + tw], gev[:, 0:tw], rd=[P.R("gev")], q="actq")
            else:
                if m < 16:
                    kw = dict(out=e_[:, 0:tw], in_=pa[:, 0:tw], func=AF.Identity, scale=1.0 / 16.0)
                elif m < 48:
                    kw = dict(out=e_[:, 0:tw], in_=pa[:, 0:tw], func=AF.Copy)
                else:
                    kw = dict(out=e_[:, 0:tw], in_=pa[:, 0:tw], func=AF.Sigmoid)
                P.i("act", "activation", kw, rd=[paR], wr=[eR])
                P.dma(o_o[m * 128:(m + 1) * 128, t0:t0 + tw], e_[:, 0:tw], rd=[eR], q="actq")
    return P.finish()


def build_mlstm():
    NP = 2
    NCH = 32
    P = Prog()
    qT_d = P.din("qT", [NP, 2, 128, S], BF16)
    kT_d = P.din("kT", [NP, 2, 128, S], BF16)
    k_d = P.din("k", [NP, 128, NCH, 256], BF16)
    v_d = P.din("v", [NP, 128, NCH, 256], BF16)
    gt_d = P.din("gt", [NP, 128, 4, NCH], F32)
    tri_d = P.din("tri", [128, 2, 128], F32)
    hn_o = P.dout("hn", [NP, 128, NCH, 256], BF16)
    c = consts(P)
    tri = P.sb([128, 2, 128], F32, "tri")
    onesf = P.sb([128, 128], F32, "onesf")
    one1 = P.sb([128, 1], F32, "one1")
    P.dma(tri[:], tri_d, wr=[P.R("tri")])
    P.i("pool", "memset", dict(ap=onesf[:], constant=1.0), wr=[P.R("onesf")])
    P.i("pool", "memset", dict(ap=one1[:], constant=1.0), wr=[P.R("one1")])
    qT = P.sb([128, 2, S], BF16, "qT")
    kT = P.sb([128, 2, S], BF16, "kT")
    kk = P.sb([128, NCH, 256], BF16, "kk")
    vx = P.sb([128, NCH, 257], BF16, "vx")
    gt = P.sb([128, 4, NCH], F32, "gt")
    lf = P.sb([128, 2, NCH], F32, "lf")
    bc = P.sb([128, 2, NCH], F32, "bc")
    tot = P.sb([128, 2, NCH], F32, "tot")
    av = P.sb([128, 2, NCH], F32, "av")
    bv = P.sb([128, 2, NCH], F32, "bv")
    b2 = P.sb([128, 2, NCH], F32, "b2")
    dc = P.sb([128, 2, NCH], F32, "dc")
    tmp = P.sb([128, 2, NCH], F32, "tmp")
    hacc = P.sb([128, NCH, 256], F32, "hacc")
    ssq = P.sb([128, NCH], F32, "ssq")
    junk = P.sb([128, 256], F32, "junk")
    psS = [P.ps("psS0"), P.ps("psS1")]
    ps_g = psS[0]
    psN = [P.ps("psN0"), P.ps("psN1")]
    psC = [[P.ps("psC00"), P.ps("psC01")], [P.ps("psC10"), P.ps("psC11")]]
    Cst = [P.sb([128, 2, 257], F32, f"Cst{d}") for d in range(2)]
    Cbf = [P.sb([128, 2, 257], BF16, f"Cbf{d}") for d in range(2)]
    Sm = [P.sb([128, 128], BF16, f"Sm{d}") for d in range(2)]
    k2 = [P.sb([128, 256], BF16, f"k2{d}") for d in range(2)]
    dn = [P.sb([128, 4], F32, f"dn{d}") for d in range(2)]
    hev = [P.sb([128, 256], BF16, f"hev{i}") for i in range(2)]
    for p in range(NP):
        for h in range(2):
            P.dma(qT[:, h, :], qT_d[p, h], wr=[P.R("qT")])
            P.dma(kT[:, h, :], kT_d[p, h], wr=[P.R("kT")])
        P.dma(kk[:], k_d[p], wr=[P.R("kk")])
        P.dma(vx[:, :, 0:256], v_d[p], wr=[P.R("vx")])
        P.i("pool", "memset", dict(ap=vx[:, :, 256:257], constant=1.0), wr=[P.R("vx")], rd=[])
        P.dma(gt[:], gt_d[p], wr=[P.R("gt")])
        for d in range(2):
            P.i("act", "activation", dict(out=lf[:, d, :], in_=gt[:, 2 * d + 1, :], func=AF.Exp,
                                                            scale=-1.0), rd=[P.R("gt")], wr=[P.R("lf")])
        P.i("act", "activation", dict(out=lf[:], in_=lf[:], func=AF.Ln, bias=one1[:], scale=1.0),
             rd=[P.R("lf"), P.R("one1")], wr=[P.R("lf")])
        P.i("dve", "tensor_scalar", dict(out=lf[:], in0=lf[:], scalar1=-1.0, scalar2=None, op0=ALU.mult),
             rd=[P.R("lf")], wr=[P.R("lf")])
        for d in range(2):
            mm_group(P, ps_g[:, d * NCH:(d + 1) * NCH], P.R("psS", 0), [(tri[:, d, :], lf[:, d, :])],
                     rd=[P.R("tri"), P.R("lf")])
        mm_group(P, ps_g[:, 2 * NCH:4 * NCH], P.R("psS", 0), [(onesf[:], lf[:].rearrange("p d c -> p (d c)"))],
                 rd=[P.R("onesf"), P.R("lf")])
        P.i("dve", "tensor_copy", dict(out=bc[:].rearrange("p d c -> p (d c)"), in_=ps_g[:, 0:2 * NCH]),
             rd=[P.R("psS", 0)], wr=[P.R("bc")])
        P.i("dve", "tensor_copy", dict(out=tot[:].rearrange("p d c -> p (d c)"), in_=ps_g[:, 2 * NCH:4 * NCH]),
             rd=[P.R("psS", 0)], wr=[P.R("tot")])
        P.i("act", "activation", dict(out=av[:], in_=bc[:], func=AF.Exp), rd=[P.R("bc")], wr=[P.R("av")])
        P.i("act", "activation", dict(out=dc[:], in_=tot[:], func=AF.Exp), rd=[P.R("tot")], wr=[P.R("dc")])
        for d in range(2):
            P.i("dve", "tensor_tensor", dict(out=tmp[:, d, :], in0=gt[:, 2 * d, :], in1=bc[:, d, :],
                                                               op=ALU.subtract),
                 rd=[P.R("gt"), P.R("bc")], wr=[P.R("tmp")])
        P.i("act", "activation", dict(out=bv[:], in_=tmp[:], func=AF.Exp), rd=[P.R("tmp")], wr=[P.R("bv")])
        P.i("dve", "tensor_tensor", dict(out=tmp[:], in0=tmp[:], in1=tot[:], op=ALU.add),
             rd=[P.R("tmp"), P.R("tot"), P.R("bv")], wr=[P.R("tmp")])
        P.i("act", "activation", dict(out=b2[:], in_=tmp[:], func=AF.Exp), rd=[P.R("tmp")], wr=[P.R("b2")])
        gR = [P.R("av"), P.R("bv"), P.R("b2"), P.R("dc")]
        for d in range(2):
            P.i("pool", "memset", dict(ap=Cst[d][:], constant=0.0), wr=[P.R("Cst", d)])
            P.i("pool", "memset", dict(ap=Cbf[d][:], constant=0.0), wr=[P.R("Cbf", d)])
        for step in range(NCH):
            for d in range(2):
                ch = step if d == 0 else NCH - 1 - step
                cs_ = slice(ch * 128, (ch + 1) * 128)
                S_ap, SR = psS[d], P.R("psS", d)
                N_ap, NR = psN[d], P.R("psN", d)
                mm_group(P, S_ap[:, 0:128], SR, [(kT[:, h, cs_], qT[:, h, cs_]) for h in range(2)],
                         rd=[P.R("kT"), P.R("qT")])
                P.i("dve", "scalar_tensor_tensor", dict(
                    out=Sm[d][:], in0=S_ap[:, 0:128], scalar=bv[:, d, ch:ch + 1], in1=tri[:, d, :],
                    op0=ALU.mult, op1=ALU.mult), rd=[SR, P.R("tri")] + gR, wr=[P.R("Sm", d)])
                P.i("pool", "tensor_scalar", dict(
                    out=k2[d][:], in0=kk[:, ch, :], scalar1=b2[:, d, ch:ch + 1], scalar2=None, op0=ALU.mult),
                     rd=[P.R("kk")] + gR, wr=[P.R("k2", d)])
                for h in range(2):
                    mm_group(P, psC[d][h][:, 0:257], P.R("psC", d, h), [(k2[d][:, h * 128:(h + 1) * 128], vx[:, ch, :])],
                             rd=[P.R("k2", d), P.R("vx")])
            for d in range(2):
                ch = step if d == 0 else NCH - 1 - step
                cs_ = slice(ch * 128, (ch + 1) * 128)
                S_ap, SR = psS[d], P.R("psS", d)
                N_ap, NR = psN[d], P.R("psN", d)
                mm_group(P, N_ap[:, 0:257], NR,
                         [(Sm[d][:], vx[:, ch, :])] + [(qT[:, h, cs_], Cbf[d][:, h, :]) for h in range(2)],
                         rd=[P.R("Sm", d), P.R("vx"), P.R("qT"), P.R("Cbf", d)])
                for h in range(2):
                    P.i("dve", "scalar_tensor_tensor", dict(
                        out=Cst[d][:, h, :], in0=Cst[d][:, h, :], scalar=dc[:, d, ch:ch + 1], in1=psC[d][h][:, 0:257],
                        op0=ALU.mult, op1=ALU.add), rd=[P.R("psC", d, h), P.R("Cst", d)] + gR, wr=[P.R("Cst", d)])
                P.i("act", "activation", dict(out=Cbf[d][:], in_=Cst[d][:], func=AF.Copy),
                     rd=[P.R("Cst", d)], wr=[P.R("Cbf", d)])
                P.i("act", "activation", dict(out=dn[d][:, 0:1], in_=N_ap[:, 256:257], func=AF.Abs,
                                              scale=av[:, d, ch:ch + 1]), rd=[NR] + gR, wr=[P.R("dn", d)])
                P.i("dve", "tensor_scalar", dict(
                    out=dn[d][:, 1:2], in0=dn[d][:, 0:1], scalar1=1.0, scalar2=None, op0=ALU.max),
                     rd=[P.R("dn", d)], wr=[P.R("dn", d)])
                P.i("dve", "reciprocal", dict(out=dn[d][:, 2:3], in_=dn[d][:, 1:2]),
                     rd=[P.R("dn", d)], wr=[P.R("dn", d)])
                P.i("dve", "tensor_tensor", dict(
                    out=dn[d][:, 3:4], in0=dn[d][:, 2:3], in1=av[:, d, ch:ch + 1], op=ALU.mult),
                     rd=[P.R("dn", d)] + gR, wr=[P.R("dn", d)])
                if step < NCH // 2:
                    P.i("act", "activation", dict(
                        out=hacc[:, ch, :], in_=N_ap[:, 0:256], func=AF.Identity, scale=dn[d][:, 3:4]),
                         rd=[NR, P.R("dn", d)], wr=[P.R("hacc", ch)])
                else:
                    P.i("dve", "scalar_tensor_tensor", dict(
                        out=hacc[:, ch, :], in0=N_ap[:, 0:256], scalar=dn[d][:, 3:4], in1=hacc[:, ch, :],
                        op0=ALU.mult, op1=ALU.add), rd=[NR, P.R("dn", d), P.R("hacc", ch)], wr=[P.R("hacc", ch)])
        for ch in range(NCH):
            P.i("act", "activation", dict(out=junk[:], in_=hacc[:, ch, :], func=AF.Square,
                                                              accum_out=ssq[:, ch:ch + 1]),
                 rd=[P.R("hacc", ch)], wr=[P.R("junk"), P.R("ssq")])
        P.i("act", "activation", dict(out=ssq[:], in_=ssq[:], func=AF.Ln, bias=c["eps"][:], scale=1.0 / 256),
             rd=[P.R("ssq"), P.R("eps")], wr=[P.R("ssq")])
        P.i("act", "activation", dict(out=ssq[:], in_=ssq[:], func=AF.Exp, scale=-0.5),
             rd=[P.R("ssq")], wr=[P.R("ssq")])
        for ch in range(NCH):
            P.i("dve", "tensor_scalar", dict(
                out=hev[ch % 2][:], in0=hacc[:, ch, :], scalar1=ssq[:, ch:ch + 1], scalar2=None, op0=ALU.mult),
                 rd=[P.R("hacc", ch), P.R("ssq")], wr=[P.R("hev", ch % 2)])
            P.dma(hn_o[p, :, ch, :], hev[ch % 2][:], rd=[P.R("hev", ch % 2)])
    return P.finish()


N_LAUNCH = [0]


def run(nc, in_maps):
    N_LAUNCH[0] += 1
    res = run_bass_kernel_spmd(nc, in_maps, core_ids=list(range(NCORES)))
    return res.results


def pc128(v, kc):
    return np.ascontiguousarray(np.asarray(v, np.float32).reshape(kc, 128).T)


def rope_tables():
    rows = S // 64
    row = np.repeat(np.arange(rows), 64).astype(np.float32)
    col = np.tile(np.arange(64), rows).astype(np.float32)
    inv = (1.0 / (np.float32(10000.0) ** (np.arange(0, 64, 2, dtype=np.float32) / np.float32(64)))).astype(np.float32)
    a_r = row[:, None] * inv[None, :]
    a_c = col[:, None] * inv[None, :]
    ang = np.concatenate([a_r, a_r, a_c, a_c], axis=-1)
    return np.cos(ang).astype(np.float32), np.sin(ang).astype(np.float32)


def rot_matrix_T():
    R = np.zeros((128, 128), np.float32)
    for base in (0, 64):
        for j in range(32):
            R[base + j, base + j + 32] = -1.0
            R[base + j + 32, base + j] = 1.0
    return np.ascontiguousarray(R.T).astype(NPBF)


def halo_T(a_tok, b, q):
    F_ = a_tok.shape[-1]
    out = np.zeros((F_, 1026), a_tok.dtype)
    lo, hi = q * 1024 - 1, q * 1024 + 1025
    l2, h2 = max(lo, 0), min(hi, S)
    out[:, l2 - lo:h2 - lo] = a_tok[b, l2:h2].T
    return out


def ffn_inputs(layer, norm_ffn, w_up, conv_w, conv_b, w_down):
    cw = np.zeros((128, 88, 4), np.float32)
    cwl = np.asarray(conv_w[layer], np.float32)
    cbl = np.asarray(conv_b[layer], np.float32)
    for i in range(3):
        cw[:, :, i] = cwl[i].reshape(88, 128).T
    cw[:, :, 3] = cbl.reshape(88, 128).T
    return {"g": pc128(norm_ffn[layer], 16), "wu": np.ascontiguousarray(w_up[layer], np.float32), "cw": cw,
            "wd": np.ascontiguousarray(w_down[layer], np.float32)}


def kernel(x, norm_mix, norm_ffn, w_in_ab, pool_w, pool_scale, q_norm, k_norm, w_out_ab,
           w_in_c, b_gate_c, h_norm_c, w_out_c, w_up, conv_w, conv_b, w_down):
    x = np.asarray(x, np.float32)
    B = x.shape[0]
    cores = [(c // 4, c % 4) for c in range(NCORES)]
    cos, sin = rope_tables()
    nc = build_inproj0()
    ims = []
    for (b, q) in cores:
        sl = slice(q * 1024, (q + 1) * 1024)
        cs = np.stack([cos[sl].T, sin[sl].T], axis=1)
        ims.append({"xT": np.ascontiguousarray(x[b, sl].T), "g": pc128(norm_mix[0], 16),
                    "w": np.ascontiguousarray(w_in_ab[0], np.float32),
                    "qkn": np.ascontiguousarray(np.stack([q_norm[0], k_norm[0]], axis=1), np.float32),
                    "cs": np.ascontiguousarray(cs, np.float32), "rt": rot_matrix_T()})
    r1 = run(nc, ims)
    qT = np.zeros((B, 1536, S), NPBF)
    kT = np.zeros((B, 512, S), NPBF)
    vT = np.zeros((B, 512, S), NPBF)
    uT = np.zeros((B, 512, S), NPBF)
    for ci, (b, q) in enumerate(cores):
        sl = slice(q * 1024, (q + 1) * 1024)
        qT[b][:, sl] = r1[ci]["qT"]
        kT[b][:, sl] = r1[ci]["kT"]
        vT[b][:, sl] = r1[ci]["vT"]
        uT[b][:, sl] = r1[ci]["uT"]
    nc = build_attn()
    ims = []
    t = np.arange(S)
    for (b, g) in cores:
        w = (2, 4, 8, 16)[g]
        lo = np.clip(t - w // 2, 0, S)
        hi = np.clip(t + w // 2, 0, S)
        ic = np.broadcast_to((1.0 / (hi - lo).astype(np.float32))[None, :], (128, S))
        pc = np.zeros((128, 6), np.float32)
        pc[:, g] = 1.0
        pc[:, 4] = np.asarray(pool_scale[0], np.float32)[g * 128:(g + 1) * 128]
        vg = vT[b][g * 128:(g + 1) * 128]
        v_tm = np.ascontiguousarray(vg.T.reshape(32, 128, 128).transpose(1, 0, 2))
        ims.append({"qT": np.ascontiguousarray(qT[b][g * 384:(g + 1) * 384].reshape(3, 128, S)),
                    "kT": np.ascontiguousarray(kT[b][g * 128:(g + 1) * 128]), "v": v_tm,
                    "uT": np.ascontiguousarray(uT[b][g * 128:(g + 1) * 128]),
                    "pw": np.ascontiguousarray(pool_w[0][g], np.float32), "pc": pc,
                    "ic": np.ascontiguousarray(ic, np.float32)})
    r2 = run(nc, ims)
    cat = np.zeros((B, S, D), NPBF)
    for ci, (b, g) in enumerate(cores):
        cat[b][:, g * 128:(g + 1) * 128] = r2[ci]["poolT"].T
        cat[b][:, 512 + g * 384:512 + (g + 1) * 384] = r2[ci]["attT"].reshape(384, S).T
    nc = build_outffn("ab")
    f0 = ffn_inputs(0, norm_ffn, w_up, conv_w, conv_b, w_down)
    ims = []
    for (b, q) in cores:
        d = {"aT": halo_T(cat, b, q), "xT": halo_T(x, b, q), "wo": np.ascontiguousarray(w_out_ab[0], np.float32)}
        d.update(f0)
        ims.append(d)
    r3 = run(nc, ims)
    x1 = np.zeros((B, S, D), np.float32)
    for ci, (b, q) in enumerate(cores):
        x1[b, q * 1024:(q + 1) * 1024] = r3[ci]["oxT"].T
    nc = build_inproj1()
    ims = []
    for (b, q) in cores:
        sl = slice(q * 1024, (q + 1) * 1024)
        ims.append({"xT": np.ascontiguousarray(x1[b, sl].T), "g": pc128(norm_mix[1], 16),
                    "w": np.ascontiguousarray(w_in_c[0], np.float32),
                    "bg": np.ascontiguousarray(np.asarray(b_gate_c[0], np.float32).reshape(32, 1))})
    r4 = run(nc, ims)
    pT = np.zeros((B, 8192, S), NPBF)
    gT = np.zeros((B, 32, S), np.float32)
    for ci, (b, q) in enumerate(cores):
        sl = slice(q * 1024, (q + 1) * 1024)
        pT[b][:, sl] = r4[ci]["pT"]
        gT[b][:, sl] = r4[ci]["gT"]
    nc = build_mlstm()
    tri = np.zeros((128, 2, 128), np.float32)
    ii = np.arange(128)
    tri[:, 0, :] = (ii[:, None] <= ii[None, :])
    tri[:, 1, :] = (ii[:, None] >= ii[None, :])
    ims = []
    pairs = [(p // 8, p % 8) for p in range(16)]
    for ci in range(NCORES):
        d = {k: [] for k in ("qT", "kT", "k", "v", "gt")}
        for (b, h) in pairs[2 * ci:2 * ci + 2]:
            qh = pT[b][h * 256:(h + 1) * 256]
            kh = pT[b][2048 + h * 256:2048 + (h + 1) * 256]
            vh = pT[b][4096 + h * 256:4096 + (h + 1) * 256]
            d["qT"].append(qh.reshape(2, 128, S))
            d["kT"].append(kh.reshape(2, 128, S))
            d["k"].append(kh.T.reshape(32, 128, 256).transpose(1, 0, 2))
            d["v"].append(vh.T.reshape(32, 128, 256).transpose(1, 0, 2))
            gg = gT[b].reshape(4, 8, S)[:, h]
            d["gt"].append(gg.reshape(4, 32, 128).transpose(2, 0, 1))
        im = {k: np.ascontiguousarray(np.stack(v_)) for k, v_ in d.items()}
        im["tri"] = tri
        ims.append(im)
    r5 = run(nc, ims)
    hn = np.zeros((B, S, D), NPBF)
    for ci in range(NCORES):
        for j, (b, h) in enumerate(pairs[2 * ci:2 * ci + 2]):
            hh = r5[ci]["hn"][j]
            hn[b][:, h * 256:(h + 1) * 256] = hh.transpose(1, 0, 2).reshape(S, 256)
    og = np.ascontiguousarray(pT[:, 6144:8192].transpose(0, 2, 1))
    nc = build_outffn("c")
    f1 = ffn_inputs(1, norm_ffn, w_up, conv_w, conv_b, w_down)
    ims = []
    for (b, q) in cores:
        d = {"aT": halo_T(hn, b, q), "oT": halo_T(og, b, q), "xT": halo_T(x1, b, q),
             "hn": pc128(h_norm_c[0], 16), "wo": np.ascontiguousarray(w_out_c[0], np.float32)}
        d.update(f1)
        ims.append(d)
    r6 = run(nc, ims)
    out = np.zeros((B, S, D), np.float32)
    for ci, (b, q) in enumerate(cores):
        out[b, q * 1024:(q + 1) * 1024] = r6[ci]["oxT"].T
    return out
```
